# Optimizing a Trainium2 kernel written in Bass

```python
import jax, jax.numpy as jnp
from jax import lax
import numpy as np

D_MODEL = 2048
BATCH = 4
SEQ = 4096
DEPTH = 2

BRANCH_WIDTH = D_MODEL // 2
N_BRANCHES = 4
EPS = 1e-6

SSD_HEAD_DIM = 64
SSD_HEADS = BRANCH_WIDTH // SSD_HEAD_DIM
SSD_GROUPS = 2
SSD_STATE = 128
SSD_CONV = 4
SSD_CHUNK = 128
SSD_XBC = BRANCH_WIDTH + 2 * SSD_GROUPS * SSD_STATE

FOX_HEAD_DIM = 64
FOX_HEADS = BRANCH_WIDTH // FOX_HEAD_DIM
Q_BLOCK = 128

GLA_HEADS = 4
GLA_KEY_DIM = 128
GLA_VAL_DIM = BRANCH_WIDTH // GLA_HEADS
GLA_GATE_RANK = 16
GLA_GATE_TAU = 16.0
GLA_CHUNK = 64

MLA_HEADS = 8
MLA_NOPE = 128
MLA_ROPE = 64
MLA_V = BRANCH_WIDTH // MLA_HEADS
MLA_Q_RANK = 512
MLA_KV_RANK = 512
ROPE_BASE = 10000.0

IN_SPLITS = (
    BRANCH_WIDTH, SSD_XBC, SSD_HEADS,
    BRANCH_WIDTH, BRANCH_WIDTH, BRANCH_WIDTH, FOX_HEADS, BRANCH_WIDTH,
    GLA_HEADS * GLA_KEY_DIM, GLA_HEADS * GLA_KEY_DIM, BRANCH_WIDTH, GLA_GATE_RANK, BRANCH_WIDTH,
    MLA_Q_RANK, MLA_KV_RANK, MLA_ROPE, BRANCH_WIDTH,
    N_BRANCHES * D_MODEL,
)
N_IN = sum(IN_SPLITS)

kernel_name = "hybrid_gated_parallel_mixer_trunk"


def rmsnorm(x, w):
    xf = x.astype(jnp.float32)
    y = xf * lax.rsqrt(jnp.mean(xf * xf, axis=-1, keepdims=True) + EPS)
    return (y * w.astype(jnp.float32)).astype(x.dtype)


def causal_depthwise_conv(x, w, b):
    ch = x.shape[-1]
    y = lax.conv_general_dilated(
        x, w[:, None, :].astype(x.dtype), window_strides=(1,),
        padding=[(SSD_CONV - 1, 0)], dimension_numbers=('NWC', 'WIO', 'NWC'),
        feature_group_count=ch)
    return y + b.astype(x.dtype)


def ssd_chunked(x, dt, a, b_in, c_in):
    bsz, s, h, p = x.shape
    g, n = b_in.shape[-2:]
    r = h // g
    L = SSD_CHUNK
    nc = s // L
    log_a = (dt * a).reshape(bsz, nc, L, g, r)
    xdt = (x.astype(jnp.float32) * dt[..., None]).reshape(bsz, nc, L, g, r, p)
    bc = b_in.astype(jnp.float32).reshape(bsz, nc, L, g, n)
    cc = c_in.astype(jnp.float32).reshape(bsz, nc, L, g, n)
    a_cum = jnp.cumsum(log_a, axis=2)
    seg = a_cum[:, :, :, None] - a_cum[:, :, None, :]
    causal = jnp.tril(jnp.ones((L, L), bool))[None, None, :, :, None, None]
    decay = jnp.where(causal, jnp.exp(jnp.minimum(seg, 0.0)), 0.0)
    cb = jnp.einsum('bclgn,bcsgn->bclsg', cc, bc)
    y_diag = jnp.einsum('bclsgr,bcsgrp->bclgrp', cb[..., None] * decay, xdt)
    decay_to_end = jnp.exp(a_cum[:, :, -1:] - a_cum)
    chunk_states = jnp.einsum('bclgn,bclgrp->bcgrpn', bc, xdt * decay_to_end[..., None])
    chunk_decay = jnp.exp(a_cum[:, :, -1])

    def step(state, inp):
        st, dec = inp
        return state * dec[..., None, None] + st, state

    h0 = jnp.zeros((bsz, g, r, p, n), jnp.float32)
    _, prev = lax.scan(step, h0, (chunk_states.transpose(1, 0, 2, 3, 4, 5),
                                  chunk_decay.transpose(1, 0, 2, 3)))
    prev = prev.transpose(1, 0, 2, 3, 4, 5)
    y_off = jnp.einsum('bclgn,bcgrpn->bclgrp', cc, prev) * jnp.exp(a_cum)[..., None]
    return (y_diag + y_off).reshape(bsz, s, h, p)


def gla_chunked(q, k, v, log_a):
    bsz, h, s, dk = q.shape
    dv = v.shape[-1]
    C = GLA_CHUNK
    nc = s // C

    def to_chunks(t):
        return t.astype(jnp.float32).reshape(bsz, h, nc, C, t.shape[-1]).transpose(2, 0, 1, 3, 4)

    causal = jnp.tril(jnp.ones((C, C), bool))[:, :, None]

    def step(state, inp):
        qc, kc, vc, gc = inp
        bcum = jnp.cumsum(gc, axis=2)
        diff = bcum[:, :, :, None, :] - bcum[:, :, None, :, :]
        decay = jnp.where(causal, jnp.exp(jnp.minimum(diff, 0.0)), 0.0)
        attn = jnp.einsum('bhijd,bhjd->bhij', qc[:, :, :, None, :] * decay, kc)
        o = (jnp.einsum('bhij,bhje->bhie', attn, vc)
             + jnp.einsum('bhid,bhde->bhie', qc * jnp.exp(bcum), state))
        b_last = bcum[:, :, -1:, :]
        state = (state * jnp.exp(b_last)[:, :, 0, :, None]
                 + jnp.einsum('bhjd,bhje->bhde', kc * jnp.exp(b_last - bcum), vc))
        return state, o

    s0 = jnp.zeros((bsz, h, dk, dv), jnp.float32)
    _, o = lax.scan(step, s0, (to_chunks(q), to_chunks(k), to_chunks(v), to_chunks(log_a)))
    return o.transpose(1, 2, 0, 3, 4).reshape(bsz, h, s, dv)


def causal_block_attention(q, k, v, scale, log_f_cum=None):
    bsz, h, s, dk = q.shape
    dv = v.shape[-1]
    nb = s // Q_BLOCK
    qb = q.reshape(bsz, h, nb, Q_BLOCK, dk).transpose(2, 0, 1, 3, 4)
    key_pos = jnp.arange(s)

    def one_block(args):
        i, q_blk = args
        start = i * Q_BLOCK
        logits = jnp.einsum('bhqd,bhkd->bhqk', q_blk, k).astype(jnp.float32) * scale
        if log_f_cum is not None:
            fq = lax.dynamic_slice_in_dim(log_f_cum, start, Q_BLOCK, axis=2)
            logits = logits + fq[..., :, None] - log_f_cum[..., None, :]
        q_pos = start + jnp.arange(Q_BLOCK)
        logits = jnp.where(key_pos[None, :] <= q_pos[:, None], logits, -jnp.inf)
        probs = jax.nn.softmax(logits, axis=-1).astype(v.dtype)
        return jnp.einsum('bhqk,bhkd->bhqd', probs, v)

    out = lax.map(one_block, (jnp.arange(nb), qb))
    return out.transpose(1, 2, 0, 3, 4).reshape(bsz, h, s, dv)


def apply_rope(t, cos, sin):
    half = MLA_ROPE // 2
    tf = t.astype(jnp.float32)
    t1, t2 = tf[..., :half], tf[..., half:]
    return jnp.concatenate([t1 * cos - t2 * sin, t1 * sin + t2 * cos], axis=-1).astype(t.dtype)


def hybrid_layer(x, positions, pre_norm, post_norm, w_in, conv_w, conv_b, dt_bias, a_log, d_skip,
                 ssm_norm, fgate_b, gla_w2, gla_b, gla_norm, q_norm, kv_norm, w_uq, w_ukv,
                 w_branch, w_out):
    bsz, s, _ = x.shape
    f32 = jnp.float32
    h = rmsnorm(x, pre_norm)
    proj = h @ w_in
    (m_z, m_xbc, m_dt, f_q, f_k, f_v, f_f, f_gate, g_q, g_k, g_v, g_lr, g_gate,
     l_cq, l_ckv, l_kr, l_gate, merge) = jnp.split(
        proj, np.cumsum(IN_SPLITS)[:-1].tolist(), axis=-1)

    def heads(t, nh):
        return t.reshape(bsz, s, nh, -1).transpose(0, 2, 1, 3)

    xbc = jax.nn.silu(causal_depthwise_conv(m_xbc, conv_w, conv_b))
    xs, bs, cs = jnp.split(xbc, [BRANCH_WIDTH, BRANCH_WIDTH + SSD_GROUPS * SSD_STATE], axis=-1)
    xs = xs.reshape(bsz, s, SSD_HEADS, SSD_HEAD_DIM)
    dt = jax.nn.softplus(m_dt.astype(f32) + dt_bias.astype(f32))
    a = -jnp.exp(a_log.astype(f32))
    y = ssd_chunked(xs, dt, a, bs.reshape(bsz, s, SSD_GROUPS, SSD_STATE),
                    cs.reshape(bsz, s, SSD_GROUPS, SSD_STATE))
    y = y + xs.astype(f32) * d_skip.astype(f32)[:, None]
    y = (y.reshape(bsz, s, BRANCH_WIDTH) * jax.nn.silu(m_z.astype(f32))).astype(x.dtype)
    out_a = rmsnorm(y.reshape(bsz, s, SSD_GROUPS, -1),
                    ssm_norm.reshape(SSD_GROUPS, -1)).reshape(bsz, s, BRANCH_WIDTH)

    log_f = jax.nn.log_sigmoid(f_f.astype(f32) + fgate_b.astype(f32))
    f_cum = jnp.cumsum(log_f, axis=1).transpose(0, 2, 1)
    o = causal_block_attention(heads(f_q, FOX_HEADS), heads(f_k, FOX_HEADS), heads(f_v, FOX_HEADS),
                               FOX_HEAD_DIM ** -0.5, f_cum)
    out_b = o.transpose(0, 2, 1, 3).reshape(bsz, s, BRANCH_WIDTH) * jax.nn.silu(f_gate)

    log_alpha = jax.nn.log_sigmoid((g_lr @ gla_w2 + gla_b).astype(f32)) / GLA_GATE_TAU
    o = gla_chunked(heads(g_q, GLA_HEADS).astype(f32) * GLA_KEY_DIM ** -0.5,
                    heads(g_k, GLA_HEADS), heads(g_v, GLA_HEADS), heads(log_alpha, GLA_HEADS))
    o = o.transpose(0, 2, 1, 3).astype(x.dtype)
    out_c = rmsnorm(o, gla_norm).reshape(bsz, s, BRANCH_WIDTH) * jax.nn.silu(g_gate)

    cq = rmsnorm(l_cq, q_norm)
    ckv = rmsnorm(l_ckv, kv_norm)
    q = (cq @ w_uq).reshape(bsz, s, MLA_HEADS, MLA_NOPE + MLA_ROPE)
    kv = (ckv @ w_ukv).reshape(bsz, s, MLA_HEADS, MLA_NOPE + MLA_V)
    inv_freq = 1.0 / (ROPE_BASE ** (jnp.arange(MLA_ROPE // 2, dtype=f32) * 2.0 / MLA_ROPE))
    ang = positions.astype(f32)[..., None] * inv_freq
    cos, sin = jnp.cos(ang), jnp.sin(ang)
    q_rope = apply_rope(q[..., MLA_NOPE:], cos[:, :, None], sin[:, :, None])
    k_rope = apply_rope(l_kr, cos, sin)
    q_full = jnp.concatenate([q[..., :MLA_NOPE], q_rope], axis=-1).transpose(0, 2, 1, 3)
    k_full = jnp.concatenate(
        [kv[..., :MLA_NOPE], jnp.broadcast_to(k_rope[:, :, None], (bsz, s, MLA_HEADS, MLA_ROPE))],
        axis=-1).transpose(0, 2, 1, 3)
    v = kv[..., MLA_NOPE:].transpose(0, 2, 1, 3)
    o = causal_block_attention(q_full, k_full, v, (MLA_NOPE + MLA_ROPE) ** -0.5)
    out_d = o.transpose(0, 2, 1, 3).reshape(bsz, s, BRANCH_WIDTH) * jax.nn.silu(l_gate)

    gates = jax.nn.sigmoid(merge.reshape(bsz, s, N_BRANCHES, D_MODEL))
    mixed = (gates[:, :, 0] * (out_a @ w_branch[0])
             + gates[:, :, 1] * (out_b @ w_branch[1])
             + gates[:, :, 2] * (out_c @ w_branch[2])
             + gates[:, :, 3] * (out_d @ w_branch[3]))
    return x + rmsnorm(mixed @ w_out, post_norm)


def setup_inputs(seed: int = 0) -> dict:
    key = jax.random.key(seed)
    ks = jax.random.split(key, 24)
    L = DEPTH
    nrm = jax.random.normal
    dt0 = jnp.exp(jax.random.uniform(ks[7], (L, SSD_HEADS)) * (np.log(0.1) - np.log(0.001)) + np.log(0.001))
    offsets = jax.random.randint(ks[1], (BATCH, 1), 0, 1024)
    return {
        'x': nrm(ks[0], (BATCH, SEQ, D_MODEL), jnp.float32),
        'positions': (offsets + jnp.arange(SEQ)[None, :]).astype(jnp.int32),
        'pre_norm': 1.0 + 0.02 * nrm(ks[2], (L, D_MODEL)),
        'post_norm': 1.0 + 0.02 * nrm(ks[3], (L, D_MODEL)),
        'w_in': nrm(ks[4], (L, D_MODEL, N_IN)) * D_MODEL ** -0.5,
        'conv_w': nrm(ks[5], (L, SSD_CONV, SSD_XBC)) * SSD_CONV ** -0.5,
        'conv_b': 0.02 * nrm(ks[6], (L, SSD_XBC)),
        'dt_bias': dt0 + jnp.log(-jnp.expm1(-dt0)),
        'a_log': jnp.log(jax.random.uniform(ks[8], (L, SSD_HEADS), minval=1.0, maxval=16.0)),
        'd_skip': 1.0 + 0.1 * nrm(ks[9], (L, SSD_HEADS)),
        'ssm_norm': 1.0 + 0.02 * nrm(ks[10], (L, BRANCH_WIDTH)),
        'fgate_b': jax.random.uniform(ks[11], (L, FOX_HEADS), minval=1.0, maxval=4.0),
        'gla_w2': nrm(ks[12], (L, GLA_GATE_RANK, GLA_HEADS * GLA_KEY_DIM)) * GLA_GATE_RANK ** -0.5,
        'gla_b': 0.02 * nrm(ks[13], (L, GLA_HEADS * GLA_KEY_DIM)),
        'gla_norm': 1.0 + 0.02 * nrm(ks[14], (L, GLA_VAL_DIM)),
        'q_norm': 1.0 + 0.02 * nrm(ks[15], (L, MLA_Q_RANK)),
        'kv_norm': 1.0 + 0.02 * nrm(ks[16], (L, MLA_KV_RANK)),
        'w_uq': nrm(ks[17], (L, MLA_Q_RANK, MLA_HEADS * (MLA_NOPE + MLA_ROPE))) * MLA_Q_RANK ** -0.5,
        'w_ukv': nrm(ks[18], (L, MLA_KV_RANK, MLA_HEADS * (MLA_NOPE + MLA_V))) * MLA_KV_RANK ** -0.5,
        'w_branch': nrm(ks[19], (L, N_BRANCHES, BRANCH_WIDTH, D_MODEL)) * BRANCH_WIDTH ** -0.5,
        'w_out': nrm(ks[20], (L, D_MODEL, D_MODEL)) * D_MODEL ** -0.5,
    }


def reference(x, positions, pre_norm, post_norm, w_in, conv_w, conv_b, dt_bias, a_log, d_skip,
              ssm_norm, fgate_b, gla_w2, gla_b, gla_norm, q_norm, kv_norm, w_uq, w_ukv,
              w_branch, w_out):
    for l in range(DEPTH):
        x = hybrid_layer(x, positions, pre_norm[l], post_norm[l], w_in[l], conv_w[l], conv_b[l],
                         dt_bias[l], a_log[l], d_skip[l], ssm_norm[l], fgate_b[l], gla_w2[l],
                         gla_b[l], gla_norm[l], q_norm[l], kv_norm[l], w_uq[l], w_ukv[l],
                         w_branch[l], w_out[l])
    return x
```

```python
import numpy as np
from contextlib import ExitStack
import concourse.bass as bass
import concourse.mybir as mybir
from concourse.bass_utils import run_bass_kernel_spmd

F32 = mybir.dt.float32
BF16 = mybir.dt.bfloat16
AF = mybir.ActivationFunctionType
ALU = mybir.AluOpType
AX = mybir.AxisListType

D = 2048
N_IN = 20080
EPS = 1e-6


class Dep:
    __slots__ = ("w", "r", "sem", "cnt", "name", "dead", "key")

    def __init__(self, name=""):
        self.w = {}
        self.r = {}
        self.sem = None
        self.cnt = 0
        self.name = name
        self.dead = False
        self.key = None


class StopEmit(Exception):
    pass


class KB:
    def __init__(self, nc):
        self.nc = nc
        self.E = dict(pe=nc.tensor, dve=nc.vector, act=nc.scalar, pool=nc.gpsimd, sp=nc.sync)
        self.sem = {}
        self.cnt = {}
        for e in ("pe", "dve", "act", "pool"):
            self.sem[e] = nc.alloc_semaphore("sem_" + e)
            self.cnt[e] = 0
        self.seen = {e: {} for e in self.E}
        self.slots = []
        self.ninstr = 0
        self.nops = 0
        self.limit = None
        self.uid = 0
        self.nslot = 0
        self.free_sems = []

    def u(self, name):
        self.uid += 1
        return "%s_%d" % (name, self.uid)

    def _need(self, e, reads, writes):
        need = {}

        def add(evs, skip_same):
            for k, (sem, val, slot) in evs.items():
                if k == e and (skip_same or e == "pe"):
                    continue
                if slot is not None and not slot.dead:
                    val = slot.cnt
                if need.get(k, (None, 0))[1] < val:
                    need[k] = (sem, val)

        for d in reads:
            add(d.w, False)
        for d in writes:
            add(d.w, False)
            add(d.r, True)
        return need

    def _waits(self, e, need):
        for k, (sem, val) in need.items():
            if self.seen[e].get(k, 0) >= val:
                continue
            self.E[e].wait_ge(sem, val)
            self.seen[e][k] = val
            self.ninstr += 1

    def op(self, e, fn, reads=(), writes=(), inc=True):
        if self.limit is not None and self.nops >= self.limit and inc and e != "pe":
            raise StopEmit()
        self.nops += 1
        self._waits(e, self._need(e, reads, writes))
        ins = fn(self.E[e])
        self.ninstr += 1
        if inc:
            self.cnt[e] += 1
            ins.then_inc(self.sem[e], 1)
            val = self.cnt[e]
        else:
            val = self.cnt[e] + 1
        ev = (self.sem[e], val, None)
        for d in reads:
            d.r[e] = ev
        for d in writes:
            d.w = {e: ev}
            d.r = {}
        return ins

    def dma(self, q, out, in_, reads=(), writes=(), slot=None, merge_w=False):
        if slot is None:
            slot = writes[0] if writes else reads[0]
        need = self._need(q, reads, () if merge_w else writes)
        if merge_w:
            for d in writes:
                for k, (sem, val, sl) in d.r.items():
                    if sl is not None and not sl.dead:
                        val = sl.cnt
                    if need.get(k, (None, 0))[1] < val:
                        need[k] = (sem, val)
        self._waits(q, need)
        assert not slot.dead
        if slot.sem is None:
            self.nslot += 1
            slot.key = "dma%d" % self.nslot
            if self.free_sems:
                slot.sem, slot.cnt = self.free_sems.pop()
            else:
                slot.sem = self.nc.alloc_semaphore("dsem%d" % self.nslot)
            self.slots.append(slot)
        ins = self.E[q].dma_start(out=out, in_=in_)
        self.ninstr += 1
        slot.cnt += 16
        ins.then_inc(slot.sem, 16)
        key = slot.key
        ev = (slot.sem, slot.cnt, slot)
        for d in reads:
            d.r[key] = ev
        for d in writes:
            if merge_w:
                d.w[key] = ev
            else:
                d.w = {key: ev}
                d.r = {}
        return ins

    def barrier(self, engines=("pe", "dve", "act", "pool", "sp")):
        need = {}
        for e in ("pe", "dve", "act", "pool"):
            if self.cnt[e]:
                need[e] = (self.sem[e], self.cnt[e])
        for s in self.slots:
            need[s.key] = (s.sem, s.cnt)
        for e in engines:
            n2 = {k: v for k, v in need.items() if k != e}
            self._waits(e, n2)
        if len(engines) == 5:
            for s in self.slots:
                s.dead = True
                self.free_sems.append((s.sem, s.cnt))
            self.slots = []


def _f32(a):
    return np.ascontiguousarray(np.asarray(a, dtype=np.float32))


C_MZ, C_XBC, C_DT = 0, 1024, 2560
C_FQ, C_FK, C_FV, C_FF, C_FG = 2576, 3600, 4624, 5648, 5664
C_GQ, C_GK, C_GV, C_GLR, C_GG = 6688, 7200, 7712, 8736, 8752
C_LCQ, C_LCKV, C_LKR, C_LG = 9776, 10288, 10800, 10864
C_MERGE = 11888

TM_BLOCKS = [
    (C_MZ, True), (C_MZ + 512, True),
    (C_FV, False), (C_FV + 512, False),
    (C_FG, True), (C_FG + 512, True),
    (C_GQ, False), (C_GK, False),
    (C_GV, False), (C_GV + 512, False),
    (C_GG, True), (C_GG + 512, True),
    (C_LCQ, False), (C_LCKV, False),
    (C_LG, True), (C_LG + 512, True),
]
TM_Z, TM_FV, TM_FG, TM_GQ, TM_GK, TM_GV, TM_GG, TM_LCQ, TM_LCKV, TM_LG = (
    0, 1024, 2048, 3072, 3584, 4096, 5120, 6144, 6656, 7168)
NTM = len(TM_BLOCKS) * 512
FM_XBC, FM_FQ, FM_FK, FM_MERGE = 0, 1536, 2560, 3584
NFM = 11776


def host_layout_win(w_in_l):
    w = w_in_l
    tm = np.concatenate([w[:, c:c + 512] for c, _ in TM_BLOCKS], axis=1)
    small = np.zeros((D, 128), np.float32)
    small[:, 0:16] = w[:, C_DT:C_DT + 16]
    small[:, 16:32] = w[:, C_FF:C_FF + 16]
    small[:, 32:48] = w[:, C_GLR:C_GLR + 16]
    small[:, 48:112] = w[:, C_LKR:C_LKR + 64]
    fm = np.concatenate([w[:, C_XBC:C_XBC + 1536], w[:, C_FQ:C_FQ + 1024],
                         w[:, C_FK:C_FK + 1024], w[:, C_MERGE:C_MERGE + 8192]], axis=1)
    return np.ascontiguousarray(tm), small, np.ascontiguousarray(fm)


def phase_P(kb, S, x_ap, gamma_ap, wtm_ap, wsm_ap, wfm_ap, tm, tms, fm, ident):
    nc = kb.nc
    TT = 1024 if S >= 1024 else S
    NT = S // TT
    NB = TT // 128
    NH = TT // 512
    with ExitStack() as es:
        def sb(name, shape, dt):
            return es.enter_context(nc.sbuf_tensor(kb.u(name), shape, dt))

        def ps(name, shape, dt):
            return es.enter_context(nc.psum_tensor(kb.u(name), shape, dt))

        gam = sb("P_gam", [128, D], F32)
        xt = [sb("P_xt%d" % i, [128, D], F32) for i in range(2)]
        sq = sb("P_sq", [128, D], F32)
        st = [sb("P_st%d" % i, [128, 4], F32) for i in range(2)]
        hb = [sb("P_h%d" % i, [128, D], BF16) for i in range(2)]
        hT = sb("P_hT", [128, 16, TT], BF16)
        wt = [sb("P_wt%d" % i, [128, 16, 512], BF16) for i in range(2)]
        wsmall = sb("P_wsm", [128, 16, 128], BF16)
        stg = [sb("P_stg%d" % i, [128, NB, 512], BF16) for i in range(2)]
        stgs = sb("P_stgs", [128, NB, 128], F32)
        stgF = [sb("P_stgF%d" % i, [128, 4, TT], BF16) for i in range(2)]
        pT = [ps("P_pT%d" % i, [128, D], BF16) for i in range(1)]
        pm = [ps("P_pm%d" % i, [128, 512], F32) for i in range(4)]

        d_gam = Dep(); d_xt = [Dep(), Dep()]; d_sq = Dep(); d_st = [Dep(), Dep()]
        d_hb = [Dep(), Dep()]; d_hT = Dep(); d_wt = [Dep(), Dep()]; d_ws = Dep()
        d_stg = [Dep(), Dep()]; d_stgs = Dep(); d_stgF = [Dep(), Dep()]
        d_pT = [Dep()]; d_pm = [Dep() for _ in range(4)]
        d_tm = Dep(); d_tms = Dep(); d_fm = Dep()

        kb.dma("sp", gam[:], gamma_ap.partition_broadcast(128), writes=[d_gam])
        pmi = 0
        wti = 0
        sgi = 0
        sfi = 0
        wtm_v = wtm_ap.rearrange("(kc p) n -> p kc n", p=128)
        wsm_v = wsm_ap.rearrange("(kc p) n -> p kc n", p=128)
        wfm_v = wfm_ap.rearrange("(kc p) n -> p kc n", p=128)
        for tt in range(NT):
            t0 = tt * TT
            for tb in range(NB):
                b = tb % 2
                kb.dma("sp", xt[b][:], x_ap[t0 + tb * 128:t0 + (tb + 1) * 128, :], writes=[d_xt[b]])
                kb.op("act", lambda e: e.activation(out=sq[:], in_=xt[b][:], func=AF.Square, scale=float(D ** -0.5)),
                      reads=[d_xt[b]], writes=[d_sq])
                kb.op("dve", lambda e: e.reduce_sum(out=st[b][:, 0:1], in_=sq[:], axis=AX.X),
                      reads=[d_sq], writes=[d_st[b]])
                kb.op("act", lambda e: e.activation(out=st[b][:, 1:2], in_=st[b][:, 0:1], func=AF.Sqrt, bias=EPS),
                      reads=[d_st[b]], writes=[d_st[b]])
                kb.op("dve", lambda e: e.reciprocal(out=st[b][:, 2:3], in_=st[b][:, 1:2]),
                      reads=[d_st[b]], writes=[d_st[b]])
                kb.op("dve", lambda e: e.scalar_tensor_tensor(out=hb[b][:], in0=xt[b][:], scalar=st[b][:, 2:3],
                                                              in1=gam[:], op0=ALU.mult, op1=ALU.mult),
                      reads=[d_xt[b], d_st[b], d_gam], writes=[d_hb[b]])
                for kc in range(16):
                    kb.op("pe", lambda e: e.transpose(out=pT[0][:, kc * 128:(kc + 1) * 128],
                                                      in_=hb[b][:, kc * 128:(kc + 1) * 128], identity=ident[0][:]),
                          reads=[d_hb[b], ident[1]], writes=[d_pT[0]], inc=(kc == 15))
                kb.op("act", lambda e: e.activation(out=hT[:, :, tb * 128:(tb + 1) * 128],
                                                    in_=pT[0][:].rearrange("p (k t) -> p k t", k=16), func=AF.Copy),
                      reads=[d_pT[0]], writes=[d_hT])

            for blk in range(len(TM_BLOCKS) + 1):
                small = blk == len(TM_BLOCKS)
                if small:
                    kb.dma("pool", wsmall[:], wsm_v, writes=[d_ws])
                    w_t, d_w, ncol = wsmall, d_ws, 128
                else:
                    wb = wti % 2; wti += 1
                    kb.dma("pool", wt[wb][:], wtm_v[:, :, blk * 512:(blk + 1) * 512], writes=[d_wt[wb]])
                    w_t, d_w, ncol = wt[wb], d_wt[wb], 512
                    sg = sgi % 2; sgi += 1
                for tb in range(NB):
                    pi = pmi % 4; pmi += 1
                    for kc in range(16):
                        kb.op("pe", lambda e: e.matmul(pm[pi][:, 0:ncol], hT[:, kc, tb * 128:(tb + 1) * 128],
                                                       w_t[:, kc, :], start=(kc == 0), stop=(kc == 15)),
                              reads=[d_hT, d_w], writes=[d_pm[pi]], inc=(kc == 15))
                    if small:
                        kb.op("dve", lambda e: e.tensor_copy(out=stgs[:, tb, :], in_=pm[pi][:, 0:128]),
                              reads=[d_pm[pi]], writes=[d_stgs])
                    elif TM_BLOCKS[blk][1]:
                        kb.op("act", lambda e: e.activation(out=stg[sg][:, tb, :], in_=pm[pi][:], func=AF.Silu),
                              reads=[d_pm[pi]], writes=[d_stg[sg]])
                    else:
                        kb.op("dve", lambda e: e.tensor_copy(out=stg[sg][:, tb, :], in_=pm[pi][:]),
                              reads=[d_pm[pi]], writes=[d_stg[sg]])
                if small:
                    kb.dma("sp", tms[t0:t0 + TT, :].rearrange("(tb p) n -> p tb n", p=128), stgs[:],
                           reads=[d_stgs], writes=[d_tms], slot=d_stgs, merge_w=True)
                else:
                    kb.dma("sp", tm[t0:t0 + TT, blk * 512:(blk + 1) * 512].rearrange("(tb p) n -> p tb n", p=128),
                           stg[sg][:], reads=[d_stg[sg]], writes=[d_tm], slot=d_stg[sg], merge_w=True)

            for blk in range(NFM // 512):
                wb = wti % 2; wti += 1
                kb.dma("pool", wt[wb][:], wfm_v[:, :, blk * 512:(blk + 1) * 512], writes=[d_wt[wb]])
                sf = sfi % 2; sfi += 1
                r0 = blk * 512
                for cb in range(4):
                    for th in range(NH):
                        pi = pmi % 4; pmi += 1
                        for kc in range(16):
                            kb.op("pe", lambda e: e.matmul(pm[pi][:], wt[wb][:, kc, cb * 128:(cb + 1) * 128],
                                                           hT[:, kc, th * 512:(th + 1) * 512],
                                                           start=(kc == 0), stop=(kc == 15)),
                                  reads=[d_hT, d_wt[wb]], writes=[d_pm[pi]], inc=(kc == 15))
                        dst = stgF[sf][:, cb, th * 512:(th + 1) * 512]
                        if r0 >= FM_MERGE:
                            kb.op("act", lambda e: e.activation(out=dst, in_=pm[pi][:], func=AF.Sigmoid),
                                  reads=[d_pm[pi]], writes=[d_stgF[sf]])
                        elif FM_FQ <= r0 < FM_FK:
                            kb.op("dve", lambda e: e.tensor_scalar(out=dst, in0=pm[pi][:], scalar1=0.125,
                                                                   scalar2=None, op0=ALU.mult),
                                  reads=[d_pm[pi]], writes=[d_stgF[sf]])
                        else:
                            kb.op("dve", lambda e: e.tensor_copy(out=dst, in_=pm[pi][:]),
                                  reads=[d_pm[pi]], writes=[d_stgF[sf]])
                kb.dma("sp", fm[r0:r0 + 512, t0:t0 + TT].rearrange("(cb p) t -> p cb t", p=128), stgF[sf][:],
                       reads=[d_stgF[sf]], writes=[d_fm], slot=d_stgF[sf], merge_w=True)
        kb.barrier()


def load_consts(kb, es, cst_ap):
    nc = kb.nc
    d = Dep()
    C = {"dep": d}
    for i, (name, dt) in enumerate([("ident", BF16), ("U", F32), ("ones", F32), ("U64", F32), ("U64s", F32), ("BO64s", F32)]):
        t = es.enter_context(nc.sbuf_tensor("c_" + name, [128, 128], dt))
        kb.dma("pool", t[:], cst_ap[i], writes=[d], slot=d, merge_w=True)
        C[name] = t
    t = es.enter_context(nc.sbuf_tensor("c_Ub", [128, 128], BF16))
    kb.dma("pool", t[:], cst_ap[1], writes=[d], slot=d, merge_w=True)
    C["Ub"] = t
    t = es.enter_context(nc.sbuf_tensor("c_identf", [128, 128], F32))
    kb.dma("pool", t[:], cst_ap[0], writes=[d], slot=d, merge_w=True)
    C["identf"] = t
    return C


def host_consts():
    c = np.zeros((6, 128, 128), np.float32)
    c[0] = np.eye(128)
    c[1] = np.triu(np.ones((128, 128)))
    c[2] = 1.0
    u64 = np.triu(np.ones((64, 64)))
    c[3, 0:64, 0:64] = u64
    c[3, 64:128, 64:128] = u64
    c[4] = c[3] / 16.0
    c[5, 0:64, 0:64] = 1.0 / 16.0
    c[5, 64:128, 64:128] = 1.0 / 16.0
    return c


def attention_core(kb, es, S, C, groups, load_head, finish_head, bias_for, dv, tagp):
    nc = kb.nc
    NBLK = S // 128
    pS = [es.enter_context(nc.psum_tensor(kb.u(tagp + "_pS%d" % i), [128, 128], F32)) for i in range(3)]
    pO = [es.enter_context(nc.psum_tensor(kb.u(tagp + "_pO%d" % i), [128, dv + 1], F32)) for i in range(2)]
    pt = [es.enter_context(nc.sbuf_tensor(kb.u(tagp + "_pt%d" % i), [128, 128], BF16)) for i in range(4)]
    d_pS = [Dep() for _ in pS]
    d_pO = [Dep() for _ in pO]
    d_pt = [Dep() for _ in pt]
    LA = 2
    cnt = {"i": 0, "o": 0}
    for grp in groups:
        loaded = {h: load_head(h) for h in grp}
        tasks = []
        for qb in range(NBLK):
            for h in grp:
                for kbk in range(qb + 1):
                    tasks.append((h, qb, kbk))
        info = {}
        n = len(tasks)
        for i in range(n + LA):
            if i < n:
                h, qb, kbk = tasks[i]
                kparts, qparts, vaug_fn, hdeps = loaded[h]
                if kbk == 0:
                    bias_fn, bdeps = bias_for(h, qb)
                    o = cnt["o"] % 2; cnt["o"] += 1
                    info[(h, qb)] = (bias_fn, bdeps, o)
                bias_fn, bdeps, o = info[(h, qb)]
                gi = cnt["i"]; cnt["i"] += 1
                s_ = gi % 3
                t_ = gi % 4
                npart = len(kparts)
                for pi_ in range(npart):
                    kb.op("pe", lambda e: e.matmul(pS[s_][:], kparts[pi_](kbk), qparts[pi_](qb),
                                                   start=(pi_ == 0), stop=(pi_ == npart - 1)),
                          reads=hdeps, writes=[d_pS[s_]], inc=(pi_ == npart - 1))
                if bias_fn is not None:
                    b_ap = bias_fn(kbk)
                    kb.op("act", lambda e: e.activation(out=pt[t_][:], in_=pS[s_][:], func=AF.Exp, bias=b_ap),
                          reads=[d_pS[s_]] + bdeps, writes=[d_pt[t_]])
                else:
                    kb.op("act", lambda e: e.activation(out=pt[t_][:], in_=pS[s_][:], func=AF.Exp),
                          reads=[d_pS[s_]], writes=[d_pt[t_]])
                if kbk == qb:
                    kb.op("pool", lambda e: e.tensor_tensor(out=pt[t_][:], in0=pt[t_][:], in1=C["Ub"][:], op=ALU.mult),
                          reads=[d_pt[t_], C["dep"]], writes=[d_pt[t_]])
                tasks[i] = (h, qb, kbk, t_)
            if i >= LA:
                h, qb, kbk, t_ = tasks[i - LA]
                kparts, qparts, vaug_fn, hdeps = loaded[h]
                o = info[(h, qb)][2]
                kb.op("pe", lambda e: e.matmul(pO[o][:], pt[t_][:], vaug_fn(kbk), start=(kbk == 0), stop=(kbk == qb)),
                      reads=[d_pt[t_]] + hdeps, writes=[d_pO[o]], inc=(kbk == qb))
                if kbk == qb:
                    finish_head(h, qb, pO[o], d_pO[o])


def phase_B(kb, S, fm, tm, tms, fgb_ap, obr, C):
    nc = kb.nc
    NBLK = S // 128
    with ExitStack() as es:
        def sb(name, shape, dt):
            return es.enter_context(nc.sbuf_tensor(kb.u(name), shape, dt))

        G = sb("B_G", [128, NBLK, 16], F32)
        bt = sb("B_bt", [128, 16], F32)
        Tt = sb("B_Tt", [128, NBLK, 16], F32)
        Gc = sb("B_Gc", [128, NBLK, 16], F32)
        Gend = sb("B_Gend", [128, NBLK, 16], F32)
        d_G = Dep(); d_bt = Dep(); d_Tt = Dep(); d_Gc = Dep(); d_Gend = Dep()
        with ExitStack() as es0:
            psC = es0.enter_context(nc.psum_tensor(kb.u("B_psC"), [128, NBLK * 16], F32))
            psT = es0.enter_context(nc.psum_tensor(kb.u("B_psT"), [128, NBLK * 16], F32))
            d_psC = Dep(); d_psT = Dep()
            kb.dma("sp", G[:], tms[:, 16:32].rearrange("(b p) h -> p b h", p=128), writes=[d_G])
            kb.dma("sp", bt[:], fgb_ap.partition_broadcast(128), writes=[d_bt])
            kb.op("dve", lambda e: e.tensor_tensor(out=G[:], in0=G[:],
                                                   in1=bt[:].unsqueeze(1).broadcast_to([128, NBLK, 16]), op=ALU.add),
                  reads=[d_G, d_bt], writes=[d_G])
            kb.op("act", lambda e: e.activation(out=G[:], in_=G[:], func=AF.Exp, scale=-1.0), reads=[d_G], writes=[d_G])
            kb.op("act", lambda e: e.activation(out=G[:], in_=G[:], func=AF.Ln, bias=1.0), reads=[d_G], writes=[d_G])
            G2 = G[:].rearrange("p b h -> p (b h)")
            kb.op("pe", lambda e: e.matmul(psC[:], C["U"][:], G2, start=True, stop=True),
                  reads=[d_G, C["dep"]], writes=[d_psC])
            kb.op("pe", lambda e: e.matmul(psT[:], C["ones"][:], G2, start=True, stop=True),
                  reads=[d_G, C["dep"]], writes=[d_psT])
            kb.op("dve", lambda e: e.tensor_copy(out=Tt[:].rearrange("p b h -> p (b h)"), in_=psT[:]),
                  reads=[d_psT], writes=[d_Tt])
            kb.op("dve", lambda e: e.tensor_copy(out=Gend[:, 0, :], in_=Tt[:, 0, :]), reads=[d_Tt], writes=[d_Gend])
            for b in range(1, NBLK):
                kb.op("dve", lambda e: e.tensor_tensor(out=Gend[:, b, :], in0=Gend[:, b - 1, :], in1=Tt[:, b, :], op=ALU.add),
                      reads=[d_Tt, d_Gend], writes=[d_Gend])
            kb.op("dve", lambda e: e.tensor_copy(out=Gc[:, 0, :], in_=psC[:, 0:16]), reads=[d_psC], writes=[d_Gc])
            if NBLK > 1:
                kb.op("dve", lambda e: e.tensor_tensor(out=Gc[:, 1:, :], in0=psC[:, 16:].rearrange("p (b h) -> p b h", h=16),
                                                       in1=Gend[:, 0:NBLK - 1, :], op=ALU.add),
                      reads=[d_psC, d_Gend], writes=[d_Gc])
            kb.barrier()

        qT = [sb("B_qT%d" % i, [128, S], BF16) for i in range(2)]
        kT = [sb("B_kT%d" % i, [128, S], BF16) for i in range(2)]
        vr = [sb("B_vr%d" % i, [128, NBLK, 128], BF16) for i in range(2)]
        va = [sb("B_va%d" % i, [128, NBLK, 2, 72], BF16) for i in range(2)]
        gt = [sb("B_gt%d" % i, [128, NBLK, 128], BF16) for i in range(2)]
        obT = [sb("B_obT%d" % i, [128, S], BF16) for i in range(2)]
        ob = [sb("B_ob%d" % i, [128, 128], BF16) for i in range(2)]
        rs = [sb("B_rs%d" % i, [128, 1], F32) for i in range(4)]
        bias = [sb("B_bias%d" % i, [128, NBLK], F32) for i in range(4)]
        pTr = es.enter_context(nc.psum_tensor(kb.u("B_pTr"), [128, 128], BF16))
        d_q = [Dep(), Dep()]; d_k = [Dep(), Dep()]; d_vr = [Dep(), Dep()]; d_va = [Dep(), Dep()]
        d_gt = [Dep(), Dep()]; d_obT = [Dep(), Dep()]; d_ob = [Dep(), Dep()]
        d_rs = [Dep() for _ in rs]; d_bias = [Dep() for _ in bias]; d_pTr = Dep(); d_obr = Dep()
        for i in range(2):
            kb.op("pool", lambda e: e.memset(va[i][:], 1.0), writes=[d_va[i]])
        state = {"bi": 0, "ri": 0, "obi": 0}

        def load_head(h):
            hp, hh = h // 2, h % 2
            b = hp % 2
            if hh == 0:
                kb.dma("sp", qT[b][:], fm[FM_FQ + hp * 128:FM_FQ + (hp + 1) * 128, :], writes=[d_q[b]])
                kb.dma("sp", kT[b][:], fm[FM_FK + hp * 128:FM_FK + (hp + 1) * 128, :], writes=[d_k[b]])
                kb.dma("sp", vr[b][:], tm[:, TM_FV + hp * 128:TM_FV + (hp + 1) * 128].rearrange("(b p) c -> p b c", p=128),
                       writes=[d_vr[b]])
                kb.dma("sp", gt[b][:], tm[:, TM_FG + hp * 128:TM_FG + (hp + 1) * 128].rearrange("(b p) c -> p b c", p=128),
                       writes=[d_gt[b]])
                kb.op("dve", lambda e: e.tensor_copy(out=va[b][:, :, :, 0:64],
                                                     in_=vr[b][:].rearrange("p b (h c) -> p b h c", h=2)),
                      reads=[d_vr[b]], writes=[d_va[b]])
            p0 = hh * 64
            kparts = [lambda blk: kT[b][p0:p0 + 64, blk * 128:(blk + 1) * 128]]
            qparts = [lambda blk: qT[b][p0:p0 + 64, blk * 128:(blk + 1) * 128]]
            return kparts, qparts, (lambda blk: va[b][:, blk, hh, 0:65]), [d_q[b], d_k[b], d_va[b]]

        def bias_for(h, qb):
            bi = state["bi"] % 4; state["bi"] += 1
            kb.op("dve", lambda e: e.tensor_scalar(out=bias[bi][:, 0:qb + 1], in0=Gc[:, 0:qb + 1, h],
                                                   scalar1=Gend[:, qb, h:h + 1], scalar2=None, op0=ALU.subtract),
                  reads=[d_Gc, d_Gend], writes=[d_bias[bi]])
            return (lambda blk: bias[bi][:, blk:blk + 1]), [d_bias[bi]]

        def finish_head(h, qb, pO, d_pO):
            hp, hh = h // 2, h % 2
            b = hp % 2
            ri = state["ri"] % 4; state["ri"] += 1
            if hh == 0:
                state["obi"] += 1
            o = state["obi"] % 2
            kb.op("dve", lambda e: e.reciprocal(out=rs[ri][:], in_=pO[:, 64:65]), reads=[d_pO], writes=[d_rs[ri]])
            kb.op("dve", lambda e: e.scalar_tensor_tensor(out=ob[o][:, hh * 64:(hh + 1) * 64], in0=pO[:, 0:64],
                                                          scalar=rs[ri][:, 0:1], in1=gt[b][:, qb, hh * 64:(hh + 1) * 64],
                                                          op0=ALU.mult, op1=ALU.mult),
                  reads=[d_pO, d_rs[ri], d_gt[b]], writes=[d_ob[o]])
            if hh == 1:
                kb.op("pe", lambda e: e.transpose(out=pTr[:], in_=ob[o][:], identity=C["ident"][:]),
                      reads=[d_ob[o], C["dep"]], writes=[d_pTr])
                kb.op("dve", lambda e: e.tensor_copy(out=obT[b][:, qb * 128:(qb + 1) * 128], in_=pTr[:]),
                      reads=[d_pTr], writes=[d_obT[b]])
                if qb == NBLK - 1:
                    kb.dma("sp", obr[1, hp * 128:(hp + 1) * 128, :], obT[b][:], reads=[d_obT[b]], writes=[d_obr],
                           slot=d_obT[b], merge_w=True)

        attention_core(kb, es, S, C, [[2 * i, 2 * i + 1] for i in range(8)], load_head, finish_head, bias_for, 64, "B")
        kb.barrier()


TWO_PI = 6.283185307179586
CW1 = 6.28125
CW2 = TWO_PI - CW1


def setup_rope(kb, es, S, pos_ap, invf_ap):
    nc = kb.nc
    NBLK = S // 128
    cosT = es.enter_context(nc.sbuf_tensor("R_cos", [128, NBLK, 32], F32))
    sinT = es.enter_context(nc.sbuf_tensor("R_sin", [128, NBLK, 32], F32))
    d_rope = Dep()
    with ExitStack() as es0:
        def sb(name, shape, dt):
            return es0.enter_context(nc.sbuf_tensor(kb.u(name), shape, dt))
        posi = sb("R_posi", [128, NBLK], mybir.dt.int32)
        posf = sb("R_posf", [128, NBLK], F32)
        invf = sb("R_invf", [128, 32], F32)
        ang = sb("R_ang", [128, NBLK, 32], F32)
        tq = sb("R_tq", [128, NBLK, 32], F32)
        ni = sb("R_ni", [128, NBLK, 32], mybir.dt.int32)
        nf = sb("R_nf", [128, NBLK, 32], F32)
        r = sb("R_r", [128, NBLK, 32], F32)
        m = sb("R_m", [128, NBLK, 32], F32)
        d = Dep()
        kb.dma("sp", posi[:], pos_ap, writes=[d])
        kb.dma("sp", invf[:], invf_ap.partition_broadcast(128), writes=[d], slot=d, merge_w=True)
        kb.op("dve", lambda e: e.tensor_copy(out=posf[:], in_=posi[:]), reads=[d], writes=[d])
        kb.op("dve", lambda e: e.tensor_tensor(out=ang[:], in0=posf[:].unsqueeze(2).broadcast_to([128, NBLK, 32]),
                                               in1=invf[:].unsqueeze(1).broadcast_to([128, NBLK, 32]), op=ALU.mult),
              reads=[d], writes=[d])
        for which, dst in ((0, sinT), (1, cosT)):
            if which == 1:
                kb.op("dve", lambda e: e.tensor_scalar_add(out=ang[:], in0=ang[:], scalar1=float(np.pi / 2)),
                      reads=[d], writes=[d])
            kb.op("dve", lambda e: e.tensor_scalar_mul(out=tq[:], in0=ang[:], scalar1=float(1.0 / TWO_PI)), reads=[d], writes=[d])
            kb.op("dve", lambda e: e.tensor_copy(out=ni[:], in_=tq[:]), reads=[d], writes=[d])
            kb.op("dve", lambda e: e.tensor_copy(out=nf[:], in_=ni[:]), reads=[d], writes=[d])
            kb.op("dve", lambda e: e.scalar_tensor_tensor(out=r[:], in0=nf[:], scalar=-CW1, in1=ang[:], op0=ALU.mult, op1=ALU.add),
                  reads=[d], writes=[d])
            kb.op("dve", lambda e: e.scalar_tensor_tensor(out=r[:], in0=nf[:], scalar=-CW2, in1=r[:], op0=ALU.mult, op1=ALU.add),
                  reads=[d], writes=[d])
            kb.op("dve", lambda e: e.tensor_single_scalar(out=m[:], in_=r[:], scalar=float(np.pi), op=ALU.is_gt), reads=[d], writes=[d])
            kb.op("dve", lambda e: e.scalar_tensor_tensor(out=r[:], in0=m[:], scalar=-TWO_PI, in1=r[:], op0=ALU.mult, op1=ALU.add),
                  reads=[d], writes=[d])
            kb.op("dve", lambda e: e.tensor_single_scalar(out=m[:], in_=r[:], scalar=float(-np.pi), op=ALU.is_lt), reads=[d], writes=[d])
            kb.op("dve", lambda e: e.scalar_tensor_tensor(out=r[:], in0=m[:], scalar=TWO_PI, in1=r[:], op0=ALU.mult, op1=ALU.add),
                  reads=[d], writes=[d])
            kb.op("dve", lambda e: e.tensor_scalar(out=r[:], in0=r[:], scalar1=float(np.pi), scalar2=float(-np.pi),
                                                   op0=ALU.min, op1=ALU.max), reads=[d], writes=[d])
            kb.op("act", lambda e: e.activation(out=dst[:], in_=r[:], func=AF.Sin), reads=[d], writes=[d_rope])
        kb.barrier()
    return cosT, sinT, d_rope


def host_invf():
    return (1.0 / (np.float32(10000.0) ** (np.arange(32, dtype=np.float32) * np.float32(2.0) / np.float32(64)))).astype(
        np.float32).reshape(1, 32)


def emit_rmsnorm(kb, src, d_src, n, gam, d_gam, dst, d_dst, sq, d_sq, st, d_st):
    kb.op("act", lambda e: e.activation(out=sq, in_=src, func=AF.Square, scale=float(n ** -0.5)), reads=[d_src], writes=[d_sq])
    kb.op("dve", lambda e: e.reduce_sum(out=st[:, 0:1], in_=sq, axis=AX.X), reads=[d_sq], writes=[d_st])
    kb.op("act", lambda e: e.activation(out=st[:, 1:2], in_=st[:, 0:1], func=AF.Sqrt, bias=EPS), reads=[d_st], writes=[d_st])
    kb.op("dve", lambda e: e.reciprocal(out=st[:, 2:3], in_=st[:, 1:2]), reads=[d_st], writes=[d_st])
    kb.op("dve", lambda e: e.scalar_tensor_tensor(out=dst, in0=src, scalar=st[:, 2:3], in1=gam, op0=ALU.mult, op1=ALU.mult),
          reads=[d_src, d_st, d_gam], writes=[d_dst])


def emit_rotary(kb, src, d_src, H, cos_ap, sin_ap, d_rope, dst, d_dst, tmp, d_tmp):
    s3 = src.rearrange("p (h c) -> p h c", h=H)
    o3 = dst.rearrange("p (h c) -> p h c", h=H)
    t3 = tmp.rearrange("p (h c) -> p h c", h=H)
    cb = cos_ap.unsqueeze(1).broadcast_to([128, H, 32])
    sbc = sin_ap.unsqueeze(1).broadcast_to([128, H, 32])
    kb.op("dve", lambda e: e.tensor_tensor(out=t3[:, :, 0:32], in0=s3[:, :, 0:32], in1=cb, op=ALU.mult),
          reads=[d_src, d_rope], writes=[d_tmp])
    kb.op("dve", lambda e: e.tensor_tensor(out=t3[:, :, 32:64], in0=s3[:, :, 32:64], in1=sbc, op=ALU.mult),
          reads=[d_src, d_rope], writes=[d_tmp])
    kb.op("dve", lambda e: e.tensor_tensor(out=o3[:, :, 0:32], in0=t3[:, :, 0:32], in1=t3[:, :, 32:64], op=ALU.subtract),
          reads=[d_tmp], writes=[d_dst])
    kb.op("dve", lambda e: e.tensor_tensor(out=t3[:, :, 0:32], in0=s3[:, :, 0:32], in1=sbc, op=ALU.mult),
          reads=[d_src, d_rope], writes=[d_tmp])
    kb.op("dve", lambda e: e.tensor_tensor(out=t3[:, :, 32:64], in0=s3[:, :, 32:64], in1=cb, op=ALU.mult),
          reads=[d_src, d_rope], writes=[d_tmp])
    kb.op("dve", lambda e: e.tensor_tensor(out=o3[:, :, 32:64], in0=t3[:, :, 0:32], in1=t3[:, :, 32:64], op=ALU.add),
          reads=[d_tmp], writes=[d_dst])


MLA_SCALE = float(192 ** -0.5)


def phase_D(kb, S, tm, tms, qn_ap, kvn_ap, wuq_ap, wukv_ap, mq, mk, mv, obr, C, rope):
    nc = kb.nc
    NBLK = S // 128
    cosT, sinT, d_rope = rope
    TS = 512 if S >= 512 else S
    NJ = TS // 128
    d_mq = Dep(); d_mk = Dep(); d_mv = Dep()
    with ExitStack() as es:
        def sb(name, shape, dt):
            return es.enter_context(nc.sbuf_tensor(kb.u(name), shape, dt))

        def ps(name, shape, dt):
            return es.enter_context(nc.psum_tensor(kb.u(name), shape, dt))
        wuq = sb("D_wuq", [128, 4, 1536], BF16)
        wukv = sb("D_wukv", [128, 4, 2048], BF16)
        gq = sb("D_gq", [128, 512], F32)
        gkv = sb("D_gkv", [128, 512], F32)
        d_w = Dep()
        kb.dma("pool", wuq[:], wuq_ap.rearrange("(kc p) n -> p kc n", p=128), writes=[d_w], slot=d_w, merge_w=True)
        kb.dma("pool", wukv[:], wukv_ap.rearrange("(kc p) n -> p kc n", p=128), writes=[d_w], slot=d_w, merge_w=True)
        kb.dma("sp", gq[:], qn_ap.partition_broadcast(128), writes=[d_w], slot=d_w, merge_w=True)
        kb.dma("sp", gkv[:], kvn_ap.partition_broadcast(128), writes=[d_w], slot=d_w, merge_w=True)
        cqr = sb("D_cqr", [128, NJ, 512], BF16)
        ckvr = sb("D_ckvr", [128, NJ, 512], BF16)
        krr = sb("D_krr", [128, NJ, 64], F32)
        sq = sb("D_sq", [128, 512], F32)
        st = sb("D_st", [128, 4], F32)
        cn = [sb("D_cn%d" % i, [128, 1024], BF16) for i in range(2)]
        cT = sb("D_cT", [128, 8, TS], BF16)
        stgQ = sb("D_stgQ", [128, 8, TS], BF16)
        stgK = sb("D_stgK", [128, 8, TS], BF16)
        stgV = sb("D_stgV", [128, NJ, 1024], BF16)
        stgQR = sb("D_stgQR", [128, 4, TS], BF16)
        stgKR = sb("D_stgKR", [64, TS], BF16)
        q32 = sb("D_q32", [128, 512], F32)
        qtmp = sb("D_qtmp", [128, 512], F32)
        qrb = sb("D_qrb", [128, 512], BF16)
        ktmp = sb("D_ktmp", [128, 64], F32)
        krb = sb("D_krb", [128, 64], BF16)
        pT = ps("D_pT", [128, 1024], BF16)
        pm = [ps("D_pm%d" % i, [128, 512], F32) for i in range(3)]
        pT2 = ps("D_pT2", [128, 640], BF16)
        d_cqr = Dep(); d_ckvr = Dep(); d_krr = Dep(); d_sq = Dep(); d_st = Dep(); d_cn = [Dep(), Dep()]
        d_cT = Dep(); d_stgQ = Dep(); d_stgK = Dep(); d_stgV = Dep(); d_stgQR = Dep(); d_stgKR = Dep()
        d_q32 = Dep(); d_qtmp = Dep(); d_qrb = Dep(); d_ktmp = Dep(); d_krb = Dep()
        d_pT = Dep(); d_pm = [Dep() for _ in pm]; d_pT2 = Dep()
        pmi = 0
        for ts_ in range(S // TS):
            t0 = ts_ * TS
            kb.dma("sp", cqr[:], tm[t0:t0 + TS, TM_LCQ:TM_LCQ + 512].rearrange("(j p) c -> p j c", p=128), writes=[d_cqr])
            kb.dma("sp", ckvr[:], tm[t0:t0 + TS, TM_LCKV:TM_LCKV + 512].rearrange("(j p) c -> p j c", p=128), writes=[d_ckvr])
            kb.dma("sp", krr[:], tms[t0:t0 + TS, 48:112].rearrange("(j p) c -> p j c", p=128), writes=[d_krr])
            for j in range(NJ):
                c_ = j % 2
                emit_rmsnorm(kb, cqr[:, j, :], d_cqr, 512, gq[:], d_w, cn[c_][:, 0:512], d_cn[c_], sq[:], d_sq, st, d_st)
                emit_rmsnorm(kb, ckvr[:, j, :], d_ckvr, 512, gkv[:], d_w, cn[c_][:, 512:1024], d_cn[c_], sq[:], d_sq, st, d_st)
                for kc in range(8):
                    kb.op("pe", lambda e: e.transpose(out=pT[:, kc * 128:(kc + 1) * 128], in_=cn[c_][:, kc * 128:(kc + 1) * 128],
                                                      identity=C["ident"][:]),
                          reads=[d_cn[c_], C["dep"]], writes=[d_pT], inc=(kc == 7))
                kb.op("dve", lambda e: e.tensor_copy(out=cT[:, :, j * 128:(j + 1) * 128],
                                                     in_=pT[:].rearrange("p (k t) -> p k t", k=8)),
                      reads=[d_pT], writes=[d_cT])
            for h in range(8):
                for (w_t, off, kc0, stg_, d_stg, scale) in ((wuq, 0, 0, stgQ, d_stgQ, MLA_SCALE), (wukv, 0, 4, stgK, d_stgK, 1.0)):
                    pi = pmi % 3; pmi += 1
                    for kc in range(4):
                        kb.op("pe", lambda e: e.matmul(pm[pi][:, 0:TS], w_t[:, kc, off + h * 128:off + (h + 1) * 128],
                                                       cT[:, kc0 + kc, :], start=(kc == 0), stop=(kc == 3)),
                              reads=[d_w, d_cT], writes=[d_pm[pi]], inc=(kc == 3))
                    kb.op("act", lambda e: e.activation(out=stg_[:, h, :], in_=pm[pi][:, 0:TS], func=AF.Copy, scale=scale),
                          reads=[d_pm[pi]], writes=[d_stg])
            kb.dma("sp", mq[0:1024, t0:t0 + TS].rearrange("(h p) t -> p h t", p=128), stgQ[:], reads=[d_stgQ], writes=[d_mq],
                   slot=d_stgQ, merge_w=True)
            kb.dma("sp", mk[0:1024, t0:t0 + TS].rearrange("(h p) t -> p h t", p=128), stgK[:], reads=[d_stgK], writes=[d_mk],
                   slot=d_stgK, merge_w=True)
            for j in range(NJ):
                for half in range(2):
                    pi = pmi % 3; pmi += 1
                    for kc in range(4):
                        kb.op("pe", lambda e: e.matmul(pm[pi][:], cT[:, 4 + kc, j * 128:(j + 1) * 128],
                                                       wukv[:, kc, 1024 + half * 512:1024 + (half + 1) * 512],
                                                       start=(kc == 0), stop=(kc == 3)),
                              reads=[d_w, d_cT], writes=[d_pm[pi]], inc=(kc == 3))
                    kb.op("dve", lambda e: e.tensor_copy(out=stgV[:, j, half * 512:(half + 1) * 512], in_=pm[pi][:]),
                          reads=[d_pm[pi]], writes=[d_stgV])
                blk = t0 // 128 + j
                pi = pmi % 3; pmi += 1
                for kc in range(4):
                    kb.op("pe", lambda e: e.matmul(pm[pi][:], cT[:, kc, j * 128:(j + 1) * 128], wuq[:, kc, 1024:1536],
                                                   start=(kc == 0), stop=(kc == 3)),
                          reads=[d_w, d_cT], writes=[d_pm[pi]], inc=(kc == 3))
                kb.op("act", lambda e: e.activation(out=q32[:], in_=pm[pi][:], func=AF.Copy, scale=MLA_SCALE),
                      reads=[d_pm[pi]], writes=[d_q32])
                emit_rotary(kb, q32[:], d_q32, 8, cosT[:, blk, :], sinT[:, blk, :], d_rope, qrb[:], d_qrb, qtmp[:], d_qtmp)
                emit_rotary(kb, krr[:, j, :], d_krr, 1, cosT[:, blk, :], sinT[:, blk, :], d_rope, krb[:], d_krb, ktmp[:], d_ktmp)
                for c4 in range(4):
                    kb.op("pe", lambda e: e.transpose(out=pT2[:, c4 * 128:(c4 + 1) * 128], in_=qrb[:, c4 * 128:(c4 + 1) * 128],
                                                      identity=C["ident"][:]),
                          reads=[d_qrb, C["dep"]], writes=[d_pT2], inc=False)
                kb.op("pe", lambda e: e.transpose(out=pT2[0:64, 512:640], in_=krb[:], identity=C["ident"][:]),
                      reads=[d_krb, C["dep"]], writes=[d_pT2])
                kb.op("dve", lambda e: e.tensor_copy(out=stgQR[:, :, j * 128:(j + 1) * 128],
                                                     in_=pT2[:, 0:512].rearrange("p (c t) -> p c t", c=4)),
                      reads=[d_pT2], writes=[d_stgQR])
                kb.op("dve", lambda e: e.tensor_copy(out=stgKR[:, j * 128:(j + 1) * 128], in_=pT2[0:64, 512:640]),
                      reads=[d_pT2], writes=[d_stgKR])
            kb.dma("sp", mv[t0:t0 + TS, :].rearrange("(j p) c -> p j c", p=128), stgV[:], reads=[d_stgV], writes=[d_mv],
                   slot=d_stgV, merge_w=True)
            kb.dma("sp", mq[1024:1536, t0:t0 + TS].rearrange("(c p) t -> p c t", p=128), stgQR[:], reads=[d_stgQR], writes=[d_mq],
                   slot=d_stgQR, merge_w=True)
            kb.dma("sp", mk[1024:1088, t0:t0 + TS], stgKR[:], reads=[d_stgKR], writes=[d_mk], slot=d_stgKR, merge_w=True)
        kb.barrier()

    with ExitStack() as es:
        def sb(name, shape, dt):
            return es.enter_context(nc.sbuf_tensor(kb.u(name), shape, dt))
        qn = [sb("D_qn%d" % i, [128, S], BF16) for i in range(2)]
        qr = [sb("D_qr%d" % i, [128, S], BF16) for i in range(2)]
        kn = [sb("D_kn%d" % i, [128, S], BF16) for i in range(2)]
        kr2 = sb("D_kr2", [128, S], BF16)
        vr = [sb("D_vr%d" % i, [128, NBLK, 128], BF16) for i in range(2)]
        va = [sb("D_va%d" % i, [128, NBLK, 136], BF16) for i in range(2)]
        gt = [sb("D_gt%d" % i, [128, NBLK, 128], BF16) for i in range(2)]
        obT = [sb("D_obT%d" % i, [128, S], BF16) for i in range(2)]
        ob = [sb("D_ob%d" % i, [128, 128], BF16) for i in range(2)]
        rs = [sb("D_rs%d" % i, [128, 1], F32) for i in range(4)]
        pTr = es.enter_context(nc.psum_tensor(kb.u("D_pTr"), [128, 128], BF16))
        d_qn = [Dep(), Dep()]; d_qr = [Dep(), Dep()]; d_kn = [Dep(), Dep()]; d_kr2 = Dep()
        d_vr = [Dep(), Dep()]; d_va = [Dep(), Dep()]; d_gt = [Dep(), Dep()]; d_obT = [Dep(), Dep()]
        d_ob = [Dep(), Dep()]; d_rs = [Dep() for _ in rs]; d_pTr = Dep(); d_obr = Dep()
        for i in range(2):
            kb.op("pool", lambda e: e.memset(va[i][:], 1.0), writes=[d_va[i]])
        kb.dma("sp", kr2[0:64, :], mk[1024:1088, :], reads=[d_mk], writes=[d_kr2], slot=d_kr2, merge_w=True)
        kb.dma("sp", kr2[64:128, :], mk[1024:1088, :], reads=[d_mk], writes=[d_kr2], slot=d_kr2, merge_w=True)
        state = {"ri": 0}

        def load_head(h):
            b = h % 2
            hp, hh = h // 2, h % 2
            kb.dma("sp", qn[b][:], mq[h * 128:(h + 1) * 128, :], reads=[d_mq], writes=[d_qn[b]])
            kb.dma("sp", qr[b][:], mq[1024 + hp * 128:1024 + (hp + 1) * 128, :], reads=[d_mq], writes=[d_qr[b]])
            kb.dma("sp", kn[b][:], mk[h * 128:(h + 1) * 128, :], reads=[d_mk], writes=[d_kn[b]])
            kb.dma("sp", vr[b][:], mv[:, h * 128:(h + 1) * 128].rearrange("(b p) c -> p b c", p=128), reads=[d_mv], writes=[d_vr[b]])
            kb.dma("sp", gt[b][:], tm[:, TM_LG + h * 128:TM_LG + (h + 1) * 128].rearrange("(b p) c -> p b c", p=128),
                   writes=[d_gt[b]])
            kb.op("dve", lambda e: e.tensor_copy(out=va[b][:, :, 0:128], in_=vr[b][:]), reads=[d_vr[b]], writes=[d_va[b]])
            p0 = hh * 64
            kparts = [lambda blk: kn[b][:, blk * 128:(blk + 1) * 128], lambda blk: kr2[p0:p0 + 64, blk * 128:(blk + 1) * 128]]
            qparts = [lambda blk: qn[b][:, blk * 128:(blk + 1) * 128], lambda blk: qr[b][p0:p0 + 64, blk * 128:(blk + 1) * 128]]
            return kparts, qparts, (lambda blk: va[b][:, blk, 0:129]), [d_qn[b], d_qr[b], d_kn[b], d_kr2, d_va[b]]

        def finish_head(h, qb, pO, d_pO):
            b = h % 2
            ri = state["ri"] % 4; state["ri"] += 1
            o = ri % 2
            kb.op("dve", lambda e: e.reciprocal(out=rs[ri][:], in_=pO[:, 128:129]), reads=[d_pO], writes=[d_rs[ri]])
            kb.op("dve", lambda e: e.scalar_tensor_tensor(out=ob[o][:], in0=pO[:, 0:128], scalar=rs[ri][:, 0:1],
                                                          in1=gt[b][:, qb, :], op0=ALU.mult, op1=ALU.mult),
                  reads=[d_pO, d_rs[ri], d_gt[b]], writes=[d_ob[o]])
            kb.op("pe", lambda e: e.transpose(out=pTr[:], in_=ob[o][:], identity=C["ident"][:]),
                  reads=[d_ob[o], C["dep"]], writes=[d_pTr])
            kb.op("dve", lambda e: e.tensor_copy(out=obT[b][:, qb * 128:(qb + 1) * 128], in_=pTr[:]),
                  reads=[d_pTr], writes=[d_obT[b]])
            if qb == NBLK - 1:
                kb.dma("sp", obr[3, h * 128:(h + 1) * 128, :], obT[b][:], reads=[d_obT[b]], writes=[d_obr],
                       slot=d_obT[b], merge_w=True)

        attention_core(kb, es, S, C, [[h] for h in range(8)], load_head, finish_head, lambda h, qb: (None, []), 128, "D")
        kb.barrier()


def host_layout_mla(w_uq_l, w_ukv_l):
    q = w_uq_l.reshape(512, 8, 192)
    wuq = np.concatenate([q[:, :, 0:128].reshape(512, 1024), q[:, :, 128:192].reshape(512, 512)], axis=1)
    kv = w_ukv_l.reshape(512, 8, 256)
    wukv = np.concatenate([kv[:, :, 0:128].reshape(512, 1024), kv[:, :, 128:256].reshape(512, 1024)], axis=1)
    return np.ascontiguousarray(wuq), np.ascontiguousarray(wukv)


def phase_T(kb, S, obr, fm, x_ap, wbr_ap, wout_ap, pgam_ap, out_ap, C):
    nc = kb.nc
    TS = 512 if S >= 512 else S
    NJ = TS // 128
    d_out = Dep()
    with ExitStack() as es:
        def sb(name, shape, dt):
            return es.enter_context(nc.sbuf_tensor(kb.u(name), shape, dt))

        def ps(name, shape, dt):
            return es.enter_context(nc.psum_tensor(kb.u(name), shape, dt))
        gam = sb("T_gam", [128, D], F32)
        obT = [sb("T_obT%d" % i, [128, 8, TS], BF16) for i in range(4)]
        wb = [sb("T_wb%d" % i, [128, 8, 512], BF16) for i in range(2)]
        wo = [sb("T_wo%d" % i, [128, 16, 512], BF16) for i in range(2)]
        gt = [sb("T_gt%d" % i, [128, 4, TS], BF16) for i in range(2)]
        acc = sb("T_acc", [128, 4, TS], F32)
        tmp = [sb("T_tmp%d" % i, [128, TS], F32) for i in range(2)]
        mixT = sb("T_mixT", [128, 16, TS], BF16)
        ysb = sb("T_y", [128, D], F32)
        xt = sb("T_x", [128, D], F32)
        sq = sb("T_sq", [128, D], F32)
        ot = sb("T_o", [128, D], F32)
        st = sb("T_st", [128, 4], F32)
        pb = [ps("T_pb%d" % i, [128, 512], F32) for i in range(3)]
        py = ps("T_py", [128, D], F32)
        d_gam = Dep(); d_obT = [Dep() for _ in range(4)]; d_wb = [Dep(), Dep()]; d_wo = [Dep(), Dep()]
        d_gt = [Dep(), Dep()]; d_acc = Dep(); d_tmp = [Dep(), Dep()]; d_mixT = Dep(); d_y = Dep(); d_x = Dep()
        d_sq = Dep(); d_o = Dep(); d_st = Dep(); d_pb = [Dep() for _ in pb]; d_py = Dep()
        kb.dma("sp", gam[:], pgam_ap.partition_broadcast(128), writes=[d_gam])
        wbi = 0; woi = 0; gti = 0; pbi = 0; tmi = 0
        for ts_ in range(S // TS):
            t0 = ts_ * TS
            for br in range(4):
                kb.dma("sp", obT[br][:], obr[br, :, t0:t0 + TS].rearrange("(kc p) t -> p kc t", p=128), writes=[d_obT[br]])
            for ng in range(4):
                for br in range(4):
                    w_ = wbi % 2; wbi += 1
                    kb.dma("pool", wb[w_][:], wbr_ap[br, :, ng * 512:(ng + 1) * 512].rearrange("(kc p) n -> p kc n", p=128),
                           writes=[d_wb[w_]])
                    g_ = gti % 2; gti += 1
                    r0 = FM_MERGE + br * 2048 + ng * 512
                    kb.dma("sp", gt[g_][:], fm[r0:r0 + 512, t0:t0 + TS].rearrange("(c p) t -> p c t", p=128), writes=[d_gt[g_]])
                    for c4 in range(4):
                        p_ = pbi % 3; pbi += 1
                        for kc in range(8):
                            kb.op("pe", lambda e: e.matmul(pb[p_][:, 0:TS], wb[w_][:, kc, c4 * 128:(c4 + 1) * 128], obT[br][:, kc, :],
                                                           start=(kc == 0), stop=(kc == 7)),
                                  reads=[d_wb[w_], d_obT[br]], writes=[d_pb[p_]], inc=(kc == 7))
                        if br == 0:
                            kb.op("dve", lambda e: e.tensor_tensor(out=acc[:, c4, :], in0=pb[p_][:, 0:TS], in1=gt[g_][:, c4, :], op=ALU.mult),
                                  reads=[d_pb[p_], d_gt[g_]], writes=[d_acc])
                        else:
                            m_ = tmi % 2; tmi += 1
                            kb.op("dve", lambda e: e.tensor_tensor(out=tmp[m_][:], in0=pb[p_][:, 0:TS], in1=gt[g_][:, c4, :], op=ALU.mult),
                                  reads=[d_pb[p_], d_gt[g_]], writes=[d_tmp[m_]])
                            if br < 3:
                                kb.op("pool", lambda e: e.tensor_tensor(out=acc[:, c4, :], in0=acc[:, c4, :], in1=tmp[m_][:], op=ALU.add),
                                      reads=[d_tmp[m_], d_acc], writes=[d_acc])
                            else:
                                kb.op("pool", lambda e: e.tensor_tensor(out=mixT[:, ng * 4 + c4, :], in0=acc[:, c4, :], in1=tmp[m_][:], op=ALU.add),
                                      reads=[d_tmp[m_], d_acc], writes=[d_mixT])
            for j in range(NJ):
                for mb in range(4):
                    if j == 0:
                        w_ = (woi + mb) % 2
                        kb.dma("pool", wo[w_][:], wout_ap[:, mb * 512:(mb + 1) * 512].rearrange("(kc p) n -> p kc n", p=128),
                               writes=[d_wo[w_]])
                    else:
                        w_ = (woi + mb) % 2
                        kb.dma("pool", wo[w_][:], wout_ap[:, mb * 512:(mb + 1) * 512].rearrange("(kc p) n -> p kc n", p=128),
                               writes=[d_wo[w_]])
                    for kc in range(16):
                        kb.op("pe", lambda e: e.matmul(py[:, mb * 512:(mb + 1) * 512], mixT[:, kc, j * 128:(j + 1) * 128], wo[w_][:, kc, :],
                                                       start=(kc == 0), stop=(kc == 15)),
                              reads=[d_mixT, d_wo[w_]], writes=[d_py], inc=(kc == 15))
                woi += 4
                tok = t0 + j * 128
                kb.dma("sp", xt[:], x_ap[tok:tok + 128, :], writes=[d_x])
                kb.op("act", lambda e: e.activation(out=ysb[:], in_=py[:], func=AF.Copy), reads=[d_py], writes=[d_y])
                kb.op("act", lambda e: e.activation(out=sq[:], in_=ysb[:], func=AF.Square, scale=float(D ** -0.5)), reads=[d_y], writes=[d_sq])
                kb.op("dve", lambda e: e.reduce_sum(out=st[:, 0:1], in_=sq[:], axis=AX.X), reads=[d_sq], writes=[d_st])
                kb.op("act", lambda e: e.activation(out=st[:, 1:2], in_=st[:, 0:1], func=AF.Sqrt, bias=EPS), reads=[d_st], writes=[d_st])
                kb.op("dve", lambda e: e.reciprocal(out=st[:, 2:3], in_=st[:, 1:2]), reads=[d_st], writes=[d_st])
                kb.op("dve", lambda e: e.scalar_tensor_tensor(out=ot[:], in0=ysb[:], scalar=st[:, 2:3], in1=gam[:], op0=ALU.mult, op1=ALU.mult),
                      reads=[d_y, d_st, d_gam], writes=[d_o])
                kb.op("pool", lambda e: e.tensor_tensor(out=ot[:], in0=ot[:], in1=xt[:], op=ALU.add), reads=[d_o, d_x], writes=[d_o])
                kb.dma("sp", out_ap[tok:tok + 128, :], ot[:], reads=[d_o], writes=[d_out], slot=d_o, merge_w=True)
        kb.barrier()
    return d_out


def host_layout_ssd(conv_w_l, conv_b_l):
    cw = np.ascontiguousarray(conv_w_l.reshape(4, 12, 128).transpose(2, 1, 0))
    cb = np.ascontiguousarray(conv_b_l.reshape(12, 128).T)
    return cw, cb


def phase_A(kb, S, fm, tm, tms, cw_ap, cb_ap, dtb_ap, alog_ap, dsk_ap, sn_ap, sx, obr, C, only=None):
    nc = kb.nc
    NBLK = S // 128
    d_sx = Dep()
    with ExitStack() as es:
        def sb(name, shape, dt):
            return es.enter_context(nc.sbuf_tensor(kb.u(name), shape, dt))
        cw = sb("A_cw", [128, 12, 4], F32)
        cbias = sb("A_cb", [128, 12], F32)
        xin = [sb("A_xin%d" % i, [128, S + 4], BF16) for i in range(2)]
        acc = sb("A_acc", [128, S], F32)
        outb = [sb("A_outb%d" % i, [128, S], BF16) for i in range(2)]
        d_c = Dep(); d_xin = [Dep(), Dep()]; d_acc = Dep(); d_outb = [Dep(), Dep()]
        kb.dma("sp", cw[:], cw_ap, writes=[d_c], slot=d_c, merge_w=True)
        kb.dma("sp", cbias[:], cb_ap, writes=[d_c], slot=d_c, merge_w=True)
        for i in range(2):
            kb.op("pool", lambda e: e.memset(xin[i][:, 0:4], 0.0), writes=[d_xin[i]])
        for cb_ in range(12):
            b = cb_ % 2
            kb.dma("sp", xin[b][:, 4:4 + S], fm[cb_ * 128:(cb_ + 1) * 128, :], writes=[d_xin[b]], merge_w=True)
            kb.op("dve", lambda e: e.tensor_scalar(out=acc[:], in0=xin[b][:, 1:1 + S], scalar1=cw[:, cb_, 0:1], scalar2=None, op0=ALU.mult),
                  reads=[d_xin[b], d_c], writes=[d_acc])
            for j in range(1, 4):
                kb.op("dve", lambda e: e.scalar_tensor_tensor(out=acc[:], in0=xin[b][:, 1 + j:1 + j + S], scalar=cw[:, cb_, j:j + 1],
                                                              in1=acc[:], op0=ALU.mult, op1=ALU.add),
                      reads=[d_xin[b], d_c, d_acc], writes=[d_acc])
            kb.op("act", lambda e: e.activation(out=outb[b][:], in_=acc[:], func=AF.Silu, bias=cbias[:, cb_:cb_ + 1]),
                  reads=[d_acc, d_c], writes=[d_outb[b]])
            kb.dma("sp", sx[cb_ * 128:(cb_ + 1) * 128, :], outb[b][:], reads=[d_outb[b]], writes=[d_sx], slot=d_outb[b], merge_w=True)
        kb.barrier()

    if only == 'A1':
        return
    with ExitStack() as es:
        def sb(name, shape, dt):
            return es.enter_context(nc.sbuf_tensor(kb.u(name), shape, dt))

        def ps(name, shape, dt):
            return es.enter_context(nc.psum_tensor(kb.u(name), shape, dt))
        dtb = sb("A_dtb", [128, 16], F32)
        na = sb("A_na", [128, 16], F32)
        dsk = sb("A_dsk", [128, 16], F32)
        snw = sb("A_snw", [128, 1024], F32)
        state = sb("A_state", [128, 2, 512], F32)
        stateb = sb("A_stateb", [128, 2, 512], BF16)
        xcT = [sb("A_xcT%d" % i, [128, 12, 128], BF16) for i in range(2)]
        zs = [sb("A_zs%d" % i, [128, 1024], BF16) for i in range(2)]
        dtr = [sb("A_dtr%d" % i, [128, 16], F32) for i in range(2)]
        xtm = sb("A_xtm", [128, 1024], BF16)
        btm = sb("A_btm", [128, 256], BF16)
        sm = sb("A_sm", [128, 8, 16], F32)
        rhsU = sb("A_rhsU", [128, 16, 128], F32)
        xdt = sb("A_xdt", [128, 1024], BF16)
        xdtd = sb("A_xdtd", [128, 1024], BF16)
        cbm = sb("A_cbm", [128, 2, 128], F32)
        v4 = [sb("A_v4%d" % i, [128, 4, 128], F32) for i in range(2)]
        d4 = [sb("A_d4%d" % i, [128, 4, 128], F32) for i in range(2)]
        mT = sb("A_mT", [128, 16, 128], BF16)
        yb = sb("A_y", [128, 1024], F32)
        t2 = sb("A_t2", [128, 1024], F32)
        sq = sb("A_sq", [128, 1024], F32)
        st = sb("A_st", [128, 8], F32)
        yo = sb("A_yo", [128, 1024], BF16)
        oT = [sb("A_oT%d" % i, [128, 8, 128], BF16) for i in range(2)]
        tst = sb("A_tst", [128, 512], F32)
        pA = ps("A_pA", [128, 512], F32)
        pB = [ps("A_pB%d" % i, [128, 512], F32) for i in range(2)]
        pY = ps("A_pY", [128, 1024], F32)
        pO = [ps("A_pO%d" % i, [128, 512], F32) for i in range(2)]
        pT = ps("A_pT", [128, 1024], BF16)
        d_p = Dep(); d_state = Dep(); d_stateb = Dep(); d_xcT = [Dep(), Dep()]; d_zs = [Dep(), Dep()]; d_dtr = [Dep(), Dep()]
        d_xtm = Dep(); d_btm = Dep(); d_sm = Dep(); d_rhsU = Dep(); d_xdt = Dep(); d_xdtd = Dep(); d_cbm = Dep()
        d_v4 = [Dep(), Dep()]; d_d4 = [Dep(), Dep()]; d_mT = Dep(); d_y = Dep(); d_t2 = Dep(); d_sq = Dep(); d_st = Dep()
        d_yo = Dep(); d_oT = [Dep(), Dep()]; d_tst = Dep()
        d_pA = Dep(); d_pB = [Dep(), Dep()]; d_pY = Dep(); d_pO = [Dep(), Dep()]; d_pT = Dep(); d_obr = Dep()
        kb.dma("sp", dtb[:], dtb_ap.partition_broadcast(128), writes=[d_p], slot=d_p, merge_w=True)
        kb.dma("sp", na[:], alog_ap.partition_broadcast(128), writes=[d_p], slot=d_p, merge_w=True)
        kb.dma("sp", dsk[:], dsk_ap.partition_broadcast(128), writes=[d_p], slot=d_p, merge_w=True)
        kb.dma("sp", snw[:], sn_ap.partition_broadcast(128), writes=[d_p], slot=d_p, merge_w=True)
        kb.op("act", lambda e: e.activation(out=na[:], in_=na[:], func=AF.Exp), reads=[d_p], writes=[d_p])
        kb.op("dve", lambda e: e.memset(state[:], 0.0), writes=[d_state])
        kb.op("dve", lambda e: e.memset(stateb[:], 0.0), writes=[d_stateb])
        pbi = 0
        poi = 0
        for c in range(NBLK):
            b = c % 2
            t0 = c * 128
            kb.dma("sp", xcT[b][:], sx[:, t0:t0 + 128].rearrange("(cb p) t -> p cb t", p=128), reads=[d_sx], writes=[d_xcT[b]])
            kb.dma("sp", zs[b][:], tm[t0:t0 + 128, TM_Z:TM_Z + 1024], writes=[d_zs[b]])
            kb.dma("sp", dtr[b][:], tms[t0:t0 + 128, 0:16], writes=[d_dtr[b]])
            for k in range(8):
                kb.op("pe", lambda e: e.transpose(out=pT[:, k * 128:(k + 1) * 128], in_=xcT[b][:, k, :], identity=C["ident"][:]),
                      reads=[d_xcT[b], C["dep"]], writes=[d_pT], inc=(k == 7))
            kb.op("act", lambda e: e.activation(out=xtm[:], in_=pT[:], func=AF.Copy), reads=[d_pT], writes=[d_xtm])
            for k in range(2):
                kb.op("pe", lambda e: e.transpose(out=pT[:, k * 128:(k + 1) * 128], in_=xcT[b][:, 8 + k, :], identity=C["ident"][:]),
                      reads=[d_xcT[b], C["dep"]], writes=[d_pT], inc=(k == 1))
            kb.op("act", lambda e: e.activation(out=btm[:], in_=pT[:, 0:256], func=AF.Copy), reads=[d_pT], writes=[d_btm])
            kb.op("dve", lambda e: e.tensor_tensor(out=sm[:, 7, :], in0=dtr[b][:], in1=dtb[:], op=ALU.add), reads=[d_dtr[b], d_p], writes=[d_sm])
            kb.op("act", lambda e: e.activation(out=sm[:, 7, :], in_=sm[:, 7, :], func=AF.Exp), reads=[d_sm], writes=[d_sm])
            kb.op("act", lambda e: e.activation(out=sm[:, 0, :], in_=sm[:, 7, :], func=AF.Ln, bias=1.0), reads=[d_sm], writes=[d_sm])
            kb.op("dve", lambda e: e.tensor_tensor(out=sm[:, 1, :], in0=sm[:, 0, :], in1=na[:], op=ALU.mult), reads=[d_sm, d_p], writes=[d_sm])
            kb.op("pe", lambda e: e.matmul(pA[:, 256:272], C["U"][:], sm[:, 1, :], start=True, stop=True), reads=[d_sm, C["dep"]], writes=[d_pA], inc=False)
            kb.op("pe", lambda e: e.matmul(pA[:, 272:288], C["ones"][:], sm[:, 1, :], start=True, stop=True), reads=[d_sm, C["dep"]], writes=[d_pA], inc=False)
            for g in range(2):
                kb.op("pe", lambda e: e.matmul(pA[:, g * 128:(g + 1) * 128], xcT[b][:, 8 + g, :], xcT[b][:, 10 + g, :], start=True, stop=True),
                      reads=[d_xcT[b]], writes=[d_pA], inc=(g == 1))
            kb.op("dve", lambda e: e.tensor_copy(out=sm[:, 2:4, :], in_=pA[:, 256:288].rearrange("p (a h) -> p a h", a=2)), reads=[d_pA], writes=[d_sm])
            kb.op("dve", lambda e: e.tensor_tensor(out=cbm[:], in0=pA[:, 0:256].rearrange("p (g l) -> p g l", g=2),
                                                   in1=C["U"][:].unsqueeze(1).broadcast_to([128, 2, 128]), op=ALU.mult),
                  reads=[d_pA, C["dep"]], writes=[d_cbm])
            kb.op("dve", lambda e: e.tensor_tensor(out=sm[:, 4, :], in0=sm[:, 2, :], in1=sm[:, 3, :], op=ALU.subtract), reads=[d_sm], writes=[d_sm])
            kb.op("act", lambda e: e.activation(out=sm[:, 4, :], in_=sm[:, 4, :], func=AF.Exp), reads=[d_sm], writes=[d_sm])
            kb.op("act", lambda e: e.activation(out=sm[:, 5, :], in_=sm[:, 3, :], func=AF.Exp, scale=-1.0), reads=[d_sm], writes=[d_sm])
            kb.op("act", lambda e: e.activation(out=sm[:, 6, :], in_=sm[:, 2, :], func=AF.Exp, scale=-1.0), reads=[d_sm], writes=[d_sm])
            x3 = xtm[:].rearrange("p (h c) -> p h c", h=16)
            kb.op("dve", lambda e: e.tensor_tensor(out=xdt[:].rearrange("p (h c) -> p h c", h=16), in0=x3,
                                                   in1=sm[:, 0, :].unsqueeze(2).broadcast_to([128, 16, 64]), op=ALU.mult),
                  reads=[d_xtm, d_sm], writes=[d_xdt])
            kb.op("pool", lambda e: e.tensor_tensor(out=xdtd[:].rearrange("p (h c) -> p h c", h=16), in0=xdt[:].rearrange("p (h c) -> p h c", h=16),
                                                    in1=sm[:, 4, :].unsqueeze(2).broadcast_to([128, 16, 64]), op=ALU.mult),
                  reads=[d_xdt, d_sm], writes=[d_xdtd])
            kb.op("dve", lambda e: e.tensor_tensor(out=rhsU[:], in0=sm[:, 1, :].unsqueeze(2).broadcast_to([128, 16, 128]),
                                                   in1=C["U"][:].unsqueeze(1).broadcast_to([128, 16, 128]), op=ALU.mult),
                  reads=[d_sm, C["dep"]], writes=[d_rhsU])
            for q4 in range(4):
                p_ = pbi % 2; pbi += 1
                kb.op("pe", lambda e: e.matmul(pB[p_][:], C["ones"][:], rhsU[:, q4 * 4:(q4 + 1) * 4, :].rearrange("p h l -> p (h l)"),
                                               start=True, stop=True), reads=[d_rhsU, C["dep"]], writes=[d_pB[p_]])
                kb.op("dve", lambda e: e.tensor_tensor(out=v4[p_][:], in0=pB[p_][:].rearrange("p (h l) -> p h l", h=4),
                                                       in1=sm[:, 2, q4 * 4:(q4 + 1) * 4].unsqueeze(2).broadcast_to([128, 4, 128]), op=ALU.subtract),
                      reads=[d_pB[p_], d_sm], writes=[d_v4[p_]])
                kb.op("dve", lambda e: e.tensor_scalar_max(out=v4[p_][:], in0=v4[p_][:], scalar1=0.0), reads=[d_v4[p_]], writes=[d_v4[p_]])
                kb.op("act", lambda e: e.activation(out=d4[p_][:], in_=v4[p_][:], func=AF.Exp, scale=-1.0), reads=[d_v4[p_]], writes=[d_d4[p_]])
                g = q4 // 2
                kb.op("pool", lambda e: e.tensor_tensor(out=mT[:, q4 * 4:(q4 + 1) * 4, :], in0=d4[p_][:],
                                                        in1=cbm[:, g, :].unsqueeze(1).broadcast_to([128, 4, 128]), op=ALU.mult),
                      reads=[d_d4[p_], d_cbm], writes=[d_mT])
            for h in range(16):
                kb.op("pe", lambda e: e.matmul(pY[:, h * 64:(h + 1) * 64], mT[:, h, :], xdt[:, h * 64:(h + 1) * 64], start=True, stop=True),
                      reads=[d_mT, d_xdt], writes=[d_pY], inc=(h == 15))
            for g in range(2):
                o_ = poi % 2; poi += 1
                kb.op("pe", lambda e: e.matmul(pO[o_][:], xcT[b][:, 10 + g, :], stateb[:, g, :], start=True, stop=True),
                      reads=[d_xcT[b], d_stateb], writes=[d_pO[o_]])
                kb.op("dve", lambda e: e.tensor_tensor(out=t2[:, g * 512:(g + 1) * 512].rearrange("p (h c) -> p h c", h=8),
                                                       in0=pO[o_][:].rearrange("p (h c) -> p h c", h=8),
                                                       in1=sm[:, 6, g * 8:(g + 1) * 8].unsqueeze(2).broadcast_to([128, 8, 64]), op=ALU.mult),
                      reads=[d_pO[o_], d_sm], writes=[d_t2])
            kb.op("dve", lambda e: e.tensor_tensor(out=yb[:], in0=pY[:], in1=t2[:], op=ALU.add), reads=[d_pY, d_t2], writes=[d_y])
            for g in range(2):
                o_ = poi % 2; poi += 1
                kb.op("pe", lambda e: e.matmul(pO[o_][:], btm[:, g * 128:(g + 1) * 128], xdtd[:, g * 512:(g + 1) * 512], start=True, stop=True),
                      reads=[d_btm, d_xdtd], writes=[d_pO[o_]])
                kb.op("dve", lambda e: e.tensor_tensor(out=tst[:].rearrange("p (h c) -> p h c", h=8),
                                                       in0=state[:, g, :].rearrange("p (h c) -> p h c", h=8),
                                                       in1=sm[:, 5, g * 8:(g + 1) * 8].unsqueeze(2).broadcast_to([128, 8, 64]), op=ALU.mult),
                      reads=[d_state, d_sm], writes=[d_tst])
                kb.op("dve", lambda e: e.tensor_tensor(out=state[:, g, :], in0=tst[:], in1=pO[o_][:], op=ALU.add),
                      reads=[d_tst, d_pO[o_]], writes=[d_state])
            kb.op("pool", lambda e: e.tensor_copy(out=stateb[:], in_=state[:]), reads=[d_state], writes=[d_stateb])
            kb.op("pool", lambda e: e.tensor_tensor(out=t2[:].rearrange("p (h c) -> p h c", h=16), in0=x3,
                                                    in1=dsk[:].unsqueeze(2).broadcast_to([128, 16, 64]), op=ALU.mult),
                  reads=[d_xtm, d_p], writes=[d_t2])
            kb.op("dve", lambda e: e.tensor_tensor(out=yb[:], in0=yb[:], in1=t2[:], op=ALU.add), reads=[d_y, d_t2], writes=[d_y])
            kb.op("dve", lambda e: e.tensor_tensor(out=yb[:], in0=yb[:], in1=zs[b][:], op=ALU.mult), reads=[d_y, d_zs[b]], writes=[d_y])
            kb.op("act", lambda e: e.activation(out=sq[:], in_=yb[:], func=AF.Square, scale=float(512 ** -0.5)), reads=[d_y], writes=[d_sq])
            kb.op("dve", lambda e: e.reduce_sum(out=st[:, 0:2], in_=sq[:].rearrange("p (g c) -> p g c", g=2), axis=AX.X), reads=[d_sq], writes=[d_st])
            kb.op("act", lambda e: e.activation(out=st[:, 2:4], in_=st[:, 0:2], func=AF.Sqrt, bias=EPS), reads=[d_st], writes=[d_st])
            kb.op("dve", lambda e: e.reciprocal(out=st[:, 4:6], in_=st[:, 2:4]), reads=[d_st], writes=[d_st])
            for g in range(2):
                kb.op("dve", lambda e: e.scalar_tensor_tensor(out=yo[:, g * 512:(g + 1) * 512], in0=yb[:, g * 512:(g + 1) * 512],
                                                              scalar=st[:, 4 + g:5 + g], in1=snw[:, g * 512:(g + 1) * 512], op0=ALU.mult, op1=ALU.mult),
                      reads=[d_y, d_st, d_p], writes=[d_yo])
            for k in range(8):
                kb.op("pe", lambda e: e.transpose(out=pT[:, k * 128:(k + 1) * 128], in_=yo[:, k * 128:(k + 1) * 128], identity=C["ident"][:]),
                      reads=[d_yo, C["dep"]], writes=[d_pT], inc=(k == 7))
            kb.op("act", lambda e: e.activation(out=oT[b][:], in_=pT[:].rearrange("p (k t) -> p k t", k=8), func=AF.Copy), reads=[d_pT], writes=[d_oT[b]])
            kb.dma("sp", obr[0, :, t0:t0 + 128].rearrange("(k p) t -> p k t", p=128), oT[b][:], reads=[d_oT[b]], writes=[d_obr],
                   slot=d_oT[b], merge_w=True)
        kb.barrier()


GLA_SCALE = float(128 ** -0.5)


def phase_C(kb, S, tm, tms, w2_ap, gb_ap, gn_ap, obr, C, stop=None):
    nc = kb.nc
    NBLK = S // 128
    with ExitStack() as es:
        def sb(name, shape, dt):
            return es.enter_context(nc.sbuf_tensor(kb.u(name), shape, dt))

        def ps(name, shape, dt):
            return es.enter_context(nc.psum_tensor(kb.u(name), shape, dt))
        w2a = sb("C_w2a", [33, 512], BF16)
        w2f = sb("C_w2f", [33, 512], F32)
        lrb = sb("C_lrb", [128, 16], BF16)
        qT0 = sb("C_qT0", [128, 4, 128], BF16)
        qT1 = sb("C_qT1", [128, 4, 128], BF16)
        gnw = sb("C_gnw", [128, 256], F32)
        lrT = sb("C_lrT", [33, 128], BF16)
        qr_ = [sb("C_q%d" % i, [128, 512], BF16) for i in range(2)]
        kr_ = [sb("C_k%d" % i, [128, 512], BF16) for i in range(2)]
        vr_ = [sb("C_v%d" % i, [128, 1024], BF16) for i in range(2)]
        gg_ = [sb("C_g%d" % i, [128, 1024], BF16) for i in range(2)]
        lr_ = [sb("C_lr%d" % i, [128, 16], F32) for i in range(2)]
        Gm = sb("C_Gm", [128, 512], F32)
        Bc = sb("C_Bc", [128, 512], F32)
        E1 = sb("C_E1", [128, 512], F32)
        E2 = sb("C_E2", [128, 512], F32)
        E3 = sb("C_E3", [128, 512], F32)
        qt = sb("C_qt", [128, 512], BF16)
        kt = sb("C_kt", [128, 512], BF16)
        kh0 = sb("C_kh0", [128, 512], BF16)
        kh1 = sb("C_kh1", [128, 512], BF16)
        qkT = sb("C_qkT", [128, 8, 128], BF16)
        dec = sb("C_dec", [128, 8], F32)
        attn = sb("C_attn", [128, 4, 128], BF16)
        SA = sb("C_SA", [128, 4, 256], F32)
        SB = sb("C_SB", [128, 4, 256], F32)
        S0b = sb("C_S0b", [128, 4, 256], BF16)
        S1b = sb("C_S1b", [128, 4, 256], BF16)
        osb = sb("C_osb", [128, 1024], F32)
        sq = sb("C_sq", [128, 1024], F32)
        st = sb("C_st", [128, 12], F32)
        on = sb("C_on", [128, 1024], F32)
        yo = sb("C_yo", [128, 1024], BF16)
        oT = [sb("C_oT%d" % i, [128, 8, 128], BF16) for i in range(2)]
        pZ = ps("C_pZ", [128, 512], F32)
        pBc = ps("C_pBc", [128, 512], F32)
        pBl = ps("C_pBl", [128, 512], F32)
        pT = ps("C_pT", [128, 1024], BF16)
        pA = ps("C_pA", [128, 512], F32)
        pO = ps("C_pO", [128, 1024], F32)
        pS = ps("C_pS", [128, 2, 256], F32)
        d_w = Dep(); d_lrT = Dep(); d_q = [Dep(), Dep()]; d_k = [Dep(), Dep()]; d_v = [Dep(), Dep()]; d_g = [Dep(), Dep()]
        d_lr = [Dep(), Dep()]; d_Gm = Dep(); d_Bc = Dep(); d_E1 = Dep(); d_E2 = Dep(); d_E3 = Dep(); d_qt = Dep(); d_kt = Dep()
        d_kh = Dep(); d_qkT = Dep(); d_dec = Dep(); d_attn = [Dep() for _ in range(4)]; d_SA = [Dep() for _ in range(4)]
        d_SB = [Dep() for _ in range(4)]; d_S0b = [Dep() for _ in range(4)]; d_S1b = [Dep() for _ in range(4)]
        d_osb = Dep(); d_sq = Dep(); d_st = Dep(); d_on = Dep(); d_yo = Dep(); d_oT = [Dep(), Dep()]
        d_pZ = Dep(); d_pBc = Dep(); d_pBl = Dep(); d_pT = Dep(); d_pA = [Dep() for _ in range(4)]
        d_pO = [Dep() for _ in range(4)]; d_pS = [Dep(), Dep()]; d_obr = Dep()
        kb.op("dve", lambda e: e.memset(w2f[:], 0.0), writes=[d_w])
        kb.dma("sp", w2f[0:16, :], w2_ap, writes=[d_w], slot=d_w)
        kb.dma("sp", w2f[32:33, :], gb_ap, writes=[d_w], slot=d_w, merge_w=True)
        kb.op("dve", lambda e: e.tensor_copy(out=w2a[:], in_=w2f[:]), reads=[d_w], writes=[d_w])
        d_lrb = Dep(); d_qT0 = Dep(); d_qT1 = Dep()
        kb.op("pool", lambda e: e.memset(kh0[:], 0.0), writes=[d_kh])
        kb.op("pool", lambda e: e.memset(kh1[:], 0.0), writes=[d_kh])
        kb.op("pool", lambda e: e.memset(qT0[:], 0.0), writes=[d_qT0])
        kb.op("pool", lambda e: e.memset(qT1[:], 0.0), writes=[d_qT1])
        kb.dma("sp", gnw[:], gn_ap.partition_broadcast(128), writes=[d_w], slot=d_w, merge_w=True)
        kb.op("dve", lambda e: e.memset(lrT[:], 0.0), writes=[d_lrT])
        kb.op("dve", lambda e: e.memset(lrT[32:33, :], 1.0), reads=[d_lrT], writes=[d_lrT])
        kb.op("dve", lambda e: e.memset(SA[:], 0.0), writes=d_SA)
        kb.op("dve", lambda e: e.memset(S0b[:], 0.0), writes=d_S0b)
        psi = 0
        for blk in range(NBLK):
            b = blk % 2
            t0 = blk * 128
            kb.dma("sp", qr_[b][:], tm[t0:t0 + 128, TM_GQ:TM_GQ + 512], writes=[d_q[b]])
            kb.dma("sp", kr_[b][:], tm[t0:t0 + 128, TM_GK:TM_GK + 512], writes=[d_k[b]])
            kb.dma("sp", vr_[b][:], tm[t0:t0 + 128, TM_GV:TM_GV + 1024], writes=[d_v[b]])
            kb.dma("sp", gg_[b][:], tm[t0:t0 + 128, TM_GG:TM_GG + 1024], writes=[d_g[b]])
            kb.dma("sp", lr_[b][:], tms[t0:t0 + 128, 32:48], writes=[d_lr[b]])
            kb.op("dve", lambda e: e.tensor_copy(out=lrb[:], in_=lr_[b][:]), reads=[d_lr[b]], writes=[d_lrb])
            kb.op("pe", lambda e: e.transpose(out=pT[0:16, 0:128], in_=lrb[:], identity=C["ident"][:]),
                  reads=[d_lrb, C["dep"]], writes=[d_pT])
            kb.op("dve", lambda e: e.tensor_copy(out=lrT[0:16, :], in_=pT[0:16, 0:128]), reads=[d_pT, d_lrT], writes=[d_lrT])
            kb.op("pe", lambda e: e.matmul(pZ[:], lrT[:], w2a[:], start=True, stop=True), reads=[d_lrT, d_w], writes=[d_pZ])
            kb.op("act", lambda e: e.activation(out=Gm[:], in_=pZ[:], func=AF.Exp, scale=-1.0), reads=[d_pZ], writes=[d_Gm])
            kb.op("act", lambda e: e.activation(out=Gm[:], in_=Gm[:], func=AF.Ln, bias=1.0), reads=[d_Gm], writes=[d_Gm])
            if stop == 1:
                kb.barrier()
                return
            kb.op("pe", lambda e: e.matmul(pBc[:], C["U64s"][:], Gm[:], start=True, stop=True), reads=[d_Gm, C["dep"]], writes=[d_pBc])
            kb.op("pe", lambda e: e.matmul(pBl[:], C["BO64s"][:], Gm[:], start=True, stop=True), reads=[d_Gm, C["dep"]], writes=[d_pBl])
            for h in range(4):
                kb.op("pe", lambda e: e.matmul(pZ[:, h * 128:(h + 1) * 128], Gm[:, h * 128:(h + 1) * 128], C["BO64s"][:],
                                               start=True, stop=True), reads=[d_Gm, C["dep"]], writes=[d_pZ], inc=(h == 3))
            kb.op("act", lambda e: e.activation(out=dec[:], in_=pZ[:, 0:512:64], func=AF.Exp, scale=-1.0), reads=[d_pZ], writes=[d_dec])
            if stop == 2:
                kb.barrier()
                return
            kb.op("dve", lambda e: e.tensor_copy(out=Bc[:], in_=pBc[:]), reads=[d_pBc], writes=[d_Bc])
            kb.op("act", lambda e: e.activation(out=E1[:], in_=Bc[:], func=AF.Exp, scale=-1.0), reads=[d_Bc], writes=[d_E1])
            kb.op("act", lambda e: e.activation(out=E2[:], in_=Bc[:], func=AF.Exp), reads=[d_Bc], writes=[d_E2])
            kb.op("dve", lambda e: e.tensor_tensor(out=E3[:], in0=Bc[:], in1=pBl[:], op=ALU.subtract), reads=[d_Bc, d_pBl], writes=[d_E3])
            kb.op("act", lambda e: e.activation(out=E3[:], in_=E3[:], func=AF.Exp), reads=[d_E3], writes=[d_E3])
            kb.op("dve", lambda e: e.scalar_tensor_tensor(out=qt[:], in0=qr_[b][:], scalar=GLA_SCALE, in1=E1[:], op0=ALU.mult, op1=ALU.mult),
                  reads=[d_q[b], d_E1], writes=[d_qt])
            kb.op("pool", lambda e: e.tensor_tensor(out=kt[:], in0=kr_[b][:], in1=E2[:], op=ALU.mult), reads=[d_k[b], d_E2], writes=[d_kt])
            kb.op("dve", lambda e: e.tensor_tensor(out=kh0[0:64, :], in0=kr_[b][0:64, :], in1=E3[0:64, :], op=ALU.mult), reads=[d_k[b], d_E3], writes=[d_kh])
            kb.op("dve", lambda e: e.tensor_tensor(out=kh1[64:128, :], in0=kr_[b][64:128, :], in1=E3[64:128, :], op=ALU.mult), reads=[d_k[b], d_E3, d_kh], writes=[d_kh])
            if stop == 3:
                kb.barrier()
                return
            for h in range(4):
                kb.op("pe", lambda e: e.transpose(out=pT[:, h * 128:(h + 1) * 128], in_=qt[:, h * 128:(h + 1) * 128], identity=C["ident"][:]),
                      reads=[d_qt, C["dep"]], writes=[d_pT], inc=False)
            for h in range(4):
                kb.op("pe", lambda e: e.transpose(out=pT[:, (4 + h) * 128:(5 + h) * 128], in_=kt[:, h * 128:(h + 1) * 128], identity=C["ident"][:]),
                      reads=[d_kt, C["dep"]], writes=[d_pT], inc=(h == 3))
            if stop == 31:
                kb.barrier()
                return
            kb.op("act", lambda e: e.activation(out=qkT[:], in_=pT[:].rearrange("p (k t) -> p k t", k=8), func=AF.Copy), reads=[d_pT], writes=[d_qkT])
            if stop == 32:
                kb.barrier()
                return
            kb.op("act", lambda e: e.activation(out=qT0[:, :, 0:64], in_=pT[:, 0:512].rearrange("p (k t) -> p k t", k=4)[:, :, 0:64], func=AF.Copy),
                  reads=[d_pT], writes=[d_qT0])
            kb.op("act", lambda e: e.activation(out=qT1[:, :, 64:128], in_=pT[:, 0:512].rearrange("p (k t) -> p k t", k=4)[:, :, 64:128], func=AF.Copy),
                  reads=[d_pT], writes=[d_qT1])
            if stop == 4:
                kb.barrier()
                return
            for h in range(4):
                kb.op("pe", lambda e: e.matmul(pA[:, h * 128:(h + 1) * 128], qkT[:, 4 + h, :], qkT[:, h, :], start=True, stop=True),
                      reads=[d_qkT], writes=[d_pA[h]])
                kb.op("dve", lambda e: e.tensor_tensor(out=attn[:, h, :], in0=pA[:, h * 128:(h + 1) * 128], in1=C["U64"][:], op=ALU.mult),
                      reads=[d_pA[h], C["dep"]], writes=[d_attn[h]])
                if stop == 41:
                    kb.barrier()
                    return
                s0 = psi % 2; psi += 1
                kb.op("pe", lambda e: e.matmul(pS[:, s0, :], kh0[:, h * 128:(h + 1) * 128], vr_[b][:, h * 256:(h + 1) * 256], start=True, stop=True),
                      reads=[d_kh, d_v[b]], writes=[d_pS[s0]])
                kb.op("dve", lambda e: e.scalar_tensor_tensor(out=SB[:, h, :], in0=SA[:, h, :], scalar=dec[:, 2 * h:2 * h + 1], in1=pS[:, s0, :],
                                                              op0=ALU.mult, op1=ALU.add),
                      reads=[d_SA[h], d_dec, d_pS[s0]], writes=[d_SB[h]])
                kb.op("pool", lambda e: e.tensor_copy(out=S1b[:, h, :], in_=SB[:, h, :]), reads=[d_SB[h]], writes=[d_S1b[h]])
                if stop == 42:
                    kb.barrier()
                    return
                s1 = psi % 2; psi += 1
                kb.op("pe", lambda e: e.matmul(pS[:, s1, :], kh1[:, h * 128:(h + 1) * 128], vr_[b][:, h * 256:(h + 1) * 256], start=True, stop=True),
                      reads=[d_kh, d_v[b]], writes=[d_pS[s1]])
                if stop == 43:
                    kb.barrier()
                    return
                kb.op("pe", lambda e: e.matmul(pO[:, h * 256:(h + 1) * 256], attn[:, h, :], vr_[b][:, h * 256:(h + 1) * 256], start=True, stop=False),
                      reads=[d_attn[h], d_v[b]], writes=[d_pO[h]], inc=False)
                kb.op("pe", lambda e: e.matmul(pO[:, h * 256:(h + 1) * 256], qT0[:, h, :], S0b[:, h, :], start=False, stop=False),
                      reads=[d_qT0, d_S0b[h]], writes=[d_pO[h]], inc=False)
                kb.op("pe", lambda e: e.matmul(pO[:, h * 256:(h + 1) * 256], qT1[:, h, :], S1b[:, h, :], start=False, stop=True),
                      reads=[d_qT1, d_S1b[h]], writes=[d_pO[h]])
                if stop == 44:
                    kb.barrier()
                    return
                kb.op("dve", lambda e: e.scalar_tensor_tensor(out=SA[:, h, :], in0=SB[:, h, :], scalar=dec[:, 2 * h + 1:2 * h + 2], in1=pS[:, s1, :],
                                                              op0=ALU.mult, op1=ALU.add),
                      reads=[d_SB[h], d_dec, d_pS[s1]], writes=[d_SA[h]])
                kb.op("pool", lambda e: e.tensor_copy(out=S0b[:, h, :], in_=SA[:, h, :]), reads=[d_SA[h]], writes=[d_S0b[h]])
            kb.op("act", lambda e: e.activation(out=osb[:], in_=pO[:], func=AF.Copy), reads=d_pO, writes=[d_osb])
            if stop == 5:
                kb.barrier()
                return
            kb.op("act", lambda e: e.activation(out=sq[:], in_=osb[:], func=AF.Square, scale=float(256 ** -0.5)), reads=[d_osb], writes=[d_sq])
            kb.op("dve", lambda e: e.reduce_sum(out=st[:, 0:4], in_=sq[:].rearrange("p (h c) -> p h c", h=4), axis=AX.X), reads=[d_sq], writes=[d_st])
            kb.op("act", lambda e: e.activation(out=st[:, 4:8], in_=st[:, 0:4], func=AF.Sqrt, bias=EPS), reads=[d_st], writes=[d_st])
            kb.op("dve", lambda e: e.reciprocal(out=st[:, 8:12], in_=st[:, 4:8]), reads=[d_st], writes=[d_st])
            for h in range(4):
                kb.op("dve", lambda e: e.scalar_tensor_tensor(out=on[:, h * 256:(h + 1) * 256], in0=osb[:, h * 256:(h + 1) * 256],
                                                              scalar=st[:, 8 + h:9 + h], in1=gnw[:], op0=ALU.mult, op1=ALU.mult),
                      reads=[d_osb, d_st, d_w], writes=[d_on])
            kb.op("pool", lambda e: e.tensor_tensor(out=yo[:], in0=on[:], in1=gg_[b][:], op=ALU.mult), reads=[d_on, d_g[b]], writes=[d_yo])
            for k in range(8):
                kb.op("pe", lambda e: e.transpose(out=pT[:, k * 128:(k + 1) * 128], in_=yo[:, k * 128:(k + 1) * 128], identity=C["ident"][:]),
                      reads=[d_yo, C["dep"]], writes=[d_pT], inc=(k == 7))
            kb.op("act", lambda e: e.activation(out=oT[b][:], in_=pT[:].rearrange("p (k t) -> p k t", k=8), func=AF.Copy), reads=[d_pT], writes=[d_oT[b]])
            kb.dma("sp", obr[2, :, t0:t0 + 128].rearrange("(k p) t -> p k t", p=128), oT[b][:], reads=[d_oT[b]], writes=[d_obr],
                   slot=d_oT[b], merge_w=True)
        kb.barrier()


PER_LAYER = [
    ("pre", [1, D]), ("post", [1, D]), ("wtm", [D, NTM]), ("wsm", [D, 128]), ("wfm", [D, NFM]),
    ("cw", [128, 12, 4]), ("cb", [128, 12]), ("dtb", [1, 16]), ("alog", [1, 16]), ("dsk", [1, 16]), ("sn", [1, 1024]),
    ("fgb", [1, 16]), ("w2", [16, 512]), ("gb", [1, 512]), ("gn", [1, 256]), ("qn", [1, 512]), ("kvn", [1, 512]),
    ("wuq", [512, 1536]), ("wukv", [512, 2048]), ("wbr", [4, 1024, D]), ("wout", [D, D]),
]


def build_program(S, L, debug=False):
    nc = bass.Bass("TRN2", target_bir_lowering=False)
    I32 = mybir.dt.int32
    x = nc.dram_tensor("x", [S, D], F32, kind="ExternalInput").ap()
    pos = nc.dram_tensor("pos", [128, S // 128], I32, kind="ExternalInput").ap()
    cst = nc.dram_tensor("cst", [6, 128, 128], F32, kind="ExternalInput").ap()
    invf = nc.dram_tensor("invf", [1, 32], F32, kind="ExternalInput").ap()
    W = {}
    for name, shape in PER_LAYER:
        W[name] = nc.dram_tensor(name, [L] + shape, F32, kind="ExternalInput").ap()
    out = nc.dram_tensor("out", [S, D], F32, kind="ExternalOutput").ap()
    knd = "ExternalOutput" if debug else "Internal"
    tm = nc.dram_tensor("s_tm", [S, NTM], BF16, kind=knd).ap()
    tms = nc.dram_tensor("s_tms", [S, 128], F32, kind=knd).ap()
    fm = nc.dram_tensor("s_fm", [NFM, S], BF16, kind=knd).ap()
    obr = nc.dram_tensor("s_obr", [4, 1024, S], BF16, kind=knd).ap()
    sx = nc.dram_tensor("s_sx", [1536, S], BF16, kind="Internal").ap()
    mq = nc.dram_tensor("s_mq", [1536, S], BF16, kind="Internal").ap()
    mk = nc.dram_tensor("s_mk", [1088, S], BF16, kind="Internal").ap()
    mv = nc.dram_tensor("s_mv", [S, 1024], BF16, kind="Internal").ap()
    xmid = [nc.dram_tensor("s_xmid%d" % i, [S, D], F32, kind="Internal").ap() for i in range(max(L - 1, 1))]
    kb = KB(nc)
    with ExitStack() as es:
        C = load_consts(kb, es, cst)
        rope = setup_rope(kb, es, S, pos, invf)
        cur = x
        for l in range(L):
            dst = out if l == L - 1 else xmid[l]
            phase_P(kb, S, cur, W["pre"][l], W["wtm"][l], W["wsm"][l], W["wfm"][l], tm, tms, fm, (C["ident"], C["dep"]))
            phase_A(kb, S, fm, tm, tms, W["cw"][l], W["cb"][l], W["dtb"][l], W["alog"][l], W["dsk"][l], W["sn"][l], sx, obr, C)
            phase_B(kb, S, fm, tm, tms, W["fgb"][l], obr, C)
            phase_C(kb, S, tm, tms, W["w2"][l], W["gb"][l], W["gn"][l], obr, C)
            phase_D(kb, S, tm, tms, W["qn"][l], W["kvn"][l], W["wuq"][l], W["wukv"][l], mq, mk, mv, obr, C, rope)
            phase_T(kb, S, obr, fm, cur, W["wbr"][l], W["wout"][l], W["post"][l], dst, C)
            cur = dst
        kb.barrier()
    return nc, kb


def host_prepare(inputs, L):
    g = lambda k: np.asarray(inputs[k], dtype=np.float32)
    Wd = {n: [] for n, _ in PER_LAYER}
    for l in range(L):
        wtm, wsm, wfm = host_layout_win(g("w_in")[l])
        cw, cb = host_layout_ssd(g("conv_w")[l], g("conv_b")[l])
        wuq, wukv = host_layout_mla(g("w_uq")[l], g("w_ukv")[l])
        vals = dict(pre=g("pre_norm")[l].reshape(1, D), post=g("post_norm")[l].reshape(1, D), wtm=wtm, wsm=wsm, wfm=wfm, cw=cw, cb=cb,
                    dtb=g("dt_bias")[l].reshape(1, 16), alog=g("a_log")[l].reshape(1, 16), dsk=g("d_skip")[l].reshape(1, 16),
                    sn=g("ssm_norm")[l].reshape(1, 1024), fgb=g("fgate_b")[l].reshape(1, 16), w2=g("gla_w2")[l],
                    gb=g("gla_b")[l].reshape(1, 512), gn=g("gla_norm")[l].reshape(1, 256), qn=g("q_norm")[l].reshape(1, 512),
                    kvn=g("kv_norm")[l].reshape(1, 512), wuq=wuq, wukv=wukv, wbr=g("w_branch")[l], wout=g("w_out")[l])
        for n, _ in PER_LAYER:
            Wd[n].append(np.ascontiguousarray(vals[n], dtype=np.float32))
    return {n: np.ascontiguousarray(np.stack(v)) for n, v in Wd.items()}


_CACHE = {}


def kernel(**inputs):
    x = np.asarray(inputs["x"], dtype=np.float32)
    positions = np.asarray(inputs["positions"], dtype=np.int32)
    B, S, _ = x.shape
    L = np.asarray(inputs["w_in"]).shape[0]
    key = (S, L)
    if key not in _CACHE:
        _CACHE[key] = build_program(S, L)[0]
    nc = _CACHE[key]
    Wd = host_prepare(inputs, L)
    cst = host_consts()
    invf = host_invf()
    n_cores = 8
    in_maps = []
    for c in range(n_cores):
        b = c % B
        m = {"x": np.ascontiguousarray(x[b]), "pos": np.ascontiguousarray(positions[b].reshape(S // 128, 128).T),
             "cst": cst, "invf": invf}
        m.update(Wd)
        in_maps.append(m)
    res = run_bass_kernel_spmd(nc, in_maps, core_ids=list(range(n_cores)))
    outs = [np.asarray(res.results[b]["out"], dtype=np.float32) for b in range(B)]
    return np.stack(outs, axis=0)
```

```python
import numpy as np
from contextlib import ExitStack
import concourse.bass as bass
import concourse.mybir as mybir
from concourse.bass_utils import run_bass_kernel_spmd

F32 = mybir.dt.float32
BF16 = mybir.dt.bfloat16
AF = mybir.ActivationFunctionType
ALU = mybir.AluOpType
AX = mybir.AxisListType

D = 2048
N_IN = 20080
EPS = 1e-6


class Dep:
    __slots__ = ("w", "r", "sem", "cnt", "name", "dead", "key")

    def __init__(self, name=""):
        self.w = {}
        self.r = {}
        self.sem = None
        self.cnt = 0
        self.name = name
        self.dead = False
        self.key = None


class StopEmit(Exception):
    pass


class KB:
    def __init__(self, nc):
        self.nc = nc
        self.E = dict(pe=nc.tensor, dve=nc.vector, act=nc.scalar, pool=nc.gpsimd, sp=nc.sync)
        self.sem = {}
        self.cnt = {}
        for e in ("pe", "dve", "act", "pool"):
            self.sem[e] = nc.alloc_semaphore("sem_" + e)
            self.cnt[e] = 0
        self.seen = {e: {} for e in self.E}
        self.slots = []
        self.ninstr = 0
        self.nops = 0
        self.limit = None
        self.uid = 0
        self.nslot = 0
        self.free_sems = []
        import os
        self.embed = os.environ.get('KB_EMBED', '1') == '1'

    def u(self, name):
        self.uid += 1
        return "%s_%d" % (name, self.uid)

    def _need(self, e, reads, writes):
        need = {}

        def add(evs, skip_same):
            for k, (sem, val, slot) in evs.items():
                if k == e and (skip_same or e == "pe"):
                    continue
                if slot is not None and not slot.dead:
                    val = slot.cnt
                if need.get(k, (None, 0))[1] < val:
                    need[k] = (sem, val)

        for d in reads:
            add(d.w, False)
        for d in writes:
            add(d.w, False)
            add(d.r, True)
        return need

    def _waits(self, e, need, defer=False):
        todo = [(k, sem, val) for k, (sem, val) in need.items() if self.seen[e].get(k, 0) < val]
        last = None
        if defer and todo and self.embed:
            last = todo.pop()
        for k, sem, val in todo:
            self.E[e].wait_ge(sem, val)
            self.seen[e][k] = val
            self.ninstr += 1
        if last is not None:
            self.seen[e][last[0]] = last[2]
        return last

    def op(self, e, fn, reads=(), writes=(), inc=True):
        if self.limit is not None and self.nops >= self.limit and inc and e != "pe":
            raise StopEmit()
        self.nops += 1
        last = self._waits(e, self._need(e, reads, writes), defer=True)
        ins = fn(self.E[e])
        if last is not None:
            ins._wait_ge(last[1], last[2])
        self.ninstr += 1
        if inc:
            self.cnt[e] += 1
            ins.then_inc(self.sem[e], 1)
            val = self.cnt[e]
        else:
            val = self.cnt[e] + 1
        ev = (self.sem[e], val, None)
        for d in reads:
            d.r[e] = ev
        for d in writes:
            d.w = {e: ev}
            d.r = {}
        return ins

    def dma(self, q, out, in_, reads=(), writes=(), slot=None, merge_w=False):
        if slot is None:
            slot = writes[0] if writes else reads[0]
        need = self._need(q, reads, () if merge_w else writes)
        if merge_w:
            for d in writes:
                for k, (sem, val, sl) in d.r.items():
                    if sl is not None and not sl.dead:
                        val = sl.cnt
                    if need.get(k, (None, 0))[1] < val:
                        need[k] = (sem, val)
        last = self._waits(q, need, defer=True)
        assert not slot.dead
        if slot.sem is None:
            self.nslot += 1
            slot.key = "dma%d" % self.nslot
            if self.free_sems:
                slot.sem, slot.cnt = self.free_sems.pop()
            else:
                slot.sem = self.nc.alloc_semaphore("dsem%d" % self.nslot)
            self.slots.append(slot)
        ins = self.E[q].dma_start(out=out, in_=in_)
        if last is not None:
            ins._wait_ge(last[1], last[2])
        self.ninstr += 1
        slot.cnt += 16
        ins.then_inc(slot.sem, 16)
        key = slot.key
        ev = (slot.sem, slot.cnt, slot)
        for d in reads:
            d.r[key] = ev
        for d in writes:
            if merge_w:
                d.w[key] = ev
            else:
                d.w = {key: ev}
                d.r = {}
        return ins

    def barrier(self, engines=("pe", "dve", "act", "pool", "sp")):
        need = {}
        for e in ("pe", "dve", "act", "pool"):
            if self.cnt[e]:
                need[e] = (self.sem[e], self.cnt[e])
        for s in self.slots:
            need[s.key] = (s.sem, s.cnt)
        for e in engines:
            n2 = {k: v for k, v in need.items() if k != e}
            self._waits(e, n2)
        if len(engines) == 5:
            for s in self.slots:
                s.dead = True
                self.free_sems.append((s.sem, s.cnt))
            self.slots = []


def _f32(a):
    return np.ascontiguousarray(np.asarray(a, dtype=np.float32))


C_MZ, C_XBC, C_DT = 0, 1024, 2560
C_FQ, C_FK, C_FV, C_FF, C_FG = 2576, 3600, 4624, 5648, 5664
C_GQ, C_GK, C_GV, C_GLR, C_GG = 6688, 7200, 7712, 8736, 8752
C_LCQ, C_LCKV, C_LKR, C_LG = 9776, 10288, 10800, 10864
C_MERGE = 11888

TM_BLOCKS = [
    (C_MZ, True), (C_MZ + 512, True),
    (C_FV, False), (C_FV + 512, False),
    (C_FG, True), (C_FG + 512, True),
    (C_GQ, False), (C_GK, False),
    (C_GV, False), (C_GV + 512, False),
    (C_GG, True), (C_GG + 512, True),
    (C_LCQ, False), (C_LCKV, False),
    (C_LG, True), (C_LG + 512, True),
]
TM_Z, TM_FV, TM_FG, TM_GQ, TM_GK, TM_GV, TM_GG, TM_LCQ, TM_LCKV, TM_LG = (
    0, 1024, 2048, 3072, 3584, 4096, 5120, 6144, 6656, 7168)
NTM = len(TM_BLOCKS) * 512
FM_XBC, FM_FQ, FM_FK, FM_MERGE = 0, 1536, 2560, 3584
NFM = 11776


def host_layout_win(w_in_l):
    w = w_in_l
    tm = np.concatenate([w[:, c:c + 512] for c, _ in TM_BLOCKS], axis=1)
    small = np.zeros((D, 128), np.float32)
    small[:, 0:16] = w[:, C_DT:C_DT + 16]
    small[:, 16:32] = w[:, C_FF:C_FF + 16]
    small[:, 32:48] = w[:, C_GLR:C_GLR + 16]
    small[:, 48:112] = w[:, C_LKR:C_LKR + 64]
    fm = np.concatenate([w[:, C_XBC:C_XBC + 1536], w[:, C_FQ:C_FQ + 1024],
                         w[:, C_FK:C_FK + 1024], w[:, C_MERGE:C_MERGE + 8192]], axis=1)
    return np.ascontiguousarray(tm), small, np.ascontiguousarray(fm)


def phase_P(kb, S, x_ap, gamma_ap, wtm_ap, wsm_ap, wfm_ap, tm, tms, fm, ident):
    nc = kb.nc
    TT = 1024 if S >= 1024 else S
    NT = S // TT
    NB = TT // 128
    NH = TT // 512
    with ExitStack() as es:
        def sb(name, shape, dt):
            return es.enter_context(nc.sbuf_tensor(kb.u(name), shape, dt))

        def ps(name, shape, dt):
            return es.enter_context(nc.psum_tensor(kb.u(name), shape, dt))

        gam = sb("P_gam", [128, D], F32)
        xt = [sb("P_xt%d" % i, [128, D], F32) for i in range(2)]
        sq = sb("P_sq", [128, D], F32)
        st = [sb("P_st%d" % i, [128, 4], F32) for i in range(2)]
        hb = [sb("P_h%d" % i, [128, D], BF16) for i in range(2)]
        hT = sb("P_hT", [128, 16, TT], BF16)
        wt = [sb("P_wt%d" % i, [128, 16, 512], BF16) for i in range(2)]
        wsmall = sb("P_wsm", [128, 16, 128], BF16)
        stg = [sb("P_stg%d" % i, [128, NB, 512], BF16) for i in range(2)]
        stgs = sb("P_stgs", [128, NB, 128], F32)
        stgF = [sb("P_stgF%d" % i, [128, 4, TT], BF16) for i in range(2)]
        pT = [ps("P_pT%d" % i, [128, D], BF16) for i in range(1)]
        pm = [ps("P_pm%d" % i, [128, 512], F32) for i in range(4)]

        d_gam = Dep(); d_xt = [Dep(), Dep()]; d_sq = Dep(); d_st = [Dep(), Dep()]
        d_hb = [Dep(), Dep()]; d_hT = Dep(); d_wt = [Dep(), Dep()]; d_ws = Dep()
        d_stg = [Dep(), Dep()]; d_stgs = Dep(); d_stgF = [Dep(), Dep()]
        d_pT = [Dep()]; d_pm = [Dep() for _ in range(4)]
        d_tm = Dep(); d_tms = Dep(); d_fm = Dep()

        kb.dma("sp", gam[:], gamma_ap.partition_broadcast(128), writes=[d_gam])
        pmi = 0
        wti = 0
        sgi = 0
        sfi = 0
        wtm_v = wtm_ap.rearrange("(kc p) n -> p kc n", p=128)
        wsm_v = wsm_ap.rearrange("(kc p) n -> p kc n", p=128)
        wfm_v = wfm_ap.rearrange("(kc p) n -> p kc n", p=128)
        for tt in range(NT):
            t0 = tt * TT
            for tb in range(NB):
                b = tb % 2
                kb.dma("sp", xt[b][:], x_ap[t0 + tb * 128:t0 + (tb + 1) * 128, :], writes=[d_xt[b]])
                kb.op("act", lambda e: e.activation(out=sq[:], in_=xt[b][:], func=AF.Square, scale=float(D ** -0.5)),
                      reads=[d_xt[b]], writes=[d_sq])
                kb.op("dve", lambda e: e.reduce_sum(out=st[b][:, 0:1], in_=sq[:], axis=AX.X),
                      reads=[d_sq], writes=[d_st[b]])
                kb.op("act", lambda e: e.activation(out=st[b][:, 1:2], in_=st[b][:, 0:1], func=AF.Sqrt, bias=EPS),
                      reads=[d_st[b]], writes=[d_st[b]])
                kb.op("dve", lambda e: e.reciprocal(out=st[b][:, 2:3], in_=st[b][:, 1:2]),
                      reads=[d_st[b]], writes=[d_st[b]])
                kb.op("dve", lambda e: e.scalar_tensor_tensor(out=hb[b][:], in0=xt[b][:], scalar=st[b][:, 2:3],
                                                              in1=gam[:], op0=ALU.mult, op1=ALU.mult),
                      reads=[d_xt[b], d_st[b], d_gam], writes=[d_hb[b]])
                for kc in range(16):
                    kb.op("pe", lambda e: e.transpose(out=pT[0][:, kc * 128:(kc + 1) * 128],
                                                      in_=hb[b][:, kc * 128:(kc + 1) * 128], identity=ident[0][:]),
                          reads=[d_hb[b], ident[1]], writes=[d_pT[0]], inc=(kc == 15))
                kb.op("act", lambda e: e.activation(out=hT[:, :, tb * 128:(tb + 1) * 128],
                                                    in_=pT[0][:].rearrange("p (k t) -> p k t", k=16), func=AF.Copy),
                      reads=[d_pT[0]], writes=[d_hT])

            for blk in range(len(TM_BLOCKS) + 1):
                small = blk == len(TM_BLOCKS)
                if small:
                    kb.dma("pool", wsmall[:], wsm_v, writes=[d_ws])
                    w_t, d_w, ncol = wsmall, d_ws, 128
                else:
                    wb = wti % 2; wti += 1
                    kb.dma("pool", wt[wb][:], wtm_v[:, :, blk * 512:(blk + 1) * 512], writes=[d_wt[wb]])
                    w_t, d_w, ncol = wt[wb], d_wt[wb], 512
                    sg = sgi % 2; sgi += 1
                for tb in range(NB):
                    pi = pmi % 4; pmi += 1
                    for kc in range(16):
                        kb.op("pe", lambda e: e.matmul(pm[pi][:, 0:ncol], hT[:, kc, tb * 128:(tb + 1) * 128],
                                                       w_t[:, kc, :], start=(kc == 0), stop=(kc == 15)),
                              reads=[d_hT, d_w], writes=[d_pm[pi]], inc=(kc == 15))
                    if small:
                        kb.op("dve", lambda e: e.tensor_copy(out=stgs[:, tb, :], in_=pm[pi][:, 0:128]),
                              reads=[d_pm[pi]], writes=[d_stgs])
                    elif TM_BLOCKS[blk][1]:
                        kb.op("act", lambda e: e.activation(out=stg[sg][:, tb, :], in_=pm[pi][:], func=AF.Silu),
                              reads=[d_pm[pi]], writes=[d_stg[sg]])
                    else:
                        kb.op("dve", lambda e: e.tensor_copy(out=stg[sg][:, tb, :], in_=pm[pi][:]),
                              reads=[d_pm[pi]], writes=[d_stg[sg]])
                if small:
                    kb.dma("sp", tms[t0:t0 + TT, :].rearrange("(tb p) n -> p tb n", p=128), stgs[:],
                           reads=[d_stgs], writes=[d_tms], slot=d_stgs, merge_w=True)
                else:
                    kb.dma("sp", tm[t0:t0 + TT, blk * 512:(blk + 1) * 512].rearrange("(tb p) n -> p tb n", p=128),
                           stg[sg][:], reads=[d_stg[sg]], writes=[d_tm], slot=d_stg[sg], merge_w=True)

            for blk in range(NFM // 512):
                wb = wti % 2; wti += 1
                kb.dma("pool", wt[wb][:], wfm_v[:, :, blk * 512:(blk + 1) * 512], writes=[d_wt[wb]])
                sf = sfi % 2; sfi += 1
                r0 = blk * 512
                for cb in range(4):
                    for th in range(NH):
                        pi = pmi % 4; pmi += 1
                        for kc in range(16):
                            kb.op("pe", lambda e: e.matmul(pm[pi][:], wt[wb][:, kc, cb * 128:(cb + 1) * 128],
                                                           hT[:, kc, th * 512:(th + 1) * 512],
                                                           start=(kc == 0), stop=(kc == 15)),
                                  reads=[d_hT, d_wt[wb]], writes=[d_pm[pi]], inc=(kc == 15))
                        dst = stgF[sf][:, cb, th * 512:(th + 1) * 512]
                        if r0 >= FM_MERGE:
                            kb.op("act", lambda e: e.activation(out=dst, in_=pm[pi][:], func=AF.Sigmoid),
                                  reads=[d_pm[pi]], writes=[d_stgF[sf]])
                        elif FM_FQ <= r0 < FM_FK:
                            kb.op("dve", lambda e: e.tensor_scalar(out=dst, in0=pm[pi][:], scalar1=0.125,
                                                                   scalar2=None, op0=ALU.mult),
                                  reads=[d_pm[pi]], writes=[d_stgF[sf]])
                        else:
                            kb.op("dve", lambda e: e.tensor_copy(out=dst, in_=pm[pi][:]),
                                  reads=[d_pm[pi]], writes=[d_stgF[sf]])
                kb.dma("sp", fm[r0:r0 + 512, t0:t0 + TT].rearrange("(cb p) t -> p cb t", p=128), stgF[sf][:],
                       reads=[d_stgF[sf]], writes=[d_fm], slot=d_stgF[sf], merge_w=True)
        kb.barrier()


def load_consts(kb, es, cst_ap):
    nc = kb.nc
    d = Dep()
    C = {"dep": d}
    for i, (name, dt) in enumerate([("ident", BF16), ("U", F32), ("ones", F32), ("U64", F32), ("U64s", F32), ("BO64s", F32)]):
        t = es.enter_context(nc.sbuf_tensor("c_" + name, [128, 128], dt))
        kb.dma("pool", t[:], cst_ap[i], writes=[d], slot=d, merge_w=True)
        C[name] = t
    t = es.enter_context(nc.sbuf_tensor("c_Ub", [128, 128], BF16))
    kb.dma("pool", t[:], cst_ap[1], writes=[d], slot=d, merge_w=True)
    C["Ub"] = t
    t = es.enter_context(nc.sbuf_tensor("c_identf", [128, 128], F32))
    kb.dma("pool", t[:], cst_ap[0], writes=[d], slot=d, merge_w=True)
    C["identf"] = t
    return C


def host_consts():
    c = np.zeros((6, 128, 128), np.float32)
    c[0] = np.eye(128)
    c[1] = np.triu(np.ones((128, 128)))
    c[2] = 1.0
    u64 = np.triu(np.ones((64, 64)))
    c[3, 0:64, 0:64] = u64
    c[3, 64:128, 64:128] = u64
    c[4] = c[3] / 16.0
    c[5, 0:64, 0:64] = 1.0 / 16.0
    c[5, 64:128, 64:128] = 1.0 / 16.0
    return c


def attention_core(kb, es, S, C, groups, load_head, finish_head, bias_for, dv, tagp):
    nc = kb.nc
    NBLK = S // 128
    pS = [es.enter_context(nc.psum_tensor(kb.u(tagp + "_pS%d" % i), [128, 128], F32)) for i in range(3)]
    pO = [es.enter_context(nc.psum_tensor(kb.u(tagp + "_pO%d" % i), [128, dv + 1], F32)) for i in range(2)]
    pt = [es.enter_context(nc.sbuf_tensor(kb.u(tagp + "_pt%d" % i), [128, 128], BF16)) for i in range(4)]
    d_pS = [Dep() for _ in pS]
    d_pO = [Dep() for _ in pO]
    d_pt = [Dep() for _ in pt]
    LA = 2
    cnt = {"i": 0, "o": 0}
    for grp in groups:
        loaded = {h: load_head(h) for h in grp}
        tasks = []
        for qb in range(NBLK):
            for h in grp:
                for kbk in range(qb + 1):
                    tasks.append((h, qb, kbk))
        info = {}
        n = len(tasks)
        for i in range(n + LA):
            if i < n:
                h, qb, kbk = tasks[i]
                kparts, qparts, vaug_fn, hdeps = loaded[h]
                if kbk == 0:
                    bias_fn, bdeps = bias_for(h, qb)
                    o = cnt["o"] % 2; cnt["o"] += 1
                    info[(h, qb)] = (bias_fn, bdeps, o)
                bias_fn, bdeps, o = info[(h, qb)]
                gi = cnt["i"]; cnt["i"] += 1
                s_ = gi % 3
                t_ = gi % 4
                npart = len(kparts)
                for pi_ in range(npart):
                    kb.op("pe", lambda e: e.matmul(pS[s_][:], kparts[pi_](kbk), qparts[pi_](qb),
                                                   start=(pi_ == 0), stop=(pi_ == npart - 1)),
                          reads=hdeps, writes=[d_pS[s_]], inc=(pi_ == npart - 1))
                if bias_fn is not None:
                    b_ap = bias_fn(kbk)
                    kb.op("act", lambda e: e.activation(out=pt[t_][:], in_=pS[s_][:], func=AF.Exp, bias=b_ap),
                          reads=[d_pS[s_]] + bdeps, writes=[d_pt[t_]])
                else:
                    kb.op("act", lambda e: e.activation(out=pt[t_][:], in_=pS[s_][:], func=AF.Exp),
                          reads=[d_pS[s_]], writes=[d_pt[t_]])
                if kbk == qb:
                    kb.op("pool", lambda e: e.tensor_tensor(out=pt[t_][:], in0=pt[t_][:], in1=C["Ub"][:], op=ALU.mult),
                          reads=[d_pt[t_], C["dep"]], writes=[d_pt[t_]])
                tasks[i] = (h, qb, kbk, t_)
            if i >= LA:
                h, qb, kbk, t_ = tasks[i - LA]
                kparts, qparts, vaug_fn, hdeps = loaded[h]
                o = info[(h, qb)][2]
                kb.op("pe", lambda e: e.matmul(pO[o][:], pt[t_][:], vaug_fn(kbk), start=(kbk == 0), stop=(kbk == qb)),
                      reads=[d_pt[t_]] + hdeps, writes=[d_pO[o]], inc=(kbk == qb))
                if kbk == qb:
                    finish_head(h, qb, pO[o], d_pO[o])


def attention_core_grouped(kb, es, S, C, groups, load_head, finish_head, dv, tagp, GQ=4, bias_for=None):
    nc = kb.nc
    NBLK = S // 128
    GQ = min(GQ, NBLK)
    NG = NBLK // GQ
    W = GQ * 128
    pS = [es.enter_context(nc.psum_tensor(kb.u(tagp + "_gS%d" % i), [128, W], F32)) for i in range(3)]
    pO = [es.enter_context(nc.psum_tensor(kb.u(tagp + "_gO%d" % i), [128, 512], F32)) for i in range(4)]
    pt = [es.enter_context(nc.sbuf_tensor(kb.u(tagp + "_gt%d" % i), [128, W], BF16)) for i in range(3)]
    d_pS = [Dep() for _ in pS]
    d_pO = [Dep() for _ in pO]
    d_pt = [Dep() for _ in pt]
    LA = 2
    cnt = 0
    ocnt = 0
    for grp in groups:
        loaded = {h: load_head(h) for h in grp}
        for G in range(NG):
            for h in grp:
                kparts, qparts, vaug_fn, hdeps = loaded[h]
                npart = len(kparts)
                bias_fn, bdeps = bias_for(h, G) if bias_for is not None else (None, [])
                obase = ocnt % (4 // GQ if GQ < 4 else 1) * GQ
                ocnt += 1
                n = GQ * G + GQ
                rec = {}
                for i in range(n + LA):
                    if i < n:
                        kbk = i
                        j0 = max(kbk - GQ * G, 0)
                        width = (GQ - j0) * 128
                        c0 = (GQ * G + j0) * 128
                        s_ = cnt % 3
                        cnt += 1
                        for pi_ in range(npart):
                            kb.op("pe", lambda e: e.matmul(pS[s_][:, 0:width], kparts[pi_](kbk), qparts[pi_](c0, width),
                                                           start=(pi_ == 0), stop=(pi_ == npart - 1)),
                                  reads=hdeps, writes=[d_pS[s_]], inc=(pi_ == npart - 1))
                        if bias_fn is not None:
                            b_ap = bias_fn(kbk)
                            kb.op("act", lambda e: e.activation(out=pt[s_][:, 0:width], in_=pS[s_][:, 0:width], func=AF.Exp, bias=b_ap),
                                  reads=[d_pS[s_]] + bdeps, writes=[d_pt[s_]])
                        else:
                            kb.op("act", lambda e: e.activation(out=pt[s_][:, 0:width], in_=pS[s_][:, 0:width], func=AF.Exp),
                                  reads=[d_pS[s_]], writes=[d_pt[s_]])
                        if kbk >= GQ * G:
                            kb.op("pool", lambda e: e.tensor_tensor(out=pt[s_][:, 0:128], in0=pt[s_][:, 0:128], in1=C["Ub"][:], op=ALU.mult),
                                  reads=[d_pt[s_], C["dep"]], writes=[d_pt[s_]])
                        rec[i] = (kbk, j0, s_)
                    if i >= LA:
                        kbk, j0, s_ = rec[i - LA]
                        for j in range(j0, GQ):
                            qb = GQ * G + j
                            o = obase + j
                            kb.op("pe", lambda e: e.matmul(pO[o][:, 0:dv + 1], pt[s_][:, (j - j0) * 128:(j - j0 + 1) * 128], vaug_fn(kbk),
                                                           start=(kbk == 0), stop=(kbk == qb)),
                                  reads=[d_pt[s_]] + hdeps, writes=[d_pO[o]])
                            if kbk == qb:
                                finish_head(h, qb, pO[o], d_pO[o])


def phase_B(kb, S, fm, tm, tms, fgb_ap, obr, C):
    nc = kb.nc
    NBLK = S // 128
    with ExitStack() as es:
        def sb(name, shape, dt):
            return es.enter_context(nc.sbuf_tensor(kb.u(name), shape, dt))

        G = sb("B_G", [128, NBLK, 16], F32)
        bt = sb("B_bt", [128, 16], F32)
        Tt = sb("B_Tt", [128, NBLK, 16], F32)
        Gc = sb("B_Gc", [128, NBLK, 16], F32)
        Gend = sb("B_Gend", [128, NBLK, 16], F32)
        d_G = Dep(); d_bt = Dep(); d_Tt = Dep(); d_Gc = Dep(); d_Gend = Dep()
        with ExitStack() as es0:
            psC = es0.enter_context(nc.psum_tensor(kb.u("B_psC"), [128, NBLK * 16], F32))
            psT = es0.enter_context(nc.psum_tensor(kb.u("B_psT"), [128, NBLK * 16], F32))
            d_psC = Dep(); d_psT = Dep()
            kb.dma("sp", G[:], tms[:, 16:32].rearrange("(b p) h -> p b h", p=128), writes=[d_G])
            kb.dma("sp", bt[:], fgb_ap.partition_broadcast(128), writes=[d_bt])
            kb.op("dve", lambda e: e.tensor_tensor(out=G[:], in0=G[:],
                                                   in1=bt[:].unsqueeze(1).broadcast_to([128, NBLK, 16]), op=ALU.add),
                  reads=[d_G, d_bt], writes=[d_G])
            kb.op("act", lambda e: e.activation(out=G[:], in_=G[:], func=AF.Exp, scale=-1.0), reads=[d_G], writes=[d_G])
            kb.op("act", lambda e: e.activation(out=G[:], in_=G[:], func=AF.Ln, bias=1.0), reads=[d_G], writes=[d_G])
            G2 = G[:].rearrange("p b h -> p (b h)")
            kb.op("pe", lambda e: e.matmul(psC[:], C["U"][:], G2, start=True, stop=True),
                  reads=[d_G, C["dep"]], writes=[d_psC])
            kb.op("pe", lambda e: e.matmul(psT[:], C["ones"][:], G2, start=True, stop=True),
                  reads=[d_G, C["dep"]], writes=[d_psT])
            kb.op("dve", lambda e: e.tensor_copy(out=Tt[:].rearrange("p b h -> p (b h)"), in_=psT[:]),
                  reads=[d_psT], writes=[d_Tt])
            kb.op("dve", lambda e: e.tensor_copy(out=Gend[:, 0, :], in_=Tt[:, 0, :]), reads=[d_Tt], writes=[d_Gend])
            for b in range(1, NBLK):
                kb.op("dve", lambda e: e.tensor_tensor(out=Gend[:, b, :], in0=Gend[:, b - 1, :], in1=Tt[:, b, :], op=ALU.add),
                      reads=[d_Tt, d_Gend], writes=[d_Gend])
            kb.op("dve", lambda e: e.tensor_copy(out=Gc[:, 0, :], in_=psC[:, 0:16]), reads=[d_psC], writes=[d_Gc])
            if NBLK > 1:
                kb.op("dve", lambda e: e.tensor_tensor(out=Gc[:, 1:, :], in0=psC[:, 16:].rearrange("p (b h) -> p b h", h=16),
                                                       in1=Gend[:, 0:NBLK - 1, :], op=ALU.add),
                      reads=[d_psC, d_Gend], writes=[d_Gc])
            kb.barrier()

        qT = [sb("B_qT%d" % i, [128, S], BF16) for i in range(2)]
        kT = [sb("B_kT%d" % i, [128, S], BF16) for i in range(2)]
        vr = [sb("B_vr%d" % i, [128, NBLK, 128], BF16) for i in range(2)]
        va = [sb("B_va%d" % i, [128, NBLK, 2, 72], BF16) for i in range(2)]
        gt = [sb("B_gt%d" % i, [128, NBLK, 128], BF16) for i in range(2)]
        obT = [sb("B_obT%d" % i, [128, S], BF16) for i in range(2)]
        ob = [sb("B_ob%d" % i, [128, 128], BF16) for i in range(2)]
        rs = [sb("B_rs%d" % i, [128, 1], F32) for i in range(4)]
        bias = [sb("B_bias%d" % i, [128, NBLK], F32) for i in range(4)]
        pTr = es.enter_context(nc.psum_tensor(kb.u("B_pTr"), [128, 128], BF16))
        d_q = [Dep(), Dep()]; d_k = [Dep(), Dep()]; d_vr = [Dep(), Dep()]; d_va = [Dep(), Dep()]
        d_gt = [Dep(), Dep()]; d_obT = [Dep(), Dep()]; d_ob = [Dep(), Dep()]
        d_rs = [Dep() for _ in rs]; d_bias = [Dep() for _ in bias]; d_pTr = Dep(); d_obr = Dep()
        for i in range(2):
            kb.op("pool", lambda e: e.memset(va[i][:], 1.0), writes=[d_va[i]])
        state = {"bi": 0, "ri": 0, "obi": 0}

        def load_head(h):
            hp, hh = h // 2, h % 2
            b = hp % 2
            if hh == 0:
                kb.dma("sp", qT[b][:], fm[FM_FQ + hp * 128:FM_FQ + (hp + 1) * 128, :], writes=[d_q[b]])
                kb.dma("sp", kT[b][:], fm[FM_FK + hp * 128:FM_FK + (hp + 1) * 128, :], writes=[d_k[b]])
                kb.dma("sp", vr[b][:], tm[:, TM_FV + hp * 128:TM_FV + (hp + 1) * 128].rearrange("(b p) c -> p b c", p=128),
                       writes=[d_vr[b]])
                kb.dma("sp", gt[b][:], tm[:, TM_FG + hp * 128:TM_FG + (hp + 1) * 128].rearrange("(b p) c -> p b c", p=128),
                       writes=[d_gt[b]])
                kb.op("dve", lambda e: e.tensor_copy(out=va[b][:, :, :, 0:64],
                                                     in_=vr[b][:].rearrange("p b (h c) -> p b h c", h=2)),
                      reads=[d_vr[b]], writes=[d_va[b]])
            p0 = hh * 64
            kparts = [lambda blk: kT[b][p0:p0 + 64, blk * 128:(blk + 1) * 128]]
            qparts = [lambda c0, w: qT[b][p0:p0 + 64, c0:c0 + w]]
            return kparts, qparts, (lambda blk: va[b][:, blk, hh, 0:65]), [d_q[b], d_k[b], d_va[b]]

        GQ = min(2, NBLK)

        def bias_for(h, G):
            bi = state["bi"] % 4; state["bi"] += 1
            nk = GQ * G + GQ
            kb.op("dve", lambda e: e.tensor_scalar(out=bias[bi][:, 0:nk], in0=Gc[:, 0:nk, h],
                                                   scalar1=Gend[:, GQ * G, h:h + 1], scalar2=None, op0=ALU.subtract),
                  reads=[d_Gc, d_Gend], writes=[d_bias[bi]])
            return (lambda blk: bias[bi][:, blk:blk + 1]), [d_bias[bi]]

        def finish_head(h, qb, pO, d_pO):
            hp, hh = h // 2, h % 2
            b = hp % 2
            ri = state["ri"] % 4; state["ri"] += 1
            o = qb % 2
            kb.op("dve", lambda e: e.reciprocal(out=rs[ri][:], in_=pO[:, 64:65]), reads=[d_pO], writes=[d_rs[ri]])
            kb.op("dve", lambda e: e.scalar_tensor_tensor(out=ob[o][:, hh * 64:(hh + 1) * 64], in0=pO[:, 0:64],
                                                          scalar=rs[ri][:, 0:1], in1=gt[b][:, qb, hh * 64:(hh + 1) * 64],
                                                          op0=ALU.mult, op1=ALU.mult),
                  reads=[d_pO, d_rs[ri], d_gt[b]], writes=[d_ob[o]])
            if hh == 1:
                kb.op("pe", lambda e: e.transpose(out=pTr[:], in_=ob[o][:], identity=C["ident"][:]),
                      reads=[d_ob[o], C["dep"]], writes=[d_pTr])
                kb.op("dve", lambda e: e.tensor_copy(out=obT[b][:, qb * 128:(qb + 1) * 128], in_=pTr[:]),
                      reads=[d_pTr], writes=[d_obT[b]])
                if qb == NBLK - 1:
                    kb.dma("sp", obr[1, hp * 128:(hp + 1) * 128, :], obT[b][:], reads=[d_obT[b]], writes=[d_obr],
                           slot=d_obT[b], merge_w=True)

        attention_core_grouped(kb, es, S, C, [[2 * i, 2 * i + 1] for i in range(8)], load_head, finish_head, 64, "B", GQ=GQ, bias_for=bias_for)
        kb.barrier()


TWO_PI = 6.283185307179586
CW1 = 6.28125
CW2 = TWO_PI - CW1


def setup_rope(kb, es, S, pos_ap, invf_ap):
    nc = kb.nc
    NBLK = S // 128
    cosT = es.enter_context(nc.sbuf_tensor("R_cos", [128, NBLK, 32], F32))
    sinT = es.enter_context(nc.sbuf_tensor("R_sin", [128, NBLK, 32], F32))
    d_rope = Dep()
    with ExitStack() as es0:
        def sb(name, shape, dt):
            return es0.enter_context(nc.sbuf_tensor(kb.u(name), shape, dt))
        posi = sb("R_posi", [128, NBLK], mybir.dt.int32)
        posf = sb("R_posf", [128, NBLK], F32)
        invf = sb("R_invf", [128, 32], F32)
        ang = sb("R_ang", [128, NBLK, 32], F32)
        tq = sb("R_tq", [128, NBLK, 32], F32)
        ni = sb("R_ni", [128, NBLK, 32], mybir.dt.int32)
        nf = sb("R_nf", [128, NBLK, 32], F32)
        r = sb("R_r", [128, NBLK, 32], F32)
        m = sb("R_m", [128, NBLK, 32], F32)
        d = Dep()
        kb.dma("sp", posi[:], pos_ap, writes=[d])
        kb.dma("sp", invf[:], invf_ap.partition_broadcast(128), writes=[d], slot=d, merge_w=True)
        kb.op("dve", lambda e: e.tensor_copy(out=posf[:], in_=posi[:]), reads=[d], writes=[d])
        kb.op("dve", lambda e: e.tensor_tensor(out=ang[:], in0=posf[:].unsqueeze(2).broadcast_to([128, NBLK, 32]),
                                               in1=invf[:].unsqueeze(1).broadcast_to([128, NBLK, 32]), op=ALU.mult),
              reads=[d], writes=[d])
        for which, dst in ((0, sinT), (1, cosT)):
            if which == 1:
                kb.op("dve", lambda e: e.tensor_scalar_add(out=ang[:], in0=ang[:], scalar1=float(np.pi / 2)),
                      reads=[d], writes=[d])
            kb.op("dve", lambda e: e.tensor_scalar_mul(out=tq[:], in0=ang[:], scalar1=float(1.0 / TWO_PI)), reads=[d], writes=[d])
            kb.op("dve", lambda e: e.tensor_copy(out=ni[:], in_=tq[:]), reads=[d], writes=[d])
            kb.op("dve", lambda e: e.tensor_copy(out=nf[:], in_=ni[:]), reads=[d], writes=[d])
            kb.op("dve", lambda e: e.scalar_tensor_tensor(out=r[:], in0=nf[:], scalar=-CW1, in1=ang[:], op0=ALU.mult, op1=ALU.add),
                  reads=[d], writes=[d])
            kb.op("dve", lambda e: e.scalar_tensor_tensor(out=r[:], in0=nf[:], scalar=-CW2, in1=r[:], op0=ALU.mult, op1=ALU.add),
                  reads=[d], writes=[d])
            kb.op("dve", lambda e: e.tensor_single_scalar(out=m[:], in_=r[:], scalar=float(np.pi), op=ALU.is_gt), reads=[d], writes=[d])
            kb.op("dve", lambda e: e.scalar_tensor_tensor(out=r[:], in0=m[:], scalar=-TWO_PI, in1=r[:], op0=ALU.mult, op1=ALU.add),
                  reads=[d], writes=[d])
            kb.op("dve", lambda e: e.tensor_single_scalar(out=m[:], in_=r[:], scalar=float(-np.pi), op=ALU.is_lt), reads=[d], writes=[d])
            kb.op("dve", lambda e: e.scalar_tensor_tensor(out=r[:], in0=m[:], scalar=TWO_PI, in1=r[:], op0=ALU.mult, op1=ALU.add),
                  reads=[d], writes=[d])
            kb.op("dve", lambda e: e.tensor_scalar(out=r[:], in0=r[:], scalar1=float(np.pi), scalar2=float(-np.pi),
                                                   op0=ALU.min, op1=ALU.max), reads=[d], writes=[d])
            kb.op("act", lambda e: e.activation(out=dst[:], in_=r[:], func=AF.Sin), reads=[d], writes=[d_rope])
        kb.barrier()
    return cosT, sinT, d_rope


def host_invf():
    return (1.0 / (np.float32(10000.0) ** (np.arange(32, dtype=np.float32) * np.float32(2.0) / np.float32(64)))).astype(
        np.float32).reshape(1, 32)


def emit_rmsnorm(kb, src, d_src, n, gam, d_gam, dst, d_dst, sq, d_sq, st, d_st):
    kb.op("act", lambda e: e.activation(out=sq, in_=src, func=AF.Square, scale=float(n ** -0.5)), reads=[d_src], writes=[d_sq])
    kb.op("dve", lambda e: e.reduce_sum(out=st[:, 0:1], in_=sq, axis=AX.X), reads=[d_sq], writes=[d_st])
    kb.op("act", lambda e: e.activation(out=st[:, 1:2], in_=st[:, 0:1], func=AF.Sqrt, bias=EPS), reads=[d_st], writes=[d_st])
    kb.op("dve", lambda e: e.reciprocal(out=st[:, 2:3], in_=st[:, 1:2]), reads=[d_st], writes=[d_st])
    kb.op("dve", lambda e: e.scalar_tensor_tensor(out=dst, in0=src, scalar=st[:, 2:3], in1=gam, op0=ALU.mult, op1=ALU.mult),
          reads=[d_src, d_st, d_gam], writes=[d_dst])


def emit_rotary(kb, src, d_src, H, cos_ap, sin_ap, d_rope, dst, d_dst, tmp, d_tmp):
    s3 = src.rearrange("p (h c) -> p h c", h=H)
    o3 = dst.rearrange("p (h c) -> p h c", h=H)
    t3 = tmp.rearrange("p (h c) -> p h c", h=H)
    cb = cos_ap.unsqueeze(1).broadcast_to([128, H, 32])
    sbc = sin_ap.unsqueeze(1).broadcast_to([128, H, 32])
    kb.op("dve", lambda e: e.tensor_tensor(out=t3[:, :, 0:32], in0=s3[:, :, 0:32], in1=cb, op=ALU.mult),
          reads=[d_src, d_rope], writes=[d_tmp])
    kb.op("dve", lambda e: e.tensor_tensor(out=t3[:, :, 32:64], in0=s3[:, :, 32:64], in1=sbc, op=ALU.mult),
          reads=[d_src, d_rope], writes=[d_tmp])
    kb.op("dve", lambda e: e.tensor_tensor(out=o3[:, :, 0:32], in0=t3[:, :, 0:32], in1=t3[:, :, 32:64], op=ALU.subtract),
          reads=[d_tmp], writes=[d_dst])
    kb.op("dve", lambda e: e.tensor_tensor(out=t3[:, :, 0:32], in0=s3[:, :, 0:32], in1=sbc, op=ALU.mult),
          reads=[d_src, d_rope], writes=[d_tmp])
    kb.op("dve", lambda e: e.tensor_tensor(out=t3[:, :, 32:64], in0=s3[:, :, 32:64], in1=cb, op=ALU.mult),
          reads=[d_src, d_rope], writes=[d_tmp])
    kb.op("dve", lambda e: e.tensor_tensor(out=o3[:, :, 32:64], in0=t3[:, :, 0:32], in1=t3[:, :, 32:64], op=ALU.add),
          reads=[d_tmp], writes=[d_dst])


MLA_SCALE = float(192 ** -0.5)


def phase_D(kb, S, tm, tms, qn_ap, kvn_ap, wuq_ap, wukv_ap, mq, mk, mv, obr, C, rope):
    nc = kb.nc
    NBLK = S // 128
    cosT, sinT, d_rope = rope
    TS = 512 if S >= 512 else S
    NJ = TS // 128
    d_mq = Dep(); d_mk = Dep(); d_mv = Dep()
    with ExitStack() as es:
        def sb(name, shape, dt):
            return es.enter_context(nc.sbuf_tensor(kb.u(name), shape, dt))

        def ps(name, shape, dt):
            return es.enter_context(nc.psum_tensor(kb.u(name), shape, dt))
        wuq = sb("D_wuq", [128, 4, 1536], BF16)
        wukv = sb("D_wukv", [128, 4, 2048], BF16)
        gq = sb("D_gq", [128, 512], F32)
        gkv = sb("D_gkv", [128, 512], F32)
        d_w = Dep()
        kb.dma("pool", wuq[:], wuq_ap.rearrange("(kc p) n -> p kc n", p=128), writes=[d_w], slot=d_w, merge_w=True)
        kb.dma("pool", wukv[:], wukv_ap.rearrange("(kc p) n -> p kc n", p=128), writes=[d_w], slot=d_w, merge_w=True)
        kb.dma("sp", gq[:], qn_ap.partition_broadcast(128), writes=[d_w], slot=d_w, merge_w=True)
        kb.dma("sp", gkv[:], kvn_ap.partition_broadcast(128), writes=[d_w], slot=d_w, merge_w=True)
        cqr = sb("D_cqr", [128, NJ, 512], BF16)
        ckvr = sb("D_ckvr", [128, NJ, 512], BF16)
        krr = sb("D_krr", [128, NJ, 64], F32)
        sq = sb("D_sq", [128, 512], F32)
        st = sb("D_st", [128, 4], F32)
        cn = [sb("D_cn%d" % i, [128, 1024], BF16) for i in range(2)]
        cT = sb("D_cT", [128, 8, TS], BF16)
        stgQ = sb("D_stgQ", [128, 8, TS], BF16)
        stgK = sb("D_stgK", [128, 8, TS], BF16)
        stgV = sb("D_stgV", [128, NJ, 1024], BF16)
        stgQR = sb("D_stgQR", [128, 4, TS], BF16)
        stgKR = sb("D_stgKR", [64, TS], BF16)
        q32 = sb("D_q32", [128, 512], F32)
        qtmp = sb("D_qtmp", [128, 512], F32)
        qrb = sb("D_qrb", [128, 512], BF16)
        ktmp = sb("D_ktmp", [128, 64], F32)
        krb = sb("D_krb", [128, 64], BF16)
        pT = ps("D_pT", [128, 1024], BF16)
        pm = [ps("D_pm%d" % i, [128, 512], F32) for i in range(3)]
        pT2 = ps("D_pT2", [128, 640], BF16)
        d_cqr = Dep(); d_ckvr = Dep(); d_krr = Dep(); d_sq = Dep(); d_st = Dep(); d_cn = [Dep(), Dep()]
        d_cT = Dep(); d_stgQ = Dep(); d_stgK = Dep(); d_stgV = Dep(); d_stgQR = Dep(); d_stgKR = Dep()
        d_q32 = Dep(); d_qtmp = Dep(); d_qrb = Dep(); d_ktmp = Dep(); d_krb = Dep()
        d_pT = Dep(); d_pm = [Dep() for _ in pm]; d_pT2 = Dep()
        pmi = 0
        for ts_ in range(S // TS):
            t0 = ts_ * TS
            kb.dma("sp", cqr[:], tm[t0:t0 + TS, TM_LCQ:TM_LCQ + 512].rearrange("(j p) c -> p j c", p=128), writes=[d_cqr])
            kb.dma("sp", ckvr[:], tm[t0:t0 + TS, TM_LCKV:TM_LCKV + 512].rearrange("(j p) c -> p j c", p=128), writes=[d_ckvr])
            kb.dma("sp", krr[:], tms[t0:t0 + TS, 48:112].rearrange("(j p) c -> p j c", p=128), writes=[d_krr])
            for j in range(NJ):
                c_ = j % 2
                emit_rmsnorm(kb, cqr[:, j, :], d_cqr, 512, gq[:], d_w, cn[c_][:, 0:512], d_cn[c_], sq[:], d_sq, st, d_st)
                emit_rmsnorm(kb, ckvr[:, j, :], d_ckvr, 512, gkv[:], d_w, cn[c_][:, 512:1024], d_cn[c_], sq[:], d_sq, st, d_st)
                for kc in range(8):
                    kb.op("pe", lambda e: e.transpose(out=pT[:, kc * 128:(kc + 1) * 128], in_=cn[c_][:, kc * 128:(kc + 1) * 128],
                                                      identity=C["ident"][:]),
                          reads=[d_cn[c_], C["dep"]], writes=[d_pT], inc=(kc == 7))
                kb.op("dve", lambda e: e.tensor_copy(out=cT[:, :, j * 128:(j + 1) * 128],
                                                     in_=pT[:].rearrange("p (k t) -> p k t", k=8)),
                      reads=[d_pT], writes=[d_cT])
            for h in range(8):
                for (w_t, off, kc0, stg_, d_stg, scale) in ((wuq, 0, 0, stgQ, d_stgQ, MLA_SCALE), (wukv, 0, 4, stgK, d_stgK, 1.0)):
                    pi = pmi % 3; pmi += 1
                    for kc in range(4):
                        kb.op("pe", lambda e: e.matmul(pm[pi][:, 0:TS], w_t[:, kc, off + h * 128:off + (h + 1) * 128],
                                                       cT[:, kc0 + kc, :], start=(kc == 0), stop=(kc == 3)),
                              reads=[d_w, d_cT], writes=[d_pm[pi]], inc=(kc == 3))
                    kb.op("act", lambda e: e.activation(out=stg_[:, h, :], in_=pm[pi][:, 0:TS], func=AF.Copy, scale=scale),
                          reads=[d_pm[pi]], writes=[d_stg])
            kb.dma("sp", mq[0:1024, t0:t0 + TS].rearrange("(h p) t -> p h t", p=128), stgQ[:], reads=[d_stgQ], writes=[d_mq],
                   slot=d_stgQ, merge_w=True)
            kb.dma("sp", mk[0:1024, t0:t0 + TS].rearrange("(h p) t -> p h t", p=128), stgK[:], reads=[d_stgK], writes=[d_mk],
                   slot=d_stgK, merge_w=True)
            for j in range(NJ):
                for half in range(2):
                    pi = pmi % 3; pmi += 1
                    for kc in range(4):
                        kb.op("pe", lambda e: e.matmul(pm[pi][:], cT[:, 4 + kc, j * 128:(j + 1) * 128],
                                                       wukv[:, kc, 1024 + half * 512:1024 + (half + 1) * 512],
                                                       start=(kc == 0), stop=(kc == 3)),
                              reads=[d_w, d_cT], writes=[d_pm[pi]], inc=(kc == 3))
                    kb.op("dve", lambda e: e.tensor_copy(out=stgV[:, j, half * 512:(half + 1) * 512], in_=pm[pi][:]),
                          reads=[d_pm[pi]], writes=[d_stgV])
                blk = t0 // 128 + j
                pi = pmi % 3; pmi += 1
                for kc in range(4):
                    kb.op("pe", lambda e: e.matmul(pm[pi][:], cT[:, kc, j * 128:(j + 1) * 128], wuq[:, kc, 1024:1536],
                                                   start=(kc == 0), stop=(kc == 3)),
                          reads=[d_w, d_cT], writes=[d_pm[pi]], inc=(kc == 3))
                kb.op("act", lambda e: e.activation(out=q32[:], in_=pm[pi][:], func=AF.Copy, scale=MLA_SCALE),
                      reads=[d_pm[pi]], writes=[d_q32])
                emit_rotary(kb, q32[:], d_q32, 8, cosT[:, blk, :], sinT[:, blk, :], d_rope, qrb[:], d_qrb, qtmp[:], d_qtmp)
                emit_rotary(kb, krr[:, j, :], d_krr, 1, cosT[:, blk, :], sinT[:, blk, :], d_rope, krb[:], d_krb, ktmp[:], d_ktmp)
                for c4 in range(4):
                    kb.op("pe", lambda e: e.transpose(out=pT2[:, c4 * 128:(c4 + 1) * 128], in_=qrb[:, c4 * 128:(c4 + 1) * 128],
                                                      identity=C["ident"][:]),
                          reads=[d_qrb, C["dep"]], writes=[d_pT2], inc=False)
                kb.op("pe", lambda e: e.transpose(out=pT2[0:64, 512:640], in_=krb[:], identity=C["ident"][:]),
                      reads=[d_krb, C["dep"]], writes=[d_pT2])
                kb.op("dve", lambda e: e.tensor_copy(out=stgQR[:, :, j * 128:(j + 1) * 128],
                                                     in_=pT2[:, 0:512].rearrange("p (c t) -> p c t", c=4)),
                      reads=[d_pT2], writes=[d_stgQR])
                kb.op("dve", lambda e: e.tensor_copy(out=stgKR[:, j * 128:(j + 1) * 128], in_=pT2[0:64, 512:640]),
                      reads=[d_pT2], writes=[d_stgKR])
            kb.dma("sp", mv[t0:t0 + TS, :].rearrange("(j p) c -> p j c", p=128), stgV[:], reads=[d_stgV], writes=[d_mv],
                   slot=d_stgV, merge_w=True)
            kb.dma("sp", mq[1024:1536, t0:t0 + TS].rearrange("(c p) t -> p c t", p=128), stgQR[:], reads=[d_stgQR], writes=[d_mq],
                   slot=d_stgQR, merge_w=True)
            kb.dma("sp", mk[1024:1088, t0:t0 + TS], stgKR[:], reads=[d_stgKR], writes=[d_mk], slot=d_stgKR, merge_w=True)
        kb.barrier()

    with ExitStack() as es:
        def sb(name, shape, dt):
            return es.enter_context(nc.sbuf_tensor(kb.u(name), shape, dt))
        qn = [sb("D_qn%d" % i, [128, S], BF16) for i in range(2)]
        qr = [sb("D_qr%d" % i, [128, S], BF16) for i in range(2)]
        kn = [sb("D_kn%d" % i, [128, S], BF16) for i in range(2)]
        kr2 = sb("D_kr2", [128, S], BF16)
        vr = [sb("D_vr%d" % i, [128, NBLK, 128], BF16) for i in range(2)]
        va = [sb("D_va%d" % i, [128, NBLK, 136], BF16) for i in range(2)]
        gt = [sb("D_gt%d" % i, [128, NBLK, 128], BF16) for i in range(2)]
        obT = [sb("D_obT%d" % i, [128, S], BF16) for i in range(2)]
        ob = [sb("D_ob%d" % i, [128, 128], BF16) for i in range(2)]
        rs = [sb("D_rs%d" % i, [128, 1], F32) for i in range(4)]
        pTr = es.enter_context(nc.psum_tensor(kb.u("D_pTr"), [128, 128], BF16))
        d_qn = [Dep(), Dep()]; d_qr = [Dep(), Dep()]; d_kn = [Dep(), Dep()]; d_kr2 = Dep()
        d_vr = [Dep(), Dep()]; d_va = [Dep(), Dep()]; d_gt = [Dep(), Dep()]; d_obT = [Dep(), Dep()]
        d_ob = [Dep(), Dep()]; d_rs = [Dep() for _ in rs]; d_pTr = Dep(); d_obr = Dep()
        for i in range(2):
            kb.op("pool", lambda e: e.memset(va[i][:], 1.0), writes=[d_va[i]])
        kb.dma("sp", kr2[0:64, :], mk[1024:1088, :], reads=[d_mk], writes=[d_kr2], slot=d_kr2, merge_w=True)
        kb.dma("sp", kr2[64:128, :], mk[1024:1088, :], reads=[d_mk], writes=[d_kr2], slot=d_kr2, merge_w=True)
        state = {"ri": 0}

        def load_head(h):
            b = h % 2
            hp, hh = h // 2, h % 2
            kb.dma("sp", qn[b][:], mq[h * 128:(h + 1) * 128, :], reads=[d_mq], writes=[d_qn[b]])
            kb.dma("sp", qr[b][:], mq[1024 + hp * 128:1024 + (hp + 1) * 128, :], reads=[d_mq], writes=[d_qr[b]])
            kb.dma("sp", kn[b][:], mk[h * 128:(h + 1) * 128, :], reads=[d_mk], writes=[d_kn[b]])
            kb.dma("sp", vr[b][:], mv[:, h * 128:(h + 1) * 128].rearrange("(b p) c -> p b c", p=128), reads=[d_mv], writes=[d_vr[b]])
            kb.dma("sp", gt[b][:], tm[:, TM_LG + h * 128:TM_LG + (h + 1) * 128].rearrange("(b p) c -> p b c", p=128),
                   writes=[d_gt[b]])
            kb.op("dve", lambda e: e.tensor_copy(out=va[b][:, :, 0:128], in_=vr[b][:]), reads=[d_vr[b]], writes=[d_va[b]])
            p0 = hh * 64
            kparts = [lambda blk: kn[b][:, blk * 128:(blk + 1) * 128], lambda blk: kr2[p0:p0 + 64, blk * 128:(blk + 1) * 128]]
            qparts = [lambda c0, w: qn[b][:, c0:c0 + w], lambda c0, w: qr[b][p0:p0 + 64, c0:c0 + w]]
            return kparts, qparts, (lambda blk: va[b][:, blk, 0:129]), [d_qn[b], d_qr[b], d_kn[b], d_kr2, d_va[b]]

        def finish_head(h, qb, pO, d_pO):
            b = h % 2
            ri = state["ri"] % 4; state["ri"] += 1
            o = ri % 2
            kb.op("dve", lambda e: e.reciprocal(out=rs[ri][:], in_=pO[:, 128:129]), reads=[d_pO], writes=[d_rs[ri]])
            kb.op("dve", lambda e: e.scalar_tensor_tensor(out=ob[o][:], in0=pO[:, 0:128], scalar=rs[ri][:, 0:1],
                                                          in1=gt[b][:, qb, :], op0=ALU.mult, op1=ALU.mult),
                  reads=[d_pO, d_rs[ri], d_gt[b]], writes=[d_ob[o]])
            kb.op("pe", lambda e: e.transpose(out=pTr[:], in_=ob[o][:], identity=C["ident"][:]),
                  reads=[d_ob[o], C["dep"]], writes=[d_pTr])
            kb.op("dve", lambda e: e.tensor_copy(out=obT[b][:, qb * 128:(qb + 1) * 128], in_=pTr[:]),
                  reads=[d_pTr], writes=[d_obT[b]])
            if qb == NBLK - 1:
                kb.dma("sp", obr[3, h * 128:(h + 1) * 128, :], obT[b][:], reads=[d_obT[b]], writes=[d_obr],
                       slot=d_obT[b], merge_w=True)

        attention_core_grouped(kb, es, S, C, [[h] for h in range(8)], load_head, finish_head, 128, "D")
        kb.barrier()


def host_layout_mla(w_uq_l, w_ukv_l):
    q = w_uq_l.reshape(512, 8, 192)
    wuq = np.concatenate([q[:, :, 0:128].reshape(512, 1024), q[:, :, 128:192].reshape(512, 512)], axis=1)
    kv = w_ukv_l.reshape(512, 8, 256)
    wukv = np.concatenate([kv[:, :, 0:128].reshape(512, 1024), kv[:, :, 128:256].reshape(512, 1024)], axis=1)
    return np.ascontiguousarray(wuq), np.ascontiguousarray(wukv)


def phase_T(kb, S, obr, fm, x_ap, wbr_ap, wout_ap, pgam_ap, out_ap, C, w16dep=None):
    wq = "pool" if w16dep is None else "sp"
    wreads = [] if w16dep is None else [w16dep]
    stq = "sp" if w16dep is None else "pool"
    nc = kb.nc
    TS = 512 if S >= 512 else S
    NJ = TS // 128
    d_out = Dep()
    with ExitStack() as es:
        def sb(name, shape, dt):
            return es.enter_context(nc.sbuf_tensor(kb.u(name), shape, dt))

        def ps(name, shape, dt):
            return es.enter_context(nc.psum_tensor(kb.u(name), shape, dt))
        gam = sb("T_gam", [128, D], F32)
        obT = [sb("T_obT%d" % i, [128, 8, TS], BF16) for i in range(4)]
        wb = [sb("T_wb%d" % i, [128, 8, 512], BF16) for i in range(3)]
        wo = [sb("T_wo%d" % i, [128, 16, 512], BF16) for i in range(2)]
        gt = [sb("T_gt%d" % i, [128, 4, TS], BF16) for i in range(2)]
        acc = sb("T_acc", [128, 4, TS], F32)
        tmp = [sb("T_tmp%d" % i, [128, TS], F32) for i in range(2)]
        mixT = sb("T_mixT", [128, 16, TS], BF16)
        ysb = [sb("T_y%d" % i, [128, D], F32) for i in range(NJ)]
        xt = sb("T_x", [128, D], F32)
        ot = sb("T_o", [128, D], F32)
        st = sb("T_st", [128, 4], F32)
        pb = [ps("T_pb%d" % i, [128, 512], F32) for i in range(3)]
        d_gam = Dep(); d_obT = [Dep() for _ in range(4)]; d_wb = [Dep(), Dep(), Dep()]; d_wo = [Dep(), Dep()]
        d_gt = [Dep(), Dep()]; d_acc = Dep(); d_tmp = [Dep(), Dep()]; d_mixT = Dep(); d_y = [Dep() for _ in range(NJ)]; d_x = Dep()
        d_sq = Dep(); d_o = Dep(); d_st = Dep(); d_pb = [Dep() for _ in pb]
        kb.dma("sp", gam[:], pgam_ap.partition_broadcast(128), writes=[d_gam])
        wbi = 0; woi = 0; gti = 0; pbi = 0; tmi = 0
        for ts_ in range(S // TS):
            t0 = ts_ * TS
            for br in range(4):
                kb.dma("sp", obT[br][:], obr[br, :, t0:t0 + TS].rearrange("(kc p) t -> p kc t", p=128), writes=[d_obT[br]])
            for ng in range(4):
                for br in range(4):
                    w_ = wbi % 3; wbi += 1
                    kb.dma(wq, wb[w_][:], wbr_ap[br, :, ng * 512:(ng + 1) * 512].rearrange("(kc p) n -> p kc n", p=128),
                           reads=wreads, writes=[d_wb[w_]], slot=d_wb[w_])
                    g_ = gti % 2; gti += 1
                    r0 = FM_MERGE + br * 2048 + ng * 512
                    kb.dma("sp", gt[g_][:], fm[r0:r0 + 512, t0:t0 + TS].rearrange("(c p) t -> p c t", p=128), writes=[d_gt[g_]])
                    for c4 in range(4):
                        p_ = pbi % 3; pbi += 1
                        for kc in range(8):
                            kb.op("pe", lambda e: e.matmul(pb[p_][:, 0:TS], wb[w_][:, kc, c4 * 128:(c4 + 1) * 128], obT[br][:, kc, :],
                                                           start=(kc == 0), stop=(kc == 7)),
                                  reads=[d_wb[w_], d_obT[br]], writes=[d_pb[p_]], inc=(kc == 7))
                        if br == 0:
                            kb.op("dve", lambda e: e.tensor_tensor(out=acc[:, c4, :], in0=pb[p_][:, 0:TS], in1=gt[g_][:, c4, :], op=ALU.mult),
                                  reads=[d_pb[p_], d_gt[g_]], writes=[d_acc])
                        else:
                            m_ = tmi % 2; tmi += 1
                            kb.op("dve", lambda e: e.tensor_tensor(out=tmp[m_][:], in0=pb[p_][:, 0:TS], in1=gt[g_][:, c4, :], op=ALU.mult),
                                  reads=[d_pb[p_], d_gt[g_]], writes=[d_tmp[m_]])
                            if br < 3:
                                kb.op("dve", lambda e: e.tensor_tensor(out=acc[:, c4, :], in0=acc[:, c4, :], in1=tmp[m_][:], op=ALU.add),
                                      reads=[d_tmp[m_], d_acc], writes=[d_acc])
                            else:
                                kb.op("dve", lambda e: e.tensor_tensor(out=mixT[:, ng * 4 + c4, :], in0=acc[:, c4, :], in1=tmp[m_][:], op=ALU.add),
                                      reads=[d_tmp[m_], d_acc], writes=[d_mixT])
            for mb in range(4):
                w_ = woi % 2; woi += 1
                kb.dma(wq, wo[w_][:], wout_ap[:, mb * 512:(mb + 1) * 512].rearrange("(kc p) n -> p kc n", p=128),
                       reads=wreads, writes=[d_wo[w_]], slot=d_wo[w_])
                for j in range(NJ):
                    p_ = pbi % 3; pbi += 1
                    for kc in range(16):
                        kb.op("pe", lambda e: e.matmul(pb[p_][:], mixT[:, kc, j * 128:(j + 1) * 128], wo[w_][:, kc, :],
                                                       start=(kc == 0), stop=(kc == 15)),
                              reads=[d_mixT, d_wo[w_]], writes=[d_pb[p_]], inc=(kc == 15))
                    kb.op("act", lambda e: e.activation(out=ysb[j][:, mb * 512:(mb + 1) * 512], in_=pb[p_][:], func=AF.Copy),
                          reads=[d_pb[p_]], writes=[d_y[j]])
            for j in range(NJ):
                tok = t0 + j * 128
                kb.dma("sp", xt[:], x_ap[tok:tok + 128, :], writes=[d_x])
                kb.op("act", lambda e: e.activation(out=ot[:], in_=ysb[j][:], func=AF.Square, scale=float(D ** -0.5)), reads=[d_y[j]], writes=[d_o])
                kb.op("dve", lambda e: e.reduce_sum(out=st[:, 0:1], in_=ot[:], axis=AX.X), reads=[d_o], writes=[d_st])
                kb.op("act", lambda e: e.activation(out=st[:, 1:2], in_=st[:, 0:1], func=AF.Sqrt, bias=EPS), reads=[d_st], writes=[d_st])
                kb.op("dve", lambda e: e.reciprocal(out=st[:, 2:3], in_=st[:, 1:2]), reads=[d_st], writes=[d_st])
                kb.op("dve", lambda e: e.scalar_tensor_tensor(out=ot[:], in0=ysb[j][:], scalar=st[:, 2:3], in1=gam[:], op0=ALU.mult, op1=ALU.mult),
                      reads=[d_y[j], d_st, d_gam, d_o], writes=[d_o])
                kb.op("dve", lambda e: e.tensor_tensor(out=ot[:], in0=ot[:], in1=xt[:], op=ALU.add), reads=[d_o, d_x], writes=[d_o])
                kb.dma(stq, out_ap[tok:tok + 128, :], ot[:], reads=[d_o], writes=[d_out], slot=d_o, merge_w=True)
        kb.barrier()
    return d_out


def host_layout_ssd(conv_w_l, conv_b_l):
    cw = np.ascontiguousarray(conv_w_l.reshape(4, 12, 128).transpose(2, 1, 0))
    cb = np.ascontiguousarray(conv_b_l.reshape(12, 128).T)
    return cw, cb


def phase_A(kb, S, fm, tm, tms, cw_ap, cb_ap, dtb_ap, alog_ap, dsk_ap, sn_ap, sx, obr, C, only=None):
    nc = kb.nc
    NBLK = S // 128
    d_sx = Dep()
    with ExitStack() as es:
        def sb(name, shape, dt):
            return es.enter_context(nc.sbuf_tensor(kb.u(name), shape, dt))
        cw = sb("A_cw", [128, 12, 4], F32)
        cbias = sb("A_cb", [128, 12], F32)
        xin = [sb("A_xin%d" % i, [128, S + 4], BF16) for i in range(2)]
        acc = sb("A_acc", [128, S], F32)
        outb = [sb("A_outb%d" % i, [128, S], BF16) for i in range(2)]
        d_c = Dep(); d_xin = [Dep(), Dep()]; d_acc = Dep(); d_outb = [Dep(), Dep()]
        kb.dma("sp", cw[:], cw_ap, writes=[d_c], slot=d_c, merge_w=True)
        kb.dma("sp", cbias[:], cb_ap, writes=[d_c], slot=d_c, merge_w=True)
        for i in range(2):
            kb.op("pool", lambda e: e.memset(xin[i][:, 0:4], 0.0), writes=[d_xin[i]])
        for cb_ in range(12):
            b = cb_ % 2
            kb.dma("sp", xin[b][:, 4:4 + S], fm[cb_ * 128:(cb_ + 1) * 128, :], writes=[d_xin[b]], merge_w=True)
            kb.op("dve", lambda e: e.tensor_scalar(out=acc[:], in0=xin[b][:, 1:1 + S], scalar1=cw[:, cb_, 0:1], scalar2=None, op0=ALU.mult),
                  reads=[d_xin[b], d_c], writes=[d_acc])
            for j in range(1, 4):
                kb.op("dve", lambda e: e.scalar_tensor_tensor(out=acc[:], in0=xin[b][:, 1 + j:1 + j + S], scalar=cw[:, cb_, j:j + 1],
                                                              in1=acc[:], op0=ALU.mult, op1=ALU.add),
                      reads=[d_xin[b], d_c, d_acc], writes=[d_acc])
            kb.op("act", lambda e: e.activation(out=outb[b][:], in_=acc[:], func=AF.Silu, bias=cbias[:, cb_:cb_ + 1]),
                  reads=[d_acc, d_c], writes=[d_outb[b]])
            kb.dma("sp", sx[cb_ * 128:(cb_ + 1) * 128, :], outb[b][:], reads=[d_outb[b]], writes=[d_sx], slot=d_outb[b], merge_w=True)
        kb.barrier()

    if only == 'A1':
        return
    with ExitStack() as es:
        def sb(name, shape, dt):
            return es.enter_context(nc.sbuf_tensor(kb.u(name), shape, dt))

        def ps(name, shape, dt):
            return es.enter_context(nc.psum_tensor(kb.u(name), shape, dt))
        dtb = sb("A_dtb", [128, 16], F32)
        na = sb("A_na", [128, 16], F32)
        dsk = sb("A_dsk", [128, 16], F32)
        snw = sb("A_snw", [128, 1024], F32)
        state = sb("A_state", [128, 2, 512], F32)
        stateb = sb("A_stateb", [128, 2, 512], BF16)
        xcT = [sb("A_xcT%d" % i, [128, 12, 128], BF16) for i in range(2)]
        zs = [sb("A_zs%d" % i, [128, 1024], BF16) for i in range(2)]
        dtr = [sb("A_dtr%d" % i, [128, 16], F32) for i in range(2)]
        xtm = sb("A_xtm", [128, 1024], BF16)
        btm = sb("A_btm", [128, 256], BF16)
        sm = sb("A_sm", [128, 8, 16], F32)
        rhsU = sb("A_rhsU", [128, 16, 128], F32)
        xdt = sb("A_xdt", [128, 1024], BF16)
        xdtd = sb("A_xdtd", [128, 1024], BF16)
        cbm = sb("A_cbm", [128, 2, 128], F32)
        v4 = [sb("A_v4%d" % i, [128, 4, 128], F32) for i in range(2)]
        d4 = [sb("A_d4%d" % i, [128, 4, 128], F32) for i in range(2)]
        mT = sb("A_mT", [128, 16, 128], BF16)
        yb = sb("A_y", [128, 1024], F32)
        t2 = sb("A_t2", [128, 1024], F32)
        sq = sb("A_sq", [128, 1024], F32)
        st = sb("A_st", [128, 8], F32)
        yo = sb("A_yo", [128, 1024], BF16)
        oT = [sb("A_oT%d" % i, [128, 8, 128], BF16) for i in range(2)]
        tst = sb("A_tst", [128, 512], F32)
        pA = ps("A_pA", [128, 512], F32)
        pB = [ps("A_pB%d" % i, [128, 512], F32) for i in range(2)]
        pY = ps("A_pY", [128, 1024], F32)
        pO = [ps("A_pO%d" % i, [128, 512], F32) for i in range(2)]
        pT = ps("A_pT", [128, 1024], BF16)
        d_p = Dep(); d_state = Dep(); d_stateb = Dep(); d_xcT = [Dep(), Dep()]; d_zs = [Dep(), Dep()]; d_dtr = [Dep(), Dep()]
        d_xtm = Dep(); d_btm = Dep(); d_sm = Dep(); d_rhsU = Dep(); d_xdt = Dep(); d_xdtd = Dep(); d_cbm = Dep()
        d_v4 = [Dep(), Dep()]; d_d4 = [Dep(), Dep()]; d_mT = Dep(); d_y = Dep(); d_t2 = Dep(); d_sq = Dep(); d_st = Dep()
        d_yo = Dep(); d_oT = [Dep(), Dep()]; d_tst = Dep()
        d_pA = Dep(); d_pB = [Dep(), Dep()]; d_pY = Dep(); d_pO = [Dep(), Dep()]; d_pT = Dep(); d_obr = Dep()
        kb.dma("sp", dtb[:], dtb_ap.partition_broadcast(128), writes=[d_p], slot=d_p, merge_w=True)
        kb.dma("sp", na[:], alog_ap.partition_broadcast(128), writes=[d_p], slot=d_p, merge_w=True)
        kb.dma("sp", dsk[:], dsk_ap.partition_broadcast(128), writes=[d_p], slot=d_p, merge_w=True)
        kb.dma("sp", snw[:], sn_ap.partition_broadcast(128), writes=[d_p], slot=d_p, merge_w=True)
        kb.op("act", lambda e: e.activation(out=na[:], in_=na[:], func=AF.Exp), reads=[d_p], writes=[d_p])
        kb.op("dve", lambda e: e.memset(state[:], 0.0), writes=[d_state])
        kb.op("dve", lambda e: e.memset(stateb[:], 0.0), writes=[d_stateb])
        pbi = 0
        poi = 0
        for c in range(NBLK):
            b = c % 2
            t0 = c * 128
            kb.dma("sp", xcT[b][:], sx[:, t0:t0 + 128].rearrange("(cb p) t -> p cb t", p=128), reads=[d_sx], writes=[d_xcT[b]])
            kb.dma("sp", zs[b][:], tm[t0:t0 + 128, TM_Z:TM_Z + 1024], writes=[d_zs[b]])
            kb.dma("sp", dtr[b][:], tms[t0:t0 + 128, 0:16], writes=[d_dtr[b]])
            for k in range(8):
                kb.op("pe", lambda e: e.transpose(out=pT[:, k * 128:(k + 1) * 128], in_=xcT[b][:, k, :], identity=C["ident"][:]),
                      reads=[d_xcT[b], C["dep"]], writes=[d_pT], inc=(k == 7))
            kb.op("act", lambda e: e.activation(out=xtm[:], in_=pT[:], func=AF.Copy), reads=[d_pT], writes=[d_xtm])
            for k in range(2):
                kb.op("pe", lambda e: e.transpose(out=pT[:, k * 128:(k + 1) * 128], in_=xcT[b][:, 8 + k, :], identity=C["ident"][:]),
                      reads=[d_xcT[b], C["dep"]], writes=[d_pT], inc=(k == 1))
            kb.op("act", lambda e: e.activation(out=btm[:], in_=pT[:, 0:256], func=AF.Copy), reads=[d_pT], writes=[d_btm])
            kb.op("dve", lambda e: e.tensor_tensor(out=sm[:, 7, :], in0=dtr[b][:], in1=dtb[:], op=ALU.add), reads=[d_dtr[b], d_p], writes=[d_sm])
            kb.op("act", lambda e: e.activation(out=sm[:, 7, :], in_=sm[:, 7, :], func=AF.Exp), reads=[d_sm], writes=[d_sm])
            kb.op("act", lambda e: e.activation(out=sm[:, 0, :], in_=sm[:, 7, :], func=AF.Ln, bias=1.0), reads=[d_sm], writes=[d_sm])
            kb.op("dve", lambda e: e.tensor_tensor(out=sm[:, 1, :], in0=sm[:, 0, :], in1=na[:], op=ALU.mult), reads=[d_sm, d_p], writes=[d_sm])
            kb.op("pe", lambda e: e.matmul(pA[:, 256:272], C["U"][:], sm[:, 1, :], start=True, stop=True), reads=[d_sm, C["dep"]], writes=[d_pA], inc=False)
            kb.op("pe", lambda e: e.matmul(pA[:, 272:288], C["ones"][:], sm[:, 1, :], start=True, stop=True), reads=[d_sm, C["dep"]], writes=[d_pA], inc=False)
            for g in range(2):
                kb.op("pe", lambda e: e.matmul(pA[:, g * 128:(g + 1) * 128], xcT[b][:, 8 + g, :], xcT[b][:, 10 + g, :], start=True, stop=True),
                      reads=[d_xcT[b]], writes=[d_pA], inc=(g == 1))
            kb.op("dve", lambda e: e.tensor_copy(out=sm[:, 2:4, :], in_=pA[:, 256:288].rearrange("p (a h) -> p a h", a=2)), reads=[d_pA], writes=[d_sm])
            kb.op("dve", lambda e: e.tensor_tensor(out=cbm[:], in0=pA[:, 0:256].rearrange("p (g l) -> p g l", g=2),
                                                   in1=C["U"][:].unsqueeze(1).broadcast_to([128, 2, 128]), op=ALU.mult),
                  reads=[d_pA, C["dep"]], writes=[d_cbm])
            kb.op("dve", lambda e: e.tensor_tensor(out=sm[:, 4, :], in0=sm[:, 2, :], in1=sm[:, 3, :], op=ALU.subtract), reads=[d_sm], writes=[d_sm])
            kb.op("act", lambda e: e.activation(out=sm[:, 4, :], in_=sm[:, 4, :], func=AF.Exp), reads=[d_sm], writes=[d_sm])
            kb.op("act", lambda e: e.activation(out=sm[:, 5, :], in_=sm[:, 3, :], func=AF.Exp, scale=-1.0), reads=[d_sm], writes=[d_sm])
            kb.op("act", lambda e: e.activation(out=sm[:, 6, :], in_=sm[:, 2, :], func=AF.Exp, scale=-1.0), reads=[d_sm], writes=[d_sm])
            x3 = xtm[:].rearrange("p (h c) -> p h c", h=16)
            kb.op("dve", lambda e: e.tensor_tensor(out=xdt[:].rearrange("p (h c) -> p h c", h=16), in0=x3,
                                                   in1=sm[:, 0, :].unsqueeze(2).broadcast_to([128, 16, 64]), op=ALU.mult),
                  reads=[d_xtm, d_sm], writes=[d_xdt])
            kb.op("pool", lambda e: e.tensor_tensor(out=xdtd[:].rearrange("p (h c) -> p h c", h=16), in0=xdt[:].rearrange("p (h c) -> p h c", h=16),
                                                    in1=sm[:, 4, :].unsqueeze(2).broadcast_to([128, 16, 64]), op=ALU.mult),
                  reads=[d_xdt, d_sm], writes=[d_xdtd])
            kb.op("dve", lambda e: e.tensor_tensor(out=rhsU[:], in0=sm[:, 1, :].unsqueeze(2).broadcast_to([128, 16, 128]),
                                                   in1=C["U"][:].unsqueeze(1).broadcast_to([128, 16, 128]), op=ALU.mult),
                  reads=[d_sm, C["dep"]], writes=[d_rhsU])
            for q4 in range(4):
                p_ = pbi % 2; pbi += 1
                kb.op("pe", lambda e: e.matmul(pB[p_][:], C["ones"][:], rhsU[:, q4 * 4:(q4 + 1) * 4, :].rearrange("p h l -> p (h l)"),
                                               start=True, stop=True), reads=[d_rhsU, C["dep"]], writes=[d_pB[p_]])
                kb.op("dve", lambda e: e.tensor_tensor(out=v4[p_][:], in0=pB[p_][:].rearrange("p (h l) -> p h l", h=4),
                                                       in1=sm[:, 2, q4 * 4:(q4 + 1) * 4].unsqueeze(2).broadcast_to([128, 4, 128]), op=ALU.subtract),
                      reads=[d_pB[p_], d_sm], writes=[d_v4[p_]])
                kb.op("dve", lambda e: e.tensor_scalar_max(out=v4[p_][:], in0=v4[p_][:], scalar1=0.0), reads=[d_v4[p_]], writes=[d_v4[p_]])
                kb.op("act", lambda e: e.activation(out=d4[p_][:], in_=v4[p_][:], func=AF.Exp, scale=-1.0), reads=[d_v4[p_]], writes=[d_d4[p_]])
                g = q4 // 2
                kb.op("pool", lambda e: e.tensor_tensor(out=mT[:, q4 * 4:(q4 + 1) * 4, :], in0=d4[p_][:],
                                                        in1=cbm[:, g, :].unsqueeze(1).broadcast_to([128, 4, 128]), op=ALU.mult),
                      reads=[d_d4[p_], d_cbm], writes=[d_mT])
            for h in range(16):
                kb.op("pe", lambda e: e.matmul(pY[:, h * 64:(h + 1) * 64], mT[:, h, :], xdt[:, h * 64:(h + 1) * 64], start=True, stop=True),
                      reads=[d_mT, d_xdt], writes=[d_pY], inc=(h == 15))
            for g in range(2):
                o_ = poi % 2; poi += 1
                kb.op("pe", lambda e: e.matmul(pO[o_][:], xcT[b][:, 10 + g, :], stateb[:, g, :], start=True, stop=True),
                      reads=[d_xcT[b], d_stateb], writes=[d_pO[o_]])
                kb.op("dve", lambda e: e.tensor_tensor(out=t2[:, g * 512:(g + 1) * 512].rearrange("p (h c) -> p h c", h=8),
                                                       in0=pO[o_][:].rearrange("p (h c) -> p h c", h=8),
                                                       in1=sm[:, 6, g * 8:(g + 1) * 8].unsqueeze(2).broadcast_to([128, 8, 64]), op=ALU.mult),
                      reads=[d_pO[o_], d_sm], writes=[d_t2])
            kb.op("dve", lambda e: e.tensor_tensor(out=yb[:], in0=pY[:], in1=t2[:], op=ALU.add), reads=[d_pY, d_t2], writes=[d_y])
            for g in range(2):
                o_ = poi % 2; poi += 1
                kb.op("pe", lambda e: e.matmul(pO[o_][:], btm[:, g * 128:(g + 1) * 128], xdtd[:, g * 512:(g + 1) * 512], start=True, stop=True),
                      reads=[d_btm, d_xdtd], writes=[d_pO[o_]])
                kb.op("dve", lambda e: e.tensor_tensor(out=tst[:].rearrange("p (h c) -> p h c", h=8),
                                                       in0=state[:, g, :].rearrange("p (h c) -> p h c", h=8),
                                                       in1=sm[:, 5, g * 8:(g + 1) * 8].unsqueeze(2).broadcast_to([128, 8, 64]), op=ALU.mult),
                      reads=[d_state, d_sm], writes=[d_tst])
                kb.op("dve", lambda e: e.tensor_tensor(out=state[:, g, :], in0=tst[:], in1=pO[o_][:], op=ALU.add),
                      reads=[d_tst, d_pO[o_]], writes=[d_state])
            kb.op("act", lambda e: e.activation(out=stateb[:], in_=state[:], func=AF.Copy), reads=[d_state], writes=[d_stateb])
            kb.op("pool", lambda e: e.tensor_tensor(out=t2[:].rearrange("p (h c) -> p h c", h=16), in0=x3,
                                                    in1=dsk[:].unsqueeze(2).broadcast_to([128, 16, 64]), op=ALU.mult),
                  reads=[d_xtm, d_p], writes=[d_t2])
            kb.op("dve", lambda e: e.tensor_tensor(out=yb[:], in0=yb[:], in1=t2[:], op=ALU.add), reads=[d_y, d_t2], writes=[d_y])
            kb.op("dve", lambda e: e.tensor_tensor(out=yb[:], in0=yb[:], in1=zs[b][:], op=ALU.mult), reads=[d_y, d_zs[b]], writes=[d_y])
            kb.op("act", lambda e: e.activation(out=sq[:], in_=yb[:], func=AF.Square, scale=float(512 ** -0.5)), reads=[d_y], writes=[d_sq])
            kb.op("dve", lambda e: e.reduce_sum(out=st[:, 0:2], in_=sq[:].rearrange("p (g c) -> p g c", g=2), axis=AX.X), reads=[d_sq], writes=[d_st])
            kb.op("act", lambda e: e.activation(out=st[:, 2:4], in_=st[:, 0:2], func=AF.Sqrt, bias=EPS), reads=[d_st], writes=[d_st])
            kb.op("dve", lambda e: e.reciprocal(out=st[:, 4:6], in_=st[:, 2:4]), reads=[d_st], writes=[d_st])
            for g in range(2):
                kb.op("dve", lambda e: e.scalar_tensor_tensor(out=yo[:, g * 512:(g + 1) * 512], in0=yb[:, g * 512:(g + 1) * 512],
                                                              scalar=st[:, 4 + g:5 + g], in1=snw[:, g * 512:(g + 1) * 512], op0=ALU.mult, op1=ALU.mult),
                      reads=[d_y, d_st, d_p], writes=[d_yo])
            for k in range(8):
                kb.op("pe", lambda e: e.transpose(out=pT[:, k * 128:(k + 1) * 128], in_=yo[:, k * 128:(k + 1) * 128], identity=C["ident"][:]),
                      reads=[d_yo, C["dep"]], writes=[d_pT], inc=(k == 7))
            kb.op("act", lambda e: e.activation(out=oT[b][:], in_=pT[:].rearrange("p (k t) -> p k t", k=8), func=AF.Copy), reads=[d_pT], writes=[d_oT[b]])
            kb.dma("sp", obr[0, :, t0:t0 + 128].rearrange("(k p) t -> p k t", p=128), oT[b][:], reads=[d_oT[b]], writes=[d_obr],
                   slot=d_oT[b], merge_w=True)
        kb.barrier()


GLA_SCALE = float(128 ** -0.5)


def phase_C(kb, S, tm, tms, w2_ap, gb_ap, gn_ap, obr, C, stop=None):
    old_embed = kb.embed
    kb.embed = False
    try:
        _phase_C(kb, S, tm, tms, w2_ap, gb_ap, gn_ap, obr, C, stop)
    finally:
        kb.embed = old_embed


def _phase_C(kb, S, tm, tms, w2_ap, gb_ap, gn_ap, obr, C, stop=None):
    nc = kb.nc
    NBLK = S // 128
    with ExitStack() as es:
        def sb(name, shape, dt):
            return es.enter_context(nc.sbuf_tensor(kb.u(name), shape, dt))

        def ps(name, shape, dt):
            return es.enter_context(nc.psum_tensor(kb.u(name), shape, dt))
        w2a = sb("C_w2a", [33, 512], BF16)
        w2f = sb("C_w2f", [33, 512], F32)
        lrb = sb("C_lrb", [128, 16], BF16)
        qT0 = sb("C_qT0", [128, 4, 128], BF16)
        qT1 = sb("C_qT1", [128, 4, 128], BF16)
        gnw = sb("C_gnw", [128, 256], F32)
        lrT = sb("C_lrT", [33, 128], BF16)
        qr_ = [sb("C_q%d" % i, [128, 512], BF16) for i in range(2)]
        kr_ = [sb("C_k%d" % i, [128, 512], BF16) for i in range(2)]
        vr_ = [sb("C_v%d" % i, [128, 1024], BF16) for i in range(2)]
        gg_ = [sb("C_g%d" % i, [128, 1024], BF16) for i in range(2)]
        lr_ = [sb("C_lr%d" % i, [128, 16], F32) for i in range(2)]
        Gm = sb("C_Gm", [128, 512], F32)
        Bc = sb("C_Bc", [128, 512], F32)
        E1 = sb("C_E1", [128, 512], F32)
        E2 = sb("C_E2", [128, 512], F32)
        E3 = sb("C_E3", [128, 512], F32)
        qt = sb("C_qt", [128, 512], BF16)
        kt = sb("C_kt", [128, 512], BF16)
        kh0 = sb("C_kh0", [128, 512], BF16)
        kh1 = sb("C_kh1", [128, 512], BF16)
        qkT = sb("C_qkT", [128, 8, 128], BF16)
        dec = sb("C_dec", [128, 8], F32)
        attn = sb("C_attn", [128, 4, 128], BF16)
        SA = sb("C_SA", [128, 4, 256], F32)
        SB = sb("C_SB", [128, 4, 256], F32)
        S0b = sb("C_S0b", [128, 4, 256], BF16)
        S1b = sb("C_S1b", [128, 4, 256], BF16)
        osb = sb("C_osb", [128, 1024], F32)
        sq = sb("C_sq", [128, 1024], F32)
        st = sb("C_st", [128, 12], F32)
        on = sb("C_on", [128, 1024], F32)
        yo = sb("C_yo", [128, 1024], BF16)
        oT = [sb("C_oT%d" % i, [128, 8, 128], BF16) for i in range(2)]
        pZ = ps("C_pZ", [128, 512], F32)
        pBc = ps("C_pBc", [128, 512], F32)
        pBl = ps("C_pBl", [128, 512], F32)
        pT = ps("C_pT", [128, 1024], BF16)
        pA = ps("C_pA", [128, 512], F32)
        pO = ps("C_pO", [128, 1024], F32)
        pS = ps("C_pS", [128, 2, 256], F32)
        d_w = Dep(); d_lrT = Dep(); d_q = [Dep(), Dep()]; d_k = [Dep(), Dep()]; d_v = [Dep(), Dep()]; d_g = [Dep(), Dep()]
        d_lr = [Dep(), Dep()]; d_Gm = Dep(); d_Bc = Dep(); d_E1 = Dep(); d_E2 = Dep(); d_E3 = Dep(); d_qt = Dep(); d_kt = Dep()
        d_kh = Dep(); d_qkT = Dep(); d_dec = Dep(); d_attn = [Dep() for _ in range(4)]; d_SA = [Dep() for _ in range(4)]
        d_SB = [Dep() for _ in range(4)]; d_S0b = [Dep() for _ in range(4)]; d_S1b = [Dep() for _ in range(4)]
        d_osb = Dep(); d_sq = Dep(); d_st = Dep(); d_on = Dep(); d_yo = Dep(); d_oT = [Dep(), Dep()]
        d_pZ = Dep(); d_pBc = Dep(); d_pBl = Dep(); d_pT = Dep(); d_pA = [Dep() for _ in range(4)]
        d_pO = [Dep() for _ in range(4)]; d_pS = [Dep(), Dep()]; d_obr = Dep()
        kb.op("dve", lambda e: e.memset(w2f[:], 0.0), writes=[d_w])
        kb.dma("sp", w2f[0:16, :], w2_ap, writes=[d_w], slot=d_w)
        kb.dma("sp", w2f[32:33, :], gb_ap, writes=[d_w], slot=d_w, merge_w=True)
        kb.op("dve", lambda e: e.tensor_copy(out=w2a[:], in_=w2f[:]), reads=[d_w], writes=[d_w])
        d_lrb = Dep(); d_qT0 = Dep(); d_qT1 = Dep()
        kb.op("pool", lambda e: e.memset(kh0[:], 0.0), writes=[d_kh])
        kb.op("pool", lambda e: e.memset(kh1[:], 0.0), writes=[d_kh])
        kb.op("pool", lambda e: e.memset(qT0[:], 0.0), writes=[d_qT0])
        kb.op("pool", lambda e: e.memset(qT1[:], 0.0), writes=[d_qT1])
        kb.dma("sp", gnw[:], gn_ap.partition_broadcast(128), writes=[d_w], slot=d_w, merge_w=True)
        kb.op("dve", lambda e: e.memset(lrT[:], 0.0), writes=[d_lrT])
        kb.op("dve", lambda e: e.memset(lrT[32:33, :], 1.0), reads=[d_lrT], writes=[d_lrT])
        kb.op("dve", lambda e: e.memset(SA[:], 0.0), writes=d_SA)
        kb.op("dve", lambda e: e.memset(S0b[:], 0.0), writes=d_S0b)
        psi = 0
        for blk in range(NBLK):
            b = blk % 2
            t0 = blk * 128
            kb.dma("sp", qr_[b][:], tm[t0:t0 + 128, TM_GQ:TM_GQ + 512], writes=[d_q[b]])
            kb.dma("sp", kr_[b][:], tm[t0:t0 + 128, TM_GK:TM_GK + 512], writes=[d_k[b]])
            kb.dma("sp", vr_[b][:], tm[t0:t0 + 128, TM_GV:TM_GV + 1024], writes=[d_v[b]])
            kb.dma("sp", gg_[b][:], tm[t0:t0 + 128, TM_GG:TM_GG + 1024], writes=[d_g[b]])
            kb.dma("sp", lr_[b][:], tms[t0:t0 + 128, 32:48], writes=[d_lr[b]])
            kb.op("dve", lambda e: e.tensor_copy(out=lrb[:], in_=lr_[b][:]), reads=[d_lr[b]], writes=[d_lrb])
            kb.op("pe", lambda e: e.transpose(out=pT[0:16, 0:128], in_=lrb[:], identity=C["ident"][:]),
                  reads=[d_lrb, C["dep"]], writes=[d_pT])
            kb.op("dve", lambda e: e.tensor_copy(out=lrT[0:16, :], in_=pT[0:16, 0:128]), reads=[d_pT, d_lrT], writes=[d_lrT])
            kb.op("pe", lambda e: e.matmul(pZ[:], lrT[:], w2a[:], start=True, stop=True), reads=[d_lrT, d_w], writes=[d_pZ])
            kb.op("act", lambda e: e.activation(out=Gm[:], in_=pZ[:], func=AF.Exp, scale=-1.0), reads=[d_pZ], writes=[d_Gm])
            kb.op("act", lambda e: e.activation(out=Gm[:], in_=Gm[:], func=AF.Ln, bias=1.0), reads=[d_Gm], writes=[d_Gm])
            if stop == 1:
                kb.barrier()
                return
            kb.op("pe", lambda e: e.matmul(pBc[:], C["U64s"][:], Gm[:], start=True, stop=True), reads=[d_Gm, C["dep"]], writes=[d_pBc])
            kb.op("pe", lambda e: e.matmul(pBl[:], C["BO64s"][:], Gm[:], start=True, stop=True), reads=[d_Gm, C["dep"]], writes=[d_pBl])
            for h in range(4):
                kb.op("pe", lambda e: e.matmul(pZ[:, h * 128:(h + 1) * 128], Gm[:, h * 128:(h + 1) * 128], C["BO64s"][:],
                                               start=True, stop=True), reads=[d_Gm, C["dep"]], writes=[d_pZ], inc=(h == 3))
            kb.op("act", lambda e: e.activation(out=dec[:], in_=pZ[:, 0:512:64], func=AF.Exp, scale=-1.0), reads=[d_pZ], writes=[d_dec])
            if stop == 2:
                kb.barrier()
                return
            kb.op("dve", lambda e: e.tensor_copy(out=Bc[:], in_=pBc[:]), reads=[d_pBc], writes=[d_Bc])
            kb.op("act", lambda e: e.activation(out=E1[:], in_=Bc[:], func=AF.Exp, scale=-1.0), reads=[d_Bc], writes=[d_E1])
            kb.op("act", lambda e: e.activation(out=E2[:], in_=Bc[:], func=AF.Exp), reads=[d_Bc], writes=[d_E2])
            kb.op("dve", lambda e: e.tensor_tensor(out=E3[:], in0=Bc[:], in1=pBl[:], op=ALU.subtract), reads=[d_Bc, d_pBl], writes=[d_E3])
            kb.op("act", lambda e: e.activation(out=E3[:], in_=E3[:], func=AF.Exp), reads=[d_E3], writes=[d_E3])
            kb.op("dve", lambda e: e.scalar_tensor_tensor(out=qt[:], in0=qr_[b][:], scalar=GLA_SCALE, in1=E1[:], op0=ALU.mult, op1=ALU.mult),
                  reads=[d_q[b], d_E1], writes=[d_qt])
            kb.op("pool", lambda e: e.tensor_tensor(out=kt[:], in0=kr_[b][:], in1=E2[:], op=ALU.mult), reads=[d_k[b], d_E2], writes=[d_kt])
            kb.op("dve", lambda e: e.tensor_tensor(out=kh0[0:64, :], in0=kr_[b][0:64, :], in1=E3[0:64, :], op=ALU.mult), reads=[d_k[b], d_E3], writes=[d_kh])
            kb.op("dve", lambda e: e.tensor_tensor(out=kh1[64:128, :], in0=kr_[b][64:128, :], in1=E3[64:128, :], op=ALU.mult), reads=[d_k[b], d_E3, d_kh], writes=[d_kh])
            if stop == 3:
                kb.barrier()
                return
            for h in range(4):
                kb.op("pe", lambda e: e.transpose(out=pT[:, h * 128:(h + 1) * 128], in_=qt[:, h * 128:(h + 1) * 128], identity=C["ident"][:]),
                      reads=[d_qt, C["dep"]], writes=[d_pT], inc=False)
            for h in range(4):
                kb.op("pe", lambda e: e.transpose(out=pT[:, (4 + h) * 128:(5 + h) * 128], in_=kt[:, h * 128:(h + 1) * 128], identity=C["ident"][:]),
                      reads=[d_kt, C["dep"]], writes=[d_pT], inc=(h == 3))
            if stop == 31:
                kb.barrier()
                return
            kb.op("act", lambda e: e.activation(out=qkT[:], in_=pT[:].rearrange("p (k t) -> p k t", k=8), func=AF.Copy), reads=[d_pT], writes=[d_qkT])
            if stop == 32:
                kb.barrier()
                return
            kb.op("act", lambda e: e.activation(out=qT0[:, :, 0:64], in_=pT[:, 0:512].rearrange("p (k t) -> p k t", k=4)[:, :, 0:64], func=AF.Copy),
                  reads=[d_pT], writes=[d_qT0])
            kb.op("act", lambda e: e.activation(out=qT1[:, :, 64:128], in_=pT[:, 0:512].rearrange("p (k t) -> p k t", k=4)[:, :, 64:128], func=AF.Copy),
                  reads=[d_pT], writes=[d_qT1])
            if stop == 4:
                kb.barrier()
                return
            for h in range(4):
                kb.op("pe", lambda e: e.matmul(pA[:, h * 128:(h + 1) * 128], qkT[:, 4 + h, :], qkT[:, h, :], start=True, stop=True),
                      reads=[d_qkT], writes=[d_pA[h]])
                kb.op("dve", lambda e: e.tensor_tensor(out=attn[:, h, :], in0=pA[:, h * 128:(h + 1) * 128], in1=C["U64"][:], op=ALU.mult),
                      reads=[d_pA[h], C["dep"]], writes=[d_attn[h]])
                if stop == 41:
                    kb.barrier()
                    return
                s0 = psi % 2; psi += 1
                kb.op("pe", lambda e: e.matmul(pS[:, s0, :], kh0[:, h * 128:(h + 1) * 128], vr_[b][:, h * 256:(h + 1) * 256], start=True, stop=True),
                      reads=[d_kh, d_v[b]], writes=[d_pS[s0]])
                kb.op("dve", lambda e: e.scalar_tensor_tensor(out=SB[:, h, :], in0=SA[:, h, :], scalar=dec[:, 2 * h:2 * h + 1], in1=pS[:, s0, :],
                                                              op0=ALU.mult, op1=ALU.add),
                      reads=[d_SA[h], d_dec, d_pS[s0]], writes=[d_SB[h]])
                kb.op("act", lambda e: e.activation(out=S1b[:, h, :], in_=SB[:, h, :], func=AF.Copy), reads=[d_SB[h]], writes=[d_S1b[h]])
                if stop == 42:
                    kb.barrier()
                    return
                s1 = psi % 2; psi += 1
                kb.op("pe", lambda e: e.matmul(pS[:, s1, :], kh1[:, h * 128:(h + 1) * 128], vr_[b][:, h * 256:(h + 1) * 256], start=True, stop=True),
                      reads=[d_kh, d_v[b]], writes=[d_pS[s1]])
                if stop == 43:
                    kb.barrier()
                    return
                kb.op("pe", lambda e: e.matmul(pO[:, h * 256:(h + 1) * 256], attn[:, h, :], vr_[b][:, h * 256:(h + 1) * 256], start=True, stop=False),
                      reads=[d_attn[h], d_v[b]], writes=[d_pO[h]], inc=False)
                kb.op("pe", lambda e: e.matmul(pO[:, h * 256:(h + 1) * 256], qT0[:, h, :], S0b[:, h, :], start=False, stop=False),
                      reads=[d_qT0, d_S0b[h]], writes=[d_pO[h]], inc=False)
                kb.op("pe", lambda e: e.matmul(pO[:, h * 256:(h + 1) * 256], qT1[:, h, :], S1b[:, h, :], start=False, stop=True),
                      reads=[d_qT1, d_S1b[h]], writes=[d_pO[h]])
                if stop == 44:
                    kb.barrier()
                    return
                kb.op("dve", lambda e: e.scalar_tensor_tensor(out=SA[:, h, :], in0=SB[:, h, :], scalar=dec[:, 2 * h + 1:2 * h + 2], in1=pS[:, s1, :],
                                                              op0=ALU.mult, op1=ALU.add),
                      reads=[d_SB[h], d_dec, d_pS[s1]], writes=[d_SA[h]])
                kb.op("act", lambda e: e.activation(out=S0b[:, h, :], in_=SA[:, h, :], func=AF.Copy), reads=[d_SA[h]], writes=[d_S0b[h]])
            if stop == 45:
                kb.barrier()
                return
            kb.op("act", lambda e: e.activation(out=osb[:], in_=pO[:], func=AF.Copy), reads=d_pO, writes=[d_osb])
            if stop == 5:
                kb.barrier()
                return
            kb.op("act", lambda e: e.activation(out=sq[:], in_=osb[:], func=AF.Square, scale=float(256 ** -0.5)), reads=[d_osb], writes=[d_sq])
            kb.op("dve", lambda e: e.reduce_sum(out=st[:, 0:4], in_=sq[:].rearrange("p (h c) -> p h c", h=4), axis=AX.X), reads=[d_sq], writes=[d_st])
            kb.op("act", lambda e: e.activation(out=st[:, 4:8], in_=st[:, 0:4], func=AF.Sqrt, bias=EPS), reads=[d_st], writes=[d_st])
            kb.op("dve", lambda e: e.reciprocal(out=st[:, 8:12], in_=st[:, 4:8]), reads=[d_st], writes=[d_st])
            for h in range(4):
                kb.op("dve", lambda e: e.scalar_tensor_tensor(out=on[:, h * 256:(h + 1) * 256], in0=osb[:, h * 256:(h + 1) * 256],
                                                              scalar=st[:, 8 + h:9 + h], in1=gnw[:], op0=ALU.mult, op1=ALU.mult),
                      reads=[d_osb, d_st, d_w], writes=[d_on])
            kb.op("pool", lambda e: e.tensor_tensor(out=yo[:], in0=on[:], in1=gg_[b][:], op=ALU.mult), reads=[d_on, d_g[b]], writes=[d_yo])
            for k in range(8):
                kb.op("pe", lambda e: e.transpose(out=pT[:, k * 128:(k + 1) * 128], in_=yo[:, k * 128:(k + 1) * 128], identity=C["ident"][:]),
                      reads=[d_yo, C["dep"]], writes=[d_pT], inc=(k == 7))
            kb.op("act", lambda e: e.activation(out=oT[b][:], in_=pT[:].rearrange("p (k t) -> p k t", k=8), func=AF.Copy), reads=[d_pT], writes=[d_oT[b]])
            kb.dma("sp", obr[2, :, t0:t0 + 128].rearrange("(k p) t -> p k t", p=128), oT[b][:], reads=[d_oT[b]], writes=[d_obr],
                   slot=d_oT[b], merge_w=True)
        kb.barrier()


PER_LAYER = [
    ("pre", [1, D]), ("post", [1, D]), ("wtm", [D, NTM]), ("wsm", [D, 128]), ("wfm", [D, NFM]),
    ("cw", [128, 12, 4]), ("cb", [128, 12]), ("dtb", [1, 16]), ("alog", [1, 16]), ("dsk", [1, 16]), ("sn", [1, 1024]),
    ("fgb", [1, 16]), ("w2", [16, 512]), ("gb", [1, 512]), ("gn", [1, 256]), ("qn", [1, 512]), ("kvn", [1, 512]),
    ("wuq", [512, 1536]), ("wukv", [512, 2048]), ("wbr", [4, 1024, D]), ("wout", [D, D]),
]


def build_program(S, L, debug=False):
    nc = bass.Bass("TRN2", target_bir_lowering=False)
    I32 = mybir.dt.int32
    x = nc.dram_tensor("x", [S, D], F32, kind="ExternalInput").ap()
    pos = nc.dram_tensor("pos", [128, S // 128], I32, kind="ExternalInput").ap()
    cst = nc.dram_tensor("cst", [6, 128, 128], F32, kind="ExternalInput").ap()
    invf = nc.dram_tensor("invf", [1, 32], F32, kind="ExternalInput").ap()
    W = {}
    for name, shape in PER_LAYER:
        W[name] = nc.dram_tensor(name, [L] + shape, F32, kind="ExternalInput").ap()
    out = nc.dram_tensor("out", [S, D], F32, kind="ExternalOutput").ap()
    knd = "ExternalOutput" if debug else "Internal"
    tm = nc.dram_tensor("s_tm", [S, NTM], BF16, kind=knd).ap()
    tms = nc.dram_tensor("s_tms", [S, 128], F32, kind=knd).ap()
    fm = nc.dram_tensor("s_fm", [NFM, S], BF16, kind=knd).ap()
    obr = nc.dram_tensor("s_obr", [4, 1024, S], BF16, kind=knd).ap()
    sx = nc.dram_tensor("s_sx", [1536, S], BF16, kind="Internal").ap()
    mq = nc.dram_tensor("s_mq", [1536, S], BF16, kind="Internal").ap()
    mk = nc.dram_tensor("s_mk", [1088, S], BF16, kind="Internal").ap()
    mv = nc.dram_tensor("s_mv", [S, 1024], BF16, kind="Internal").ap()
    xmid = [nc.dram_tensor("s_xmid%d" % i, [S, D], F32, kind="Internal").ap() for i in range(max(L - 1, 1))]
    wbr16 = nc.dram_tensor("s_wbr16", [4, 1024, D], BF16, kind="Internal").ap()
    wout16 = nc.dram_tensor("s_wout16", [D, D], BF16, kind="Internal").ap()
    kb = KB(nc)
    with ExitStack() as es:
        C = load_consts(kb, es, cst)
        rope = setup_rope(kb, es, S, pos, invf)
        cur = x
        d_w16 = Dep()
        for l in range(L):
            dst = out if l == L - 1 else xmid[l]
            d_cast = Dep()
            for br in range(4):
                kb.dma("pool", wbr16[br], W["wbr"][l][br], writes=[d_w16], slot=d_cast, merge_w=(br > 0))
            kb.dma("pool", wout16, W["wout"][l], writes=[d_w16], slot=d_cast, merge_w=True)
            phase_P(kb, S, cur, W["pre"][l], W["wtm"][l], W["wsm"][l], W["wfm"][l], tm, tms, fm, (C["ident"], C["dep"]))
            phase_A(kb, S, fm, tm, tms, W["cw"][l], W["cb"][l], W["dtb"][l], W["alog"][l], W["dsk"][l], W["sn"][l], sx, obr, C)
            phase_B(kb, S, fm, tm, tms, W["fgb"][l], obr, C)
            phase_C(kb, S, tm, tms, W["w2"][l], W["gb"][l], W["gn"][l], obr, C)
            phase_D(kb, S, tm, tms, W["qn"][l], W["kvn"][l], W["wuq"][l], W["wukv"][l], mq, mk, mv, obr, C, rope)
            phase_T(kb, S, obr, fm, cur, wbr16, wout16, W["post"][l], dst, C, w16dep=d_w16)
            cur = dst
        kb.barrier()
    return nc, kb


def host_prepare(inputs, L):
    g = lambda k: np.asarray(inputs[k], dtype=np.float32)
    Wd = {n: [] for n, _ in PER_LAYER}
    for l in range(L):
        wtm, wsm, wfm = host_layout_win(g("w_in")[l])
        cw, cb = host_layout_ssd(g("conv_w")[l], g("conv_b")[l])
        wuq, wukv = host_layout_mla(g("w_uq")[l], g("w_ukv")[l])
        vals = dict(pre=g("pre_norm")[l].reshape(1, D), post=g("post_norm")[l].reshape(1, D), wtm=wtm, wsm=wsm, wfm=wfm, cw=cw, cb=cb,
                    dtb=g("dt_bias")[l].reshape(1, 16), alog=g("a_log")[l].reshape(1, 16), dsk=g("d_skip")[l].reshape(1, 16),
                    sn=g("ssm_norm")[l].reshape(1, 1024), fgb=g("fgate_b")[l].reshape(1, 16), w2=g("gla_w2")[l],
                    gb=g("gla_b")[l].reshape(1, 512), gn=g("gla_norm")[l].reshape(1, 256), qn=g("q_norm")[l].reshape(1, 512),
                    kvn=g("kv_norm")[l].reshape(1, 512), wuq=wuq, wukv=wukv, wbr=g("w_branch")[l], wout=g("w_out")[l])
        for n, _ in PER_LAYER:
            Wd[n].append(np.ascontiguousarray(vals[n], dtype=np.float32))
    return {n: np.ascontiguousarray(np.stack(v)) for n, v in Wd.items()}


_CACHE = {}


def kernel(**inputs):
    x = np.asarray(inputs["x"], dtype=np.float32)
    positions = np.asarray(inputs["positions"], dtype=np.int32)
    B, S, _ = x.shape
    L = np.asarray(inputs["w_in"]).shape[0]
    key = (S, L)
    if key not in _CACHE:
        _CACHE[key] = build_program(S, L)[0]
    nc = _CACHE[key]
    Wd = host_prepare(inputs, L)
    cst = host_consts()
    invf = host_invf()
    n_cores = 8
    in_maps = []
    for c in range(n_cores):
        b = c % B
        m = {"x": np.ascontiguousarray(x[b]), "pos": np.ascontiguousarray(positions[b].reshape(S // 128, 128).T),
             "cst": cst, "invf": invf}
        m.update(Wd)
        in_maps.append(m)
    res = run_bass_kernel_spmd(nc, in_maps, core_ids=list(range(n_cores)))
    outs = [np.asarray(res.results[b]["out"], dtype=np.float32) for b in range(B)]
    return np.stack(outs, axis=0)
```

```python
import numpy as np
from contextlib import ExitStack
import concourse.bass as bass
import concourse.mybir as mybir
from concourse.bass_utils import run_bass_kernel_spmd

F32 = mybir.dt.float32
BF16 = mybir.dt.bfloat16
AF = mybir.ActivationFunctionType
ALU = mybir.AluOpType
AX = mybir.AxisListType

D = 2048
N_IN = 20080
EPS = 1e-6


class Dep:
    __slots__ = ("w", "r", "sem", "cnt", "name", "dead", "key")

    def __init__(self, name=""):
        self.w = {}
        self.r = {}
        self.sem = None
        self.cnt = 0
        self.name = name
        self.dead = False
        self.key = None


class StopEmit(Exception):
    pass


class KB:
    def __init__(self, nc):
        self.nc = nc
        self.E = dict(pe=nc.tensor, dve=nc.vector, act=nc.scalar, pool=nc.gpsimd, sp=nc.sync)
        self.sem = {}
        self.cnt = {}
        for e in ("pe", "dve", "act", "pool"):
            self.sem[e] = nc.alloc_semaphore("sem_" + e)
            self.cnt[e] = 0
        self.seen = {e: {} for e in self.E}
        self.slots = []
        self.ninstr = 0
        self.nops = 0
        self.limit = None
        self.uid = 0
        self.nslot = 0
        self.free_sems = []
        import os
        self.embed = os.environ.get('KB_EMBED', '1') == '1'

    def u(self, name):
        self.uid += 1
        return "%s_%d" % (name, self.uid)

    def _need(self, e, reads, writes):
        need = {}

        def add(evs, skip_same):
            for k, (sem, val, slot) in evs.items():
                if k == e and (skip_same or e == "pe"):
                    continue
                if slot is not None and not slot.dead:
                    val = slot.cnt
                if need.get(k, (None, 0))[1] < val:
                    need[k] = (sem, val)

        for d in reads:
            add(d.w, False)
        for d in writes:
            add(d.w, False)
            add(d.r, True)
        return need

    def _waits(self, e, need, defer=False):
        todo = [(k, sem, val) for k, (sem, val) in need.items() if self.seen[e].get(k, 0) < val]
        last = None
        if defer and todo and self.embed:
            last = todo.pop()
        for k, sem, val in todo:
            self.E[e].wait_ge(sem, val)
            self.seen[e][k] = val
            self.ninstr += 1
        if last is not None:
            self.seen[e][last[0]] = last[2]
        return last

    def op(self, e, fn, reads=(), writes=(), inc=True):
        if self.limit is not None and self.nops >= self.limit and inc and e != "pe":
            raise StopEmit()
        self.nops += 1
        last = self._waits(e, self._need(e, reads, writes), defer=True)
        ins = fn(self.E[e])
        if last is not None:
            ins._wait_ge(last[1], last[2])
        self.ninstr += 1
        if inc:
            self.cnt[e] += 1
            ins.then_inc(self.sem[e], 1)
            val = self.cnt[e]
        else:
            val = self.cnt[e] + 1
        ev = (self.sem[e], val, None)
        for d in reads:
            d.r[e] = ev
        for d in writes:
            d.w = {e: ev}
            d.r = {}
        return ins

    def dma(self, q, out, in_, reads=(), writes=(), slot=None, merge_w=False):
        if slot is None:
            slot = writes[0] if writes else reads[0]
        need = self._need(q, reads, () if merge_w else writes)
        if merge_w:
            for d in writes:
                for k, (sem, val, sl) in d.r.items():
                    if sl is not None and not sl.dead:
                        val = sl.cnt
                    if need.get(k, (None, 0))[1] < val:
                        need[k] = (sem, val)
        last = self._waits(q, need, defer=True)
        assert not slot.dead
        if slot.sem is None:
            self.nslot += 1
            slot.key = "dma%d" % self.nslot
            if self.free_sems:
                slot.sem, slot.cnt = self.free_sems.pop()
            else:
                slot.sem = self.nc.alloc_semaphore("dsem%d" % self.nslot)
            self.slots.append(slot)
        ins = self.E[q].dma_start(out=out, in_=in_)
        if last is not None:
            ins._wait_ge(last[1], last[2])
        self.ninstr += 1
        slot.cnt += 16
        ins.then_inc(slot.sem, 16)
        key = slot.key
        ev = (slot.sem, slot.cnt, slot)
        for d in reads:
            d.r[key] = ev
        for d in writes:
            if merge_w:
                d.w[key] = ev
            else:
                d.w = {key: ev}
                d.r = {}
        return ins

    def barrier(self, engines=("pe", "dve", "act", "pool", "sp")):
        need = {}
        for e in ("pe", "dve", "act", "pool"):
            if self.cnt[e]:
                need[e] = (self.sem[e], self.cnt[e])
        for s in self.slots:
            need[s.key] = (s.sem, s.cnt)
        for e in engines:
            n2 = {k: v for k, v in need.items() if k != e}
            self._waits(e, n2)
        if len(engines) == 5:
            for s in self.slots:
                s.dead = True
                self.free_sems.append((s.sem, s.cnt))
            self.slots = []


def _f32(a):
    return np.ascontiguousarray(np.asarray(a, dtype=np.float32))


C_MZ, C_XBC, C_DT = 0, 1024, 2560
C_FQ, C_FK, C_FV, C_FF, C_FG = 2576, 3600, 4624, 5648, 5664
C_GQ, C_GK, C_GV, C_GLR, C_GG = 6688, 7200, 7712, 8736, 8752
C_LCQ, C_LCKV, C_LKR, C_LG = 9776, 10288, 10800, 10864
C_MERGE = 11888

TM_BLOCKS = [
    (C_MZ, True), (C_MZ + 512, True),
    (C_FV, False), (C_FV + 512, False),
    (C_FG, True), (C_FG + 512, True),
    (C_GQ, False), (C_GK, False),
    (C_GV, False), (C_GV + 512, False),
    (C_GG, True), (C_GG + 512, True),
    (C_LCQ, False), (C_LCKV, False),
    (C_LG, True), (C_LG + 512, True),
]
TM_Z, TM_FV, TM_FG, TM_GQ, TM_GK, TM_GV, TM_GG, TM_LCQ, TM_LCKV, TM_LG = (
    0, 1024, 2048, 3072, 3584, 4096, 5120, 6144, 6656, 7168)
NTM = len(TM_BLOCKS) * 512
FM_XBC, FM_FQ, FM_FK, FM_MERGE = 0, 1536, 2560, 3584
NFM = 11776


def host_layout_win(w_in_l):
    w = w_in_l
    tm = np.concatenate([w[:, c:c + 512] for c, _ in TM_BLOCKS], axis=1)
    small = np.zeros((D, 128), np.float32)
    small[:, 0:16] = w[:, C_DT:C_DT + 16]
    small[:, 16:32] = w[:, C_FF:C_FF + 16]
    small[:, 32:48] = w[:, C_GLR:C_GLR + 16]
    small[:, 48:112] = w[:, C_LKR:C_LKR + 64]
    fm = np.concatenate([w[:, C_XBC:C_XBC + 1536], w[:, C_FQ:C_FQ + 1024],
                         w[:, C_FK:C_FK + 1024], w[:, C_MERGE:C_MERGE + 8192]], axis=1)
    return np.ascontiguousarray(tm), small, np.ascontiguousarray(fm)


def phase_P(kb, S, x_ap, gamma_ap, wtm_ap, wsm_ap, wfm_ap, tm, tms, fm, ident):
    nc = kb.nc
    TT = 1024 if S >= 1024 else S
    NT = S // TT
    NB = TT // 128
    NH = TT // 512
    with ExitStack() as es:
        def sb(name, shape, dt):
            return es.enter_context(nc.sbuf_tensor(kb.u(name), shape, dt))

        def ps(name, shape, dt):
            return es.enter_context(nc.psum_tensor(kb.u(name), shape, dt))

        gam = sb("P_gam", [128, D], F32)
        xt = [sb("P_xt%d" % i, [128, D], F32) for i in range(2)]
        sq = sb("P_sq", [128, D], F32)
        st = [sb("P_st%d" % i, [128, 4], F32) for i in range(2)]
        hb = [sb("P_h%d" % i, [128, D], BF16) for i in range(2)]
        hT = sb("P_hT", [128, 16, TT], BF16)
        wt = [sb("P_wt%d" % i, [128, 16, 512], BF16) for i in range(2)]
        wsmall = sb("P_wsm", [128, 16, 128], BF16)
        stg = [sb("P_stg%d" % i, [128, NB, 512], BF16) for i in range(2)]
        stgs = sb("P_stgs", [128, NB, 128], F32)
        stgF = [sb("P_stgF%d" % i, [128, 4, TT], BF16) for i in range(2)]
        pT = [ps("P_pT%d" % i, [128, D], BF16) for i in range(1)]
        pm = [ps("P_pm%d" % i, [128, 512], F32) for i in range(4)]

        d_gam = Dep(); d_xt = [Dep(), Dep()]; d_sq = Dep(); d_st = [Dep(), Dep()]
        d_hb = [Dep(), Dep()]; d_hT = Dep(); d_wt = [Dep(), Dep()]; d_ws = Dep()
        d_stg = [Dep(), Dep()]; d_stgs = Dep(); d_stgF = [Dep(), Dep()]
        d_pT = [Dep()]; d_pm = [Dep() for _ in range(4)]
        d_tm = Dep(); d_tms = Dep(); d_fm = Dep()

        kb.dma("sp", gam[:], gamma_ap.partition_broadcast(128), writes=[d_gam])
        pmi = 0
        wti = 0
        sgi = 0
        sfi = 0
        wtm_v = wtm_ap.rearrange("(kc p) n -> p kc n", p=128)
        wsm_v = wsm_ap.rearrange("(kc p) n -> p kc n", p=128)
        wfm_v = wfm_ap.rearrange("(kc p) n -> p kc n", p=128)
        for tt in range(NT):
            t0 = tt * TT
            for tb in range(NB):
                b = tb % 2
                kb.dma("sp", xt[b][:], x_ap[t0 + tb * 128:t0 + (tb + 1) * 128, :], writes=[d_xt[b]])
                kb.op("act", lambda e: e.activation(out=sq[:], in_=xt[b][:], func=AF.Square, scale=float(D ** -0.5)),
                      reads=[d_xt[b]], writes=[d_sq])
                kb.op("dve", lambda e: e.reduce_sum(out=st[b][:, 0:1], in_=sq[:], axis=AX.X),
                      reads=[d_sq], writes=[d_st[b]])
                kb.op("act", lambda e: e.activation(out=st[b][:, 1:2], in_=st[b][:, 0:1], func=AF.Sqrt, bias=EPS),
                      reads=[d_st[b]], writes=[d_st[b]])
                kb.op("dve", lambda e: e.reciprocal(out=st[b][:, 2:3], in_=st[b][:, 1:2]),
                      reads=[d_st[b]], writes=[d_st[b]])
                kb.op("dve", lambda e: e.scalar_tensor_tensor(out=hb[b][:], in0=xt[b][:], scalar=st[b][:, 2:3],
                                                              in1=gam[:], op0=ALU.mult, op1=ALU.mult),
                      reads=[d_xt[b], d_st[b], d_gam], writes=[d_hb[b]])
                for kc in range(16):
                    kb.op("pe", lambda e: e.transpose(out=pT[0][:, kc * 128:(kc + 1) * 128],
                                                      in_=hb[b][:, kc * 128:(kc + 1) * 128], identity=ident[0][:]),
                          reads=[d_hb[b], ident[1]], writes=[d_pT[0]], inc=(kc == 15))
                kb.op("act", lambda e: e.activation(out=hT[:, :, tb * 128:(tb + 1) * 128],
                                                    in_=pT[0][:].rearrange("p (k t) -> p k t", k=16), func=AF.Copy),
                      reads=[d_pT[0]], writes=[d_hT])

            for blk in range(len(TM_BLOCKS) + 1):
                small = blk == len(TM_BLOCKS)
                if small:
                    kb.dma("pool", wsmall[:], wsm_v, writes=[d_ws])
                    w_t, d_w, ncol = wsmall, d_ws, 128
                else:
                    wb = wti % 2; wti += 1
                    kb.dma("pool", wt[wb][:], wtm_v[:, :, blk * 512:(blk + 1) * 512], writes=[d_wt[wb]])
                    w_t, d_w, ncol = wt[wb], d_wt[wb], 512
                    sg = sgi % 2; sgi += 1
                for tb in range(NB):
                    pi = pmi % 4; pmi += 1
                    for kc in range(16):
                        kb.op("pe", lambda e: e.matmul(pm[pi][:, 0:ncol], hT[:, kc, tb * 128:(tb + 1) * 128],
                                                       w_t[:, kc, :], start=(kc == 0), stop=(kc == 15)),
                              reads=[d_hT, d_w], writes=[d_pm[pi]], inc=(kc == 15))
                    if small:
                        kb.op("dve", lambda e: e.tensor_copy(out=stgs[:, tb, :], in_=pm[pi][:, 0:128]),
                              reads=[d_pm[pi]], writes=[d_stgs])
                    elif TM_BLOCKS[blk][1]:
                        kb.op("act", lambda e: e.activation(out=stg[sg][:, tb, :], in_=pm[pi][:], func=AF.Silu),
                              reads=[d_pm[pi]], writes=[d_stg[sg]])
                    else:
                        kb.op("dve", lambda e: e.tensor_copy(out=stg[sg][:, tb, :], in_=pm[pi][:]),
                              reads=[d_pm[pi]], writes=[d_stg[sg]])
                if small:
                    kb.dma("sp", tms[t0:t0 + TT, :].rearrange("(tb p) n -> p tb n", p=128), stgs[:],
                           reads=[d_stgs], writes=[d_tms], slot=d_stgs, merge_w=True)
                else:
                    kb.dma("sp", tm[t0:t0 + TT, blk * 512:(blk + 1) * 512].rearrange("(tb p) n -> p tb n", p=128),
                           stg[sg][:], reads=[d_stg[sg]], writes=[d_tm], slot=d_stg[sg], merge_w=True)

            for blk in range(NFM // 512):
                wb = wti % 2; wti += 1
                kb.dma("pool", wt[wb][:], wfm_v[:, :, blk * 512:(blk + 1) * 512], writes=[d_wt[wb]])
                sf = sfi % 2; sfi += 1
                r0 = blk * 512
                for cb in range(4):
                    for th in range(NH):
                        pi = pmi % 4; pmi += 1
                        for kc in range(16):
                            kb.op("pe", lambda e: e.matmul(pm[pi][:], wt[wb][:, kc, cb * 128:(cb + 1) * 128],
                                                           hT[:, kc, th * 512:(th + 1) * 512],
                                                           start=(kc == 0), stop=(kc == 15)),
                                  reads=[d_hT, d_wt[wb]], writes=[d_pm[pi]], inc=(kc == 15))
                        dst = stgF[sf][:, cb, th * 512:(th + 1) * 512]
                        if r0 >= FM_MERGE:
                            kb.op("act", lambda e: e.activation(out=dst, in_=pm[pi][:], func=AF.Sigmoid),
                                  reads=[d_pm[pi]], writes=[d_stgF[sf]])
                        elif FM_FQ <= r0 < FM_FK:
                            kb.op("dve", lambda e: e.tensor_scalar(out=dst, in0=pm[pi][:], scalar1=0.125,
                                                                   scalar2=None, op0=ALU.mult),
                                  reads=[d_pm[pi]], writes=[d_stgF[sf]])
                        else:
                            kb.op("dve", lambda e: e.tensor_copy(out=dst, in_=pm[pi][:]),
                                  reads=[d_pm[pi]], writes=[d_stgF[sf]])
                kb.dma("sp", fm[r0:r0 + 512, t0:t0 + TT].rearrange("(cb p) t -> p cb t", p=128), stgF[sf][:],
                       reads=[d_stgF[sf]], writes=[d_fm], slot=d_stgF[sf], merge_w=True)
        kb.barrier()


def load_consts(kb, es, cst_ap):
    nc = kb.nc
    d = Dep()
    C = {"dep": d}
    for i, (name, dt) in enumerate([("ident", BF16), ("U", F32), ("ones", F32), ("U64", F32), ("U64s", F32), ("BO64s", F32)]):
        t = es.enter_context(nc.sbuf_tensor("c_" + name, [128, 128], dt))
        kb.dma("pool", t[:], cst_ap[i], writes=[d], slot=d, merge_w=True)
        C[name] = t
    t = es.enter_context(nc.sbuf_tensor("c_Ub", [128, 128], BF16))
    kb.dma("pool", t[:], cst_ap[1], writes=[d], slot=d, merge_w=True)
    C["Ub"] = t
    t = es.enter_context(nc.sbuf_tensor("c_identf", [128, 128], F32))
    kb.dma("pool", t[:], cst_ap[0], writes=[d], slot=d, merge_w=True)
    C["identf"] = t
    return C


def host_consts():
    c = np.zeros((6, 128, 128), np.float32)
    c[0] = np.eye(128)
    c[1] = np.triu(np.ones((128, 128)))
    c[2] = 1.0
    u64 = np.triu(np.ones((64, 64)))
    c[3, 0:64, 0:64] = u64
    c[3, 64:128, 64:128] = u64
    c[4] = c[3] / 16.0
    c[5, 0:64, 0:64] = 1.0 / 16.0
    c[5, 64:128, 64:128] = 1.0 / 16.0
    return c


def attention_core(kb, es, S, C, groups, load_head, finish_head, bias_for, dv, tagp):
    nc = kb.nc
    NBLK = S // 128
    pS = [es.enter_context(nc.psum_tensor(kb.u(tagp + "_pS%d" % i), [128, 128], F32)) for i in range(3)]
    pO = [es.enter_context(nc.psum_tensor(kb.u(tagp + "_pO%d" % i), [128, dv + 1], F32)) for i in range(2)]
    pt = [es.enter_context(nc.sbuf_tensor(kb.u(tagp + "_pt%d" % i), [128, 128], BF16)) for i in range(4)]
    d_pS = [Dep() for _ in pS]
    d_pO = [Dep() for _ in pO]
    d_pt = [Dep() for _ in pt]
    LA = 2
    cnt = {"i": 0, "o": 0}
    for grp in groups:
        loaded = {h: load_head(h) for h in grp}
        tasks = []
        for qb in range(NBLK):
            for h in grp:
                for kbk in range(qb + 1):
                    tasks.append((h, qb, kbk))
        info = {}
        n = len(tasks)
        for i in range(n + LA):
            if i < n:
                h, qb, kbk = tasks[i]
                kparts, qparts, vaug_fn, hdeps = loaded[h]
                if kbk == 0:
                    bias_fn, bdeps = bias_for(h, qb)
                    o = cnt["o"] % 2; cnt["o"] += 1
                    info[(h, qb)] = (bias_fn, bdeps, o)
                bias_fn, bdeps, o = info[(h, qb)]
                gi = cnt["i"]; cnt["i"] += 1
                s_ = gi % 3
                t_ = gi % 4
                npart = len(kparts)
                for pi_ in range(npart):
                    kb.op("pe", lambda e: e.matmul(pS[s_][:], kparts[pi_](kbk), qparts[pi_](qb),
                                                   start=(pi_ == 0), stop=(pi_ == npart - 1)),
                          reads=hdeps, writes=[d_pS[s_]], inc=(pi_ == npart - 1))
                if bias_fn is not None:
                    b_ap = bias_fn(kbk)
                    kb.op("act", lambda e: e.activation(out=pt[t_][:], in_=pS[s_][:], func=AF.Exp, bias=b_ap),
                          reads=[d_pS[s_]] + bdeps, writes=[d_pt[t_]])
                else:
                    kb.op("act", lambda e: e.activation(out=pt[t_][:], in_=pS[s_][:], func=AF.Exp),
                          reads=[d_pS[s_]], writes=[d_pt[t_]])
                if kbk == qb:
                    kb.op("pool", lambda e: e.tensor_tensor(out=pt[t_][:], in0=pt[t_][:], in1=C["Ub"][:], op=ALU.mult),
                          reads=[d_pt[t_], C["dep"]], writes=[d_pt[t_]])
                tasks[i] = (h, qb, kbk, t_)
            if i >= LA:
                h, qb, kbk, t_ = tasks[i - LA]
                kparts, qparts, vaug_fn, hdeps = loaded[h]
                o = info[(h, qb)][2]
                kb.op("pe", lambda e: e.matmul(pO[o][:], pt[t_][:], vaug_fn(kbk), start=(kbk == 0), stop=(kbk == qb)),
                      reads=[d_pt[t_]] + hdeps, writes=[d_pO[o]], inc=(kbk == qb))
                if kbk == qb:
                    finish_head(h, qb, pO[o], d_pO[o])


def attention_core_grouped(kb, es, S, C, groups, load_head, finish_head, dv, tagp, GQ=4, bias_for=None):
    nc = kb.nc
    NBLK = S // 128
    GQ = min(GQ, NBLK)
    NG = NBLK // GQ
    W = GQ * 128
    pS = [es.enter_context(nc.psum_tensor(kb.u(tagp + "_gS%d" % i), [128, W], F32)) for i in range(3)]
    pO = [es.enter_context(nc.psum_tensor(kb.u(tagp + "_gO%d" % i), [128, 512], F32)) for i in range(4)]
    pt = [es.enter_context(nc.sbuf_tensor(kb.u(tagp + "_gt%d" % i), [128, W], BF16)) for i in range(3)]
    d_pS = [Dep() for _ in pS]
    d_pO = [Dep() for _ in pO]
    d_pt = [Dep() for _ in pt]
    LA = 2
    cnt = 0
    ocnt = 0
    for grp in groups:
        loaded = {h: load_head(h) for h in grp}
        for G in range(NG):
            for h in grp:
                kparts, qparts, vaug_fn, hdeps = loaded[h]
                npart = len(kparts)
                bias_fn, bdeps = bias_for(h, G) if bias_for is not None else (None, [])
                obase = ocnt % (4 // GQ if GQ < 4 else 1) * GQ
                ocnt += 1
                n = GQ * G + GQ
                rec = {}
                for i in range(n + LA):
                    if i < n:
                        kbk = i
                        j0 = max(kbk - GQ * G, 0)
                        width = (GQ - j0) * 128
                        c0 = (GQ * G + j0) * 128
                        s_ = cnt % 3
                        cnt += 1
                        for pi_ in range(npart):
                            kb.op("pe", lambda e: e.matmul(pS[s_][:, 0:width], kparts[pi_](kbk), qparts[pi_](c0, width),
                                                           start=(pi_ == 0), stop=(pi_ == npart - 1)),
                                  reads=hdeps, writes=[d_pS[s_]], inc=(pi_ == npart - 1))
                        if bias_fn is not None:
                            b_ap = bias_fn(kbk)
                            kb.op("act", lambda e: e.activation(out=pt[s_][:, 0:width], in_=pS[s_][:, 0:width], func=AF.Exp, bias=b_ap),
                                  reads=[d_pS[s_]] + bdeps, writes=[d_pt[s_]])
                        else:
                            kb.op("act", lambda e: e.activation(out=pt[s_][:, 0:width], in_=pS[s_][:, 0:width], func=AF.Exp),
                                  reads=[d_pS[s_]], writes=[d_pt[s_]])
                        if kbk >= GQ * G:
                            kb.op("pool", lambda e: e.tensor_tensor(out=pt[s_][:, 0:128], in0=pt[s_][:, 0:128], in1=C["Ub"][:], op=ALU.mult),
                                  reads=[d_pt[s_], C["dep"]], writes=[d_pt[s_]])
                        rec[i] = (kbk, j0, s_)
                    if i >= LA:
                        kbk, j0, s_ = rec[i - LA]
                        for j in range(j0, GQ):
                            qb = GQ * G + j
                            o = obase + j
                            kb.op("pe", lambda e: e.matmul(pO[o][:, 0:dv + 1], pt[s_][:, (j - j0) * 128:(j - j0 + 1) * 128], vaug_fn(kbk),
                                                           start=(kbk == 0), stop=(kbk == qb)),
                                  reads=[d_pt[s_]] + hdeps, writes=[d_pO[o]])
                            if kbk == qb:
                                finish_head(h, qb, pO[o], d_pO[o])


def phase_B(kb, S, fm, tm, tms, fgb_ap, obr, C):
    nc = kb.nc
    NBLK = S // 128
    with ExitStack() as es:
        def sb(name, shape, dt):
            return es.enter_context(nc.sbuf_tensor(kb.u(name), shape, dt))

        G = sb("B_G", [128, NBLK, 16], F32)
        bt = sb("B_bt", [128, 16], F32)
        Tt = sb("B_Tt", [128, NBLK, 16], F32)
        Gc = sb("B_Gc", [128, NBLK, 16], F32)
        Gend = sb("B_Gend", [128, NBLK, 16], F32)
        d_G = Dep(); d_bt = Dep(); d_Tt = Dep(); d_Gc = Dep(); d_Gend = Dep()
        with ExitStack() as es0:
            psC = es0.enter_context(nc.psum_tensor(kb.u("B_psC"), [128, NBLK * 16], F32))
            psT = es0.enter_context(nc.psum_tensor(kb.u("B_psT"), [128, NBLK * 16], F32))
            d_psC = Dep(); d_psT = Dep()
            kb.dma("sp", G[:], tms[:, 16:32].rearrange("(b p) h -> p b h", p=128), writes=[d_G])
            kb.dma("sp", bt[:], fgb_ap.partition_broadcast(128), writes=[d_bt])
            kb.op("dve", lambda e: e.tensor_tensor(out=G[:], in0=G[:],
                                                   in1=bt[:].unsqueeze(1).broadcast_to([128, NBLK, 16]), op=ALU.add),
                  reads=[d_G, d_bt], writes=[d_G])
            kb.op("act", lambda e: e.activation(out=G[:], in_=G[:], func=AF.Exp, scale=-1.0), reads=[d_G], writes=[d_G])
            kb.op("act", lambda e: e.activation(out=G[:], in_=G[:], func=AF.Ln, bias=1.0), reads=[d_G], writes=[d_G])
            G2 = G[:].rearrange("p b h -> p (b h)")
            kb.op("pe", lambda e: e.matmul(psC[:], C["U"][:], G2, start=True, stop=True),
                  reads=[d_G, C["dep"]], writes=[d_psC])
            kb.op("pe", lambda e: e.matmul(psT[:], C["ones"][:], G2, start=True, stop=True),
                  reads=[d_G, C["dep"]], writes=[d_psT])
            kb.op("dve", lambda e: e.tensor_copy(out=Tt[:].rearrange("p b h -> p (b h)"), in_=psT[:]),
                  reads=[d_psT], writes=[d_Tt])
            kb.op("dve", lambda e: e.tensor_copy(out=Gend[:, 0, :], in_=Tt[:, 0, :]), reads=[d_Tt], writes=[d_Gend])
            for b in range(1, NBLK):
                kb.op("dve", lambda e: e.tensor_tensor(out=Gend[:, b, :], in0=Gend[:, b - 1, :], in1=Tt[:, b, :], op=ALU.add),
                      reads=[d_Tt, d_Gend], writes=[d_Gend])
            kb.op("dve", lambda e: e.tensor_copy(out=Gc[:, 0, :], in_=psC[:, 0:16]), reads=[d_psC], writes=[d_Gc])
            if NBLK > 1:
                kb.op("dve", lambda e: e.tensor_tensor(out=Gc[:, 1:, :], in0=psC[:, 16:].rearrange("p (b h) -> p b h", h=16),
                                                       in1=Gend[:, 0:NBLK - 1, :], op=ALU.add),
                      reads=[d_psC, d_Gend], writes=[d_Gc])
            kb.barrier()

        qT = [sb("B_qT%d" % i, [128, S], BF16) for i in range(2)]
        kT = [sb("B_kT%d" % i, [128, S], BF16) for i in range(2)]
        vr = [sb("B_vr%d" % i, [128, NBLK, 128], BF16) for i in range(2)]
        va = [sb("B_va%d" % i, [128, NBLK, 2, 72], BF16) for i in range(2)]
        gt = [sb("B_gt%d" % i, [128, NBLK, 128], BF16) for i in range(2)]
        obT = [sb("B_obT%d" % i, [128, S], BF16) for i in range(2)]
        ob = [sb("B_ob%d" % i, [128, 128], BF16) for i in range(2)]
        rs = [sb("B_rs%d" % i, [128, 1], F32) for i in range(4)]
        bias = [sb("B_bias%d" % i, [128, NBLK], F32) for i in range(4)]
        pTr = es.enter_context(nc.psum_tensor(kb.u("B_pTr"), [128, 128], BF16))
        d_q = [Dep(), Dep()]; d_k = [Dep(), Dep()]; d_vr = [Dep(), Dep()]; d_va = [Dep(), Dep()]
        d_gt = [Dep(), Dep()]; d_obT = [Dep(), Dep()]; d_ob = [Dep(), Dep()]
        d_rs = [Dep() for _ in rs]; d_bias = [Dep() for _ in bias]; d_pTr = Dep(); d_obr = Dep()
        for i in range(2):
            kb.op("pool", lambda e: e.memset(va[i][:], 1.0), writes=[d_va[i]])
        state = {"bi": 0, "ri": 0, "obi": 0}

        def load_head(h):
            hp, hh = h // 2, h % 2
            b = hp % 2
            if hh == 0:
                kb.dma("sp", qT[b][:], fm[FM_FQ + hp * 128:FM_FQ + (hp + 1) * 128, :], writes=[d_q[b]])
                kb.dma("sp", kT[b][:], fm[FM_FK + hp * 128:FM_FK + (hp + 1) * 128, :], writes=[d_k[b]])
                kb.dma("sp", vr[b][:], tm[:, TM_FV + hp * 128:TM_FV + (hp + 1) * 128].rearrange("(b p) c -> p b c", p=128),
                       writes=[d_vr[b]])
                kb.dma("sp", gt[b][:], tm[:, TM_FG + hp * 128:TM_FG + (hp + 1) * 128].rearrange("(b p) c -> p b c", p=128),
                       writes=[d_gt[b]])
                kb.op("dve", lambda e: e.tensor_copy(out=va[b][:, :, :, 0:64],
                                                     in_=vr[b][:].rearrange("p b (h c) -> p b h c", h=2)),
                      reads=[d_vr[b]], writes=[d_va[b]])
            p0 = hh * 64
            kparts = [lambda blk: kT[b][p0:p0 + 64, blk * 128:(blk + 1) * 128]]
            qparts = [lambda c0, w: qT[b][p0:p0 + 64, c0:c0 + w]]
            return kparts, qparts, (lambda blk: va[b][:, blk, hh, 0:65]), [d_q[b], d_k[b], d_va[b]]

        GQ = min(2, NBLK)

        def bias_for(h, G):
            bi = state["bi"] % 4; state["bi"] += 1
            nk = GQ * G + GQ
            kb.op("dve", lambda e: e.tensor_scalar(out=bias[bi][:, 0:nk], in0=Gc[:, 0:nk, h],
                                                   scalar1=Gend[:, GQ * G, h:h + 1], scalar2=None, op0=ALU.subtract),
                  reads=[d_Gc, d_Gend], writes=[d_bias[bi]])
            return (lambda blk: bias[bi][:, blk:blk + 1]), [d_bias[bi]]

        def finish_head(h, qb, pO, d_pO):
            hp, hh = h // 2, h % 2
            b = hp % 2
            ri = state["ri"] % 4; state["ri"] += 1
            o = qb % 2
            kb.op("dve", lambda e: e.reciprocal(out=rs[ri][:], in_=pO[:, 64:65]), reads=[d_pO], writes=[d_rs[ri]])
            kb.op("dve", lambda e: e.scalar_tensor_tensor(out=ob[o][:, hh * 64:(hh + 1) * 64], in0=pO[:, 0:64],
                                                          scalar=rs[ri][:, 0:1], in1=gt[b][:, qb, hh * 64:(hh + 1) * 64],
                                                          op0=ALU.mult, op1=ALU.mult),
                  reads=[d_pO, d_rs[ri], d_gt[b]], writes=[d_ob[o]])
            if hh == 1:
                kb.op("pe", lambda e: e.transpose(out=pTr[:], in_=ob[o][:], identity=C["ident"][:]),
                      reads=[d_ob[o], C["dep"]], writes=[d_pTr])
                kb.op("dve", lambda e: e.tensor_copy(out=obT[b][:, qb * 128:(qb + 1) * 128], in_=pTr[:]),
                      reads=[d_pTr], writes=[d_obT[b]])
                if qb == NBLK - 1:
                    kb.dma("sp", obr[1, hp * 128:(hp + 1) * 128, :], obT[b][:], reads=[d_obT[b]], writes=[d_obr],
                           slot=d_obT[b], merge_w=True)

        attention_core_grouped(kb, es, S, C, [[2 * i, 2 * i + 1] for i in range(8)], load_head, finish_head, 64, "B", GQ=GQ, bias_for=bias_for)
        kb.barrier()


TWO_PI = 6.283185307179586
CW1 = 6.28125
CW2 = TWO_PI - CW1


def setup_rope(kb, es, S, pos_ap, invf_ap):
    nc = kb.nc
    NBLK = S // 128
    cosT = es.enter_context(nc.sbuf_tensor("R_cos", [128, NBLK, 32], F32))
    sinT = es.enter_context(nc.sbuf_tensor("R_sin", [128, NBLK, 32], F32))
    d_rope = Dep()
    with ExitStack() as es0:
        def sb(name, shape, dt):
            return es0.enter_context(nc.sbuf_tensor(kb.u(name), shape, dt))
        posi = sb("R_posi", [128, NBLK], mybir.dt.int32)
        posf = sb("R_posf", [128, NBLK], F32)
        invf = sb("R_invf", [128, 32], F32)
        ang = sb("R_ang", [128, NBLK, 32], F32)
        tq = sb("R_tq", [128, NBLK, 32], F32)
        ni = sb("R_ni", [128, NBLK, 32], mybir.dt.int32)
        nf = sb("R_nf", [128, NBLK, 32], F32)
        r = sb("R_r", [128, NBLK, 32], F32)
        m = sb("R_m", [128, NBLK, 32], F32)
        d = Dep()
        kb.dma("sp", posi[:], pos_ap, writes=[d])
        kb.dma("sp", invf[:], invf_ap.partition_broadcast(128), writes=[d], slot=d, merge_w=True)
        kb.op("dve", lambda e: e.tensor_copy(out=posf[:], in_=posi[:]), reads=[d], writes=[d])
        kb.op("dve", lambda e: e.tensor_tensor(out=ang[:], in0=posf[:].unsqueeze(2).broadcast_to([128, NBLK, 32]),
                                               in1=invf[:].unsqueeze(1).broadcast_to([128, NBLK, 32]), op=ALU.mult),
              reads=[d], writes=[d])
        for which, dst in ((0, sinT), (1, cosT)):
            if which == 1:
                kb.op("dve", lambda e: e.tensor_scalar_add(out=ang[:], in0=ang[:], scalar1=float(np.pi / 2)),
                      reads=[d], writes=[d])
            kb.op("dve", lambda e: e.tensor_scalar_mul(out=tq[:], in0=ang[:], scalar1=float(1.0 / TWO_PI)), reads=[d], writes=[d])
            kb.op("dve", lambda e: e.tensor_copy(out=ni[:], in_=tq[:]), reads=[d], writes=[d])
            kb.op("dve", lambda e: e.tensor_copy(out=nf[:], in_=ni[:]), reads=[d], writes=[d])
            kb.op("dve", lambda e: e.scalar_tensor_tensor(out=r[:], in0=nf[:], scalar=-CW1, in1=ang[:], op0=ALU.mult, op1=ALU.add),
                  reads=[d], writes=[d])
            kb.op("dve", lambda e: e.scalar_tensor_tensor(out=r[:], in0=nf[:], scalar=-CW2, in1=r[:], op0=ALU.mult, op1=ALU.add),
                  reads=[d], writes=[d])
            kb.op("dve", lambda e: e.tensor_single_scalar(out=m[:], in_=r[:], scalar=float(np.pi), op=ALU.is_gt), reads=[d], writes=[d])
            kb.op("dve", lambda e: e.scalar_tensor_tensor(out=r[:], in0=m[:], scalar=-TWO_PI, in1=r[:], op0=ALU.mult, op1=ALU.add),
                  reads=[d], writes=[d])
            kb.op("dve", lambda e: e.tensor_single_scalar(out=m[:], in_=r[:], scalar=float(-np.pi), op=ALU.is_lt), reads=[d], writes=[d])
            kb.op("dve", lambda e: e.scalar_tensor_tensor(out=r[:], in0=m[:], scalar=TWO_PI, in1=r[:], op0=ALU.mult, op1=ALU.add),
                  reads=[d], writes=[d])
            kb.op("dve", lambda e: e.tensor_scalar(out=r[:], in0=r[:], scalar1=float(np.pi), scalar2=float(-np.pi),
                                                   op0=ALU.min, op1=ALU.max), reads=[d], writes=[d])
            kb.op("act", lambda e: e.activation(out=dst[:], in_=r[:], func=AF.Sin), reads=[d], writes=[d_rope])
        kb.barrier()
    return cosT, sinT, d_rope


def host_invf():
    return (1.0 / (np.float32(10000.0) ** (np.arange(32, dtype=np.float32) * np.float32(2.0) / np.float32(64)))).astype(
        np.float32).reshape(1, 32)


def emit_rmsnorm(kb, src, d_src, n, gam, d_gam, dst, d_dst, sq, d_sq, st, d_st):
    kb.op("act", lambda e: e.activation(out=sq, in_=src, func=AF.Square, scale=float(n ** -0.5)), reads=[d_src], writes=[d_sq])
    kb.op("dve", lambda e: e.reduce_sum(out=st[:, 0:1], in_=sq, axis=AX.X), reads=[d_sq], writes=[d_st])
    kb.op("act", lambda e: e.activation(out=st[:, 1:2], in_=st[:, 0:1], func=AF.Sqrt, bias=EPS), reads=[d_st], writes=[d_st])
    kb.op("dve", lambda e: e.reciprocal(out=st[:, 2:3], in_=st[:, 1:2]), reads=[d_st], writes=[d_st])
    kb.op("dve", lambda e: e.scalar_tensor_tensor(out=dst, in0=src, scalar=st[:, 2:3], in1=gam, op0=ALU.mult, op1=ALU.mult),
          reads=[d_src, d_st, d_gam], writes=[d_dst])


def emit_rotary(kb, src, d_src, H, cos_ap, sin_ap, d_rope, dst, d_dst, tmp, d_tmp):
    s3 = src.rearrange("p (h c) -> p h c", h=H)
    o3 = dst.rearrange("p (h c) -> p h c", h=H)
    t3 = tmp.rearrange("p (h c) -> p h c", h=H)
    cb = cos_ap.unsqueeze(1).broadcast_to([128, H, 32])
    sbc = sin_ap.unsqueeze(1).broadcast_to([128, H, 32])
    kb.op("dve", lambda e: e.tensor_tensor(out=t3[:, :, 0:32], in0=s3[:, :, 0:32], in1=cb, op=ALU.mult),
          reads=[d_src, d_rope], writes=[d_tmp])
    kb.op("dve", lambda e: e.tensor_tensor(out=t3[:, :, 32:64], in0=s3[:, :, 32:64], in1=sbc, op=ALU.mult),
          reads=[d_src, d_rope], writes=[d_tmp])
    kb.op("dve", lambda e: e.tensor_tensor(out=o3[:, :, 0:32], in0=t3[:, :, 0:32], in1=t3[:, :, 32:64], op=ALU.subtract),
          reads=[d_tmp], writes=[d_dst])
    kb.op("dve", lambda e: e.tensor_tensor(out=t3[:, :, 0:32], in0=s3[:, :, 0:32], in1=sbc, op=ALU.mult),
          reads=[d_src, d_rope], writes=[d_tmp])
    kb.op("dve", lambda e: e.tensor_tensor(out=t3[:, :, 32:64], in0=s3[:, :, 32:64], in1=cb, op=ALU.mult),
          reads=[d_src, d_rope], writes=[d_tmp])
    kb.op("dve", lambda e: e.tensor_tensor(out=o3[:, :, 32:64], in0=t3[:, :, 0:32], in1=t3[:, :, 32:64], op=ALU.add),
          reads=[d_tmp], writes=[d_dst])


MLA_SCALE = float(192 ** -0.5)


def phase_D(kb, S, tm, tms, qn_ap, kvn_ap, wuq_ap, wukv_ap, mq, mk, mv, obr, C, rope):
    nc = kb.nc
    NBLK = S // 128
    cosT, sinT, d_rope = rope
    TS = 512 if S >= 512 else S
    NJ = TS // 128
    d_mq = Dep(); d_mk = Dep(); d_mv = Dep()
    with ExitStack() as es:
        def sb(name, shape, dt):
            return es.enter_context(nc.sbuf_tensor(kb.u(name), shape, dt))

        def ps(name, shape, dt):
            return es.enter_context(nc.psum_tensor(kb.u(name), shape, dt))
        wuq = sb("D_wuq", [128, 4, 1536], BF16)
        wukv = sb("D_wukv", [128, 4, 2048], BF16)
        gq = sb("D_gq", [128, 512], F32)
        gkv = sb("D_gkv", [128, 512], F32)
        d_w = Dep()
        kb.dma("pool", wuq[:], wuq_ap.rearrange("(kc p) n -> p kc n", p=128), writes=[d_w], slot=d_w, merge_w=True)
        kb.dma("pool", wukv[:], wukv_ap.rearrange("(kc p) n -> p kc n", p=128), writes=[d_w], slot=d_w, merge_w=True)
        kb.dma("sp", gq[:], qn_ap.partition_broadcast(128), writes=[d_w], slot=d_w, merge_w=True)
        kb.dma("sp", gkv[:], kvn_ap.partition_broadcast(128), writes=[d_w], slot=d_w, merge_w=True)
        cqr = sb("D_cqr", [128, NJ, 512], BF16)
        ckvr = sb("D_ckvr", [128, NJ, 512], BF16)
        krr = sb("D_krr", [128, NJ, 64], F32)
        sq = sb("D_sq", [128, 512], F32)
        st = sb("D_st", [128, 4], F32)
        cn = [sb("D_cn%d" % i, [128, 1024], BF16) for i in range(2)]
        cT = sb("D_cT", [128, 8, TS], BF16)
        stgQ = sb("D_stgQ", [128, 8, TS], BF16)
        stgK = sb("D_stgK", [128, 8, TS], BF16)
        stgV = sb("D_stgV", [128, NJ, 1024], BF16)
        stgQR = sb("D_stgQR", [128, 4, TS], BF16)
        stgKR = sb("D_stgKR", [64, TS], BF16)
        q32 = sb("D_q32", [128, 512], F32)
        qtmp = sb("D_qtmp", [128, 512], F32)
        qrb = sb("D_qrb", [128, 512], BF16)
        ktmp = sb("D_ktmp", [128, 64], F32)
        krb = sb("D_krb", [128, 64], BF16)
        pT = ps("D_pT", [128, 1024], BF16)
        pm = [ps("D_pm%d" % i, [128, 512], F32) for i in range(3)]
        pT2 = ps("D_pT2", [128, 640], BF16)
        d_cqr = Dep(); d_ckvr = Dep(); d_krr = Dep(); d_sq = Dep(); d_st = Dep(); d_cn = [Dep(), Dep()]
        d_cT = Dep(); d_stgQ = Dep(); d_stgK = Dep(); d_stgV = Dep(); d_stgQR = Dep(); d_stgKR = Dep()
        d_q32 = Dep(); d_qtmp = Dep(); d_qrb = Dep(); d_ktmp = Dep(); d_krb = Dep()
        d_pT = Dep(); d_pm = [Dep() for _ in pm]; d_pT2 = Dep()
        pmi = 0
        for ts_ in range(S // TS):
            t0 = ts_ * TS
            kb.dma("sp", cqr[:], tm[t0:t0 + TS, TM_LCQ:TM_LCQ + 512].rearrange("(j p) c -> p j c", p=128), writes=[d_cqr])
            kb.dma("sp", ckvr[:], tm[t0:t0 + TS, TM_LCKV:TM_LCKV + 512].rearrange("(j p) c -> p j c", p=128), writes=[d_ckvr])
            kb.dma("sp", krr[:], tms[t0:t0 + TS, 48:112].rearrange("(j p) c -> p j c", p=128), writes=[d_krr])
            for j in range(NJ):
                c_ = j % 2
                emit_rmsnorm(kb, cqr[:, j, :], d_cqr, 512, gq[:], d_w, cn[c_][:, 0:512], d_cn[c_], sq[:], d_sq, st, d_st)
                emit_rmsnorm(kb, ckvr[:, j, :], d_ckvr, 512, gkv[:], d_w, cn[c_][:, 512:1024], d_cn[c_], sq[:], d_sq, st, d_st)
                for kc in range(8):
                    kb.op("pe", lambda e: e.transpose(out=pT[:, kc * 128:(kc + 1) * 128], in_=cn[c_][:, kc * 128:(kc + 1) * 128],
                                                      identity=C["ident"][:]),
                          reads=[d_cn[c_], C["dep"]], writes=[d_pT], inc=(kc == 7))
                kb.op("dve", lambda e: e.tensor_copy(out=cT[:, :, j * 128:(j + 1) * 128],
                                                     in_=pT[:].rearrange("p (k t) -> p k t", k=8)),
                      reads=[d_pT], writes=[d_cT])
            for h in range(8):
                for (w_t, off, kc0, stg_, d_stg, scale) in ((wuq, 0, 0, stgQ, d_stgQ, MLA_SCALE), (wukv, 0, 4, stgK, d_stgK, 1.0)):
                    pi = pmi % 3; pmi += 1
                    for kc in range(4):
                        kb.op("pe", lambda e: e.matmul(pm[pi][:, 0:TS], w_t[:, kc, off + h * 128:off + (h + 1) * 128],
                                                       cT[:, kc0 + kc, :], start=(kc == 0), stop=(kc == 3)),
                              reads=[d_w, d_cT], writes=[d_pm[pi]], inc=(kc == 3))
                    kb.op("act", lambda e: e.activation(out=stg_[:, h, :], in_=pm[pi][:, 0:TS], func=AF.Copy, scale=scale),
                          reads=[d_pm[pi]], writes=[d_stg])
            kb.dma("sp", mq[0:1024, t0:t0 + TS].rearrange("(h p) t -> p h t", p=128), stgQ[:], reads=[d_stgQ], writes=[d_mq],
                   slot=d_stgQ, merge_w=True)
            kb.dma("sp", mk[0:1024, t0:t0 + TS].rearrange("(h p) t -> p h t", p=128), stgK[:], reads=[d_stgK], writes=[d_mk],
                   slot=d_stgK, merge_w=True)
            for j in range(NJ):
                for half in range(2):
                    pi = pmi % 3; pmi += 1
                    for kc in range(4):
                        kb.op("pe", lambda e: e.matmul(pm[pi][:], cT[:, 4 + kc, j * 128:(j + 1) * 128],
                                                       wukv[:, kc, 1024 + half * 512:1024 + (half + 1) * 512],
                                                       start=(kc == 0), stop=(kc == 3)),
                              reads=[d_w, d_cT], writes=[d_pm[pi]], inc=(kc == 3))
                    kb.op("dve", lambda e: e.tensor_copy(out=stgV[:, j, half * 512:(half + 1) * 512], in_=pm[pi][:]),
                          reads=[d_pm[pi]], writes=[d_stgV])
                blk = t0 // 128 + j
                pi = pmi % 3; pmi += 1
                for kc in range(4):
                    kb.op("pe", lambda e: e.matmul(pm[pi][:], cT[:, kc, j * 128:(j + 1) * 128], wuq[:, kc, 1024:1536],
                                                   start=(kc == 0), stop=(kc == 3)),
                          reads=[d_w, d_cT], writes=[d_pm[pi]], inc=(kc == 3))
                kb.op("act", lambda e: e.activation(out=q32[:], in_=pm[pi][:], func=AF.Copy, scale=MLA_SCALE),
                      reads=[d_pm[pi]], writes=[d_q32])
                emit_rotary(kb, q32[:], d_q32, 8, cosT[:, blk, :], sinT[:, blk, :], d_rope, qrb[:], d_qrb, qtmp[:], d_qtmp)
                emit_rotary(kb, krr[:, j, :], d_krr, 1, cosT[:, blk, :], sinT[:, blk, :], d_rope, krb[:], d_krb, ktmp[:], d_ktmp)
                for c4 in range(4):
                    kb.op("pe", lambda e: e.transpose(out=pT2[:, c4 * 128:(c4 + 1) * 128], in_=qrb[:, c4 * 128:(c4 + 1) * 128],
                                                      identity=C["ident"][:]),
                          reads=[d_qrb, C["dep"]], writes=[d_pT2], inc=False)
                kb.op("pe", lambda e: e.transpose(out=pT2[0:64, 512:640], in_=krb[:], identity=C["ident"][:]),
                      reads=[d_krb, C["dep"]], writes=[d_pT2])
                kb.op("dve", lambda e: e.tensor_copy(out=stgQR[:, :, j * 128:(j + 1) * 128],
                                                     in_=pT2[:, 0:512].rearrange("p (c t) -> p c t", c=4)),
                      reads=[d_pT2], writes=[d_stgQR])
                kb.op("dve", lambda e: e.tensor_copy(out=stgKR[:, j * 128:(j + 1) * 128], in_=pT2[0:64, 512:640]),
                      reads=[d_pT2], writes=[d_stgKR])
            kb.dma("sp", mv[t0:t0 + TS, :].rearrange("(j p) c -> p j c", p=128), stgV[:], reads=[d_stgV], writes=[d_mv],
                   slot=d_stgV, merge_w=True)
            kb.dma("sp", mq[1024:1536, t0:t0 + TS].rearrange("(c p) t -> p c t", p=128), stgQR[:], reads=[d_stgQR], writes=[d_mq],
                   slot=d_stgQR, merge_w=True)
            kb.dma("sp", mk[1024:1088, t0:t0 + TS], stgKR[:], reads=[d_stgKR], writes=[d_mk], slot=d_stgKR, merge_w=True)
        kb.barrier()

    with ExitStack() as es:
        def sb(name, shape, dt):
            return es.enter_context(nc.sbuf_tensor(kb.u(name), shape, dt))
        qn = [sb("D_qn%d" % i, [128, S], BF16) for i in range(2)]
        qr = [sb("D_qr%d" % i, [128, S], BF16) for i in range(2)]
        kn = [sb("D_kn%d" % i, [128, S], BF16) for i in range(2)]
        kr2 = sb("D_kr2", [128, S], BF16)
        vr = [sb("D_vr%d" % i, [128, NBLK, 128], BF16) for i in range(2)]
        va = [sb("D_va%d" % i, [128, NBLK, 136], BF16) for i in range(2)]
        gt = [sb("D_gt%d" % i, [128, NBLK, 128], BF16) for i in range(2)]
        obT = [sb("D_obT%d" % i, [128, S], BF16) for i in range(2)]
        ob = [sb("D_ob%d" % i, [128, 128], BF16) for i in range(2)]
        rs = [sb("D_rs%d" % i, [128, 1], F32) for i in range(4)]
        pTr = es.enter_context(nc.psum_tensor(kb.u("D_pTr"), [128, 128], BF16))
        d_qn = [Dep(), Dep()]; d_qr = [Dep(), Dep()]; d_kn = [Dep(), Dep()]; d_kr2 = Dep()
        d_vr = [Dep(), Dep()]; d_va = [Dep(), Dep()]; d_gt = [Dep(), Dep()]; d_obT = [Dep(), Dep()]
        d_ob = [Dep(), Dep()]; d_rs = [Dep() for _ in rs]; d_pTr = Dep(); d_obr = Dep()
        for i in range(2):
            kb.op("pool", lambda e: e.memset(va[i][:], 1.0), writes=[d_va[i]])
        kb.dma("sp", kr2[0:64, :], mk[1024:1088, :], reads=[d_mk], writes=[d_kr2], slot=d_kr2, merge_w=True)
        kb.dma("sp", kr2[64:128, :], mk[1024:1088, :], reads=[d_mk], writes=[d_kr2], slot=d_kr2, merge_w=True)
        state = {"ri": 0}

        def load_head(h):
            b = h % 2
            hp, hh = h // 2, h % 2
            kb.dma("sp", qn[b][:], mq[h * 128:(h + 1) * 128, :], reads=[d_mq], writes=[d_qn[b]])
            kb.dma("sp", qr[b][:], mq[1024 + hp * 128:1024 + (hp + 1) * 128, :], reads=[d_mq], writes=[d_qr[b]])
            kb.dma("sp", kn[b][:], mk[h * 128:(h + 1) * 128, :], reads=[d_mk], writes=[d_kn[b]])
            kb.dma("sp", vr[b][:], mv[:, h * 128:(h + 1) * 128].rearrange("(b p) c -> p b c", p=128), reads=[d_mv], writes=[d_vr[b]])
            kb.dma("sp", gt[b][:], tm[:, TM_LG + h * 128:TM_LG + (h + 1) * 128].rearrange("(b p) c -> p b c", p=128),
                   writes=[d_gt[b]])
            kb.op("dve", lambda e: e.tensor_copy(out=va[b][:, :, 0:128], in_=vr[b][:]), reads=[d_vr[b]], writes=[d_va[b]])
            p0 = hh * 64
            kparts = [lambda blk: kn[b][:, blk * 128:(blk + 1) * 128], lambda blk: kr2[p0:p0 + 64, blk * 128:(blk + 1) * 128]]
            qparts = [lambda c0, w: qn[b][:, c0:c0 + w], lambda c0, w: qr[b][p0:p0 + 64, c0:c0 + w]]
            return kparts, qparts, (lambda blk: va[b][:, blk, 0:129]), [d_qn[b], d_qr[b], d_kn[b], d_kr2, d_va[b]]

        def finish_head(h, qb, pO, d_pO):
            b = h % 2
            ri = state["ri"] % 4; state["ri"] += 1
            o = ri % 2
            kb.op("dve", lambda e: e.reciprocal(out=rs[ri][:], in_=pO[:, 128:129]), reads=[d_pO], writes=[d_rs[ri]])
            kb.op("dve", lambda e: e.scalar_tensor_tensor(out=ob[o][:], in0=pO[:, 0:128], scalar=rs[ri][:, 0:1],
                                                          in1=gt[b][:, qb, :], op0=ALU.mult, op1=ALU.mult),
                  reads=[d_pO, d_rs[ri], d_gt[b]], writes=[d_ob[o]])
            kb.op("pe", lambda e: e.transpose(out=pTr[:], in_=ob[o][:], identity=C["ident"][:]),
                  reads=[d_ob[o], C["dep"]], writes=[d_pTr])
            kb.op("dve", lambda e: e.tensor_copy(out=obT[b][:, qb * 128:(qb + 1) * 128], in_=pTr[:]),
                  reads=[d_pTr], writes=[d_obT[b]])
            if qb == NBLK - 1:
                kb.dma("sp", obr[3, h * 128:(h + 1) * 128, :], obT[b][:], reads=[d_obT[b]], writes=[d_obr],
                       slot=d_obT[b], merge_w=True)

        attention_core_grouped(kb, es, S, C, [[h] for h in range(8)], load_head, finish_head, 128, "D")
        kb.barrier()


def host_layout_mla(w_uq_l, w_ukv_l):
    q = w_uq_l.reshape(512, 8, 192)
    wuq = np.concatenate([q[:, :, 0:128].reshape(512, 1024), q[:, :, 128:192].reshape(512, 512)], axis=1)
    kv = w_ukv_l.reshape(512, 8, 256)
    wukv = np.concatenate([kv[:, :, 0:128].reshape(512, 1024), kv[:, :, 128:256].reshape(512, 1024)], axis=1)
    return np.ascontiguousarray(wuq), np.ascontiguousarray(wukv)


def phase_T(kb, S, obr, fm, x_ap, wbr_ap, wout_ap, pgam_ap, out_ap, C, w16dep=None):
    wq = "pool" if w16dep is None else "sp"
    wreads = [] if w16dep is None else [w16dep]
    stq = "sp" if w16dep is None else "pool"
    nc = kb.nc
    TS = 512 if S >= 512 else S
    NJ = TS // 128
    d_out = Dep()
    with ExitStack() as es:
        def sb(name, shape, dt):
            return es.enter_context(nc.sbuf_tensor(kb.u(name), shape, dt))

        def ps(name, shape, dt):
            return es.enter_context(nc.psum_tensor(kb.u(name), shape, dt))
        gam = sb("T_gam", [128, D], F32)
        obT = [sb("T_obT%d" % i, [128, 8, TS], BF16) for i in range(4)]
        wb = [sb("T_wb%d" % i, [128, 8, 512], BF16) for i in range(3)]
        wo = [sb("T_wo%d" % i, [128, 16, 512], BF16) for i in range(2)]
        gt = [sb("T_gt%d" % i, [128, 4, TS], BF16) for i in range(2)]
        acc = sb("T_acc", [128, 4, TS], F32)
        tmp = [sb("T_tmp%d" % i, [128, TS], F32) for i in range(2)]
        mixT = sb("T_mixT", [128, 16, TS], BF16)
        ysb = [sb("T_y%d" % i, [128, D], F32) for i in range(NJ)]
        xt = sb("T_x", [128, D], F32)
        ot = sb("T_o", [128, D], F32)
        st = sb("T_st", [128, 4], F32)
        pb = [ps("T_pb%d" % i, [128, 512], F32) for i in range(3)]
        d_gam = Dep(); d_obT = [Dep() for _ in range(4)]; d_wb = [Dep(), Dep(), Dep()]; d_wo = [Dep(), Dep()]
        d_gt = [Dep(), Dep()]; d_acc = Dep(); d_tmp = [Dep(), Dep()]; d_mixT = Dep(); d_y = [Dep() for _ in range(NJ)]; d_x = Dep()
        d_sq = Dep(); d_o = Dep(); d_st = Dep(); d_pb = [Dep() for _ in pb]
        kb.dma("sp", gam[:], pgam_ap.partition_broadcast(128), writes=[d_gam])
        wbi = 0; woi = 0; gti = 0; pbi = 0; tmi = 0
        for ts_ in range(S // TS):
            t0 = ts_ * TS
            for br in range(4):
                kb.dma("sp", obT[br][:], obr[br, :, t0:t0 + TS].rearrange("(kc p) t -> p kc t", p=128), writes=[d_obT[br]])
            for ng in range(4):
                for br in range(4):
                    w_ = wbi % 3; wbi += 1
                    kb.dma(wq, wb[w_][:], wbr_ap[br, :, ng * 512:(ng + 1) * 512].rearrange("(kc p) n -> p kc n", p=128),
                           reads=wreads, writes=[d_wb[w_]], slot=d_wb[w_])
                    g_ = gti % 2; gti += 1
                    r0 = FM_MERGE + br * 2048 + ng * 512
                    kb.dma("sp", gt[g_][:], fm[r0:r0 + 512, t0:t0 + TS].rearrange("(c p) t -> p c t", p=128), writes=[d_gt[g_]])
                    for c4 in range(4):
                        p_ = pbi % 3; pbi += 1
                        for kc in range(8):
                            kb.op("pe", lambda e: e.matmul(pb[p_][:, 0:TS], wb[w_][:, kc, c4 * 128:(c4 + 1) * 128], obT[br][:, kc, :],
                                                           start=(kc == 0), stop=(kc == 7)),
                                  reads=[d_wb[w_], d_obT[br]], writes=[d_pb[p_]], inc=(kc == 7))
                        if br == 0:
                            kb.op("dve", lambda e: e.tensor_tensor(out=acc[:, c4, :], in0=pb[p_][:, 0:TS], in1=gt[g_][:, c4, :], op=ALU.mult),
                                  reads=[d_pb[p_], d_gt[g_]], writes=[d_acc])
                        else:
                            m_ = tmi % 2; tmi += 1
                            kb.op("dve", lambda e: e.tensor_tensor(out=tmp[m_][:], in0=pb[p_][:, 0:TS], in1=gt[g_][:, c4, :], op=ALU.mult),
                                  reads=[d_pb[p_], d_gt[g_]], writes=[d_tmp[m_]])
                            if br < 3:
                                kb.op("dve", lambda e: e.tensor_tensor(out=acc[:, c4, :], in0=acc[:, c4, :], in1=tmp[m_][:], op=ALU.add),
                                      reads=[d_tmp[m_], d_acc], writes=[d_acc])
                            else:
                                kb.op("dve", lambda e: e.tensor_tensor(out=mixT[:, ng * 4 + c4, :], in0=acc[:, c4, :], in1=tmp[m_][:], op=ALU.add),
                                      reads=[d_tmp[m_], d_acc], writes=[d_mixT])
            for mb in range(4):
                w_ = woi % 2; woi += 1
                kb.dma(wq, wo[w_][:], wout_ap[:, mb * 512:(mb + 1) * 512].rearrange("(kc p) n -> p kc n", p=128),
                       reads=wreads, writes=[d_wo[w_]], slot=d_wo[w_])
                for j in range(NJ):
                    p_ = pbi % 3; pbi += 1
                    for kc in range(16):
                        kb.op("pe", lambda e: e.matmul(pb[p_][:], mixT[:, kc, j * 128:(j + 1) * 128], wo[w_][:, kc, :],
                                                       start=(kc == 0), stop=(kc == 15)),
                              reads=[d_mixT, d_wo[w_]], writes=[d_pb[p_]], inc=(kc == 15))
                    kb.op("act", lambda e: e.activation(out=ysb[j][:, mb * 512:(mb + 1) * 512], in_=pb[p_][:], func=AF.Copy),
                          reads=[d_pb[p_]], writes=[d_y[j]])
            for j in range(NJ):
                tok = t0 + j * 128
                kb.dma("sp", xt[:], x_ap[tok:tok + 128, :], writes=[d_x])
                kb.op("act", lambda e: e.activation(out=ot[:], in_=ysb[j][:], func=AF.Square, scale=float(D ** -0.5)), reads=[d_y[j]], writes=[d_o])
                kb.op("dve", lambda e: e.reduce_sum(out=st[:, 0:1], in_=ot[:], axis=AX.X), reads=[d_o], writes=[d_st])
                kb.op("act", lambda e: e.activation(out=st[:, 1:2], in_=st[:, 0:1], func=AF.Sqrt, bias=EPS), reads=[d_st], writes=[d_st])
                kb.op("dve", lambda e: e.reciprocal(out=st[:, 2:3], in_=st[:, 1:2]), reads=[d_st], writes=[d_st])
                kb.op("dve", lambda e: e.scalar_tensor_tensor(out=ot[:], in0=ysb[j][:], scalar=st[:, 2:3], in1=gam[:], op0=ALU.mult, op1=ALU.mult),
                      reads=[d_y[j], d_st, d_gam, d_o], writes=[d_o])
                kb.op("dve", lambda e: e.tensor_tensor(out=ot[:], in0=ot[:], in1=xt[:], op=ALU.add), reads=[d_o, d_x], writes=[d_o])
                kb.dma(stq, out_ap[tok:tok + 128, :], ot[:], reads=[d_o], writes=[d_out], slot=d_o, merge_w=True)
        kb.barrier()
    return d_out


def host_layout_ssd(conv_w_l, conv_b_l):
    cw = np.ascontiguousarray(conv_w_l.reshape(4, 12, 128).transpose(2, 1, 0))
    cb = np.ascontiguousarray(conv_b_l.reshape(12, 128).T)
    return cw, cb


def phase_A(kb, S, fm, tm, tms, cw_ap, cb_ap, dtb_ap, alog_ap, dsk_ap, sn_ap, sx, obr, C, only=None):
    nc = kb.nc
    NBLK = S // 128
    d_sx = Dep()
    with ExitStack() as es:
        def sb(name, shape, dt):
            return es.enter_context(nc.sbuf_tensor(kb.u(name), shape, dt))
        cw = sb("A_cw", [128, 12, 4], F32)
        cbias = sb("A_cb", [128, 12], F32)
        xin = [sb("A_xin%d" % i, [128, S + 4], BF16) for i in range(2)]
        acc = sb("A_acc", [128, S], F32)
        outb = [sb("A_outb%d" % i, [128, S], BF16) for i in range(2)]
        d_c = Dep(); d_xin = [Dep(), Dep()]; d_acc = Dep(); d_outb = [Dep(), Dep()]
        kb.dma("sp", cw[:], cw_ap, writes=[d_c], slot=d_c, merge_w=True)
        kb.dma("sp", cbias[:], cb_ap, writes=[d_c], slot=d_c, merge_w=True)
        for i in range(2):
            kb.op("pool", lambda e: e.memset(xin[i][:, 0:4], 0.0), writes=[d_xin[i]])
        for cb_ in range(12):
            b = cb_ % 2
            kb.dma("sp", xin[b][:, 4:4 + S], fm[cb_ * 128:(cb_ + 1) * 128, :], writes=[d_xin[b]], merge_w=True)
            kb.op("dve", lambda e: e.tensor_scalar(out=acc[:], in0=xin[b][:, 1:1 + S], scalar1=cw[:, cb_, 0:1], scalar2=None, op0=ALU.mult),
                  reads=[d_xin[b], d_c], writes=[d_acc])
            for j in range(1, 4):
                kb.op("dve", lambda e: e.scalar_tensor_tensor(out=acc[:], in0=xin[b][:, 1 + j:1 + j + S], scalar=cw[:, cb_, j:j + 1],
                                                              in1=acc[:], op0=ALU.mult, op1=ALU.add),
                      reads=[d_xin[b], d_c, d_acc], writes=[d_acc])
            kb.op("act", lambda e: e.activation(out=outb[b][:], in_=acc[:], func=AF.Silu, bias=cbias[:, cb_:cb_ + 1]),
                  reads=[d_acc, d_c], writes=[d_outb[b]])
            kb.dma("sp", sx[cb_ * 128:(cb_ + 1) * 128, :], outb[b][:], reads=[d_outb[b]], writes=[d_sx], slot=d_outb[b], merge_w=True)
        kb.barrier()

    if only == 'A1':
        return
    with ExitStack() as es:
        def sb(name, shape, dt):
            return es.enter_context(nc.sbuf_tensor(kb.u(name), shape, dt))

        def ps(name, shape, dt):
            return es.enter_context(nc.psum_tensor(kb.u(name), shape, dt))
        dtb = sb("A_dtb", [128, 16], F32)
        na = sb("A_na", [128, 16], F32)
        dsk = sb("A_dsk", [128, 16], F32)
        snw = sb("A_snw", [128, 1024], F32)
        state = sb("A_state", [128, 2, 512], F32)
        stateb = sb("A_stateb", [128, 2, 512], BF16)
        xcT = [sb("A_xcT%d" % i, [128, 12, 128], BF16) for i in range(2)]
        zs = [sb("A_zs%d" % i, [128, 1024], BF16) for i in range(2)]
        dtr = [sb("A_dtr%d" % i, [128, 16], F32) for i in range(2)]
        xtm = sb("A_xtm", [128, 1024], BF16)
        btm = sb("A_btm", [128, 256], BF16)
        sm = sb("A_sm", [128, 8, 16], F32)
        rhsU = sb("A_rhsU", [128, 16, 128], F32)
        xdt = sb("A_xdt", [128, 1024], BF16)
        xdtd = sb("A_xdtd", [128, 1024], BF16)
        cbm = sb("A_cbm", [128, 2, 128], F32)
        v4 = [sb("A_v4%d" % i, [128, 4, 128], F32) for i in range(2)]
        d4 = [sb("A_d4%d" % i, [128, 4, 128], F32) for i in range(2)]
        mT = sb("A_mT", [128, 16, 128], BF16)
        yb = sb("A_y", [128, 1024], F32)
        t2 = sb("A_t2", [128, 1024], F32)
        sq = sb("A_sq", [128, 1024], F32)
        st = sb("A_st", [128, 8], F32)
        yo = sb("A_yo", [128, 1024], BF16)
        oT = [sb("A_oT%d" % i, [128, 8, 128], BF16) for i in range(2)]
        tst = sb("A_tst", [128, 512], F32)
        pA = ps("A_pA", [128, 512], F32)
        pB = [ps("A_pB%d" % i, [128, 512], F32) for i in range(2)]
        pY = ps("A_pY", [128, 1024], F32)
        pO = [ps("A_pO%d" % i, [128, 512], F32) for i in range(2)]
        pT = ps("A_pT", [128, 1024], BF16)
        d_p = Dep(); d_state = Dep(); d_stateb = Dep(); d_xcT = [Dep(), Dep()]; d_zs = [Dep(), Dep()]; d_dtr = [Dep(), Dep()]
        d_xtm = Dep(); d_btm = Dep(); d_sm = Dep(); d_rhsU = Dep(); d_xdt = Dep(); d_xdtd = Dep(); d_cbm = Dep()
        d_v4 = [Dep(), Dep()]; d_d4 = [Dep(), Dep()]; d_mT = Dep(); d_y = Dep(); d_t2 = Dep(); d_sq = Dep(); d_st = Dep()
        d_yo = Dep(); d_oT = [Dep(), Dep()]; d_tst = Dep()
        d_pA = Dep(); d_pB = [Dep(), Dep()]; d_pY = Dep(); d_pO = [Dep(), Dep()]; d_pT = Dep(); d_obr = Dep()
        kb.dma("sp", dtb[:], dtb_ap.partition_broadcast(128), writes=[d_p], slot=d_p, merge_w=True)
        kb.dma("sp", na[:], alog_ap.partition_broadcast(128), writes=[d_p], slot=d_p, merge_w=True)
        kb.dma("sp", dsk[:], dsk_ap.partition_broadcast(128), writes=[d_p], slot=d_p, merge_w=True)
        kb.dma("sp", snw[:], sn_ap.partition_broadcast(128), writes=[d_p], slot=d_p, merge_w=True)
        kb.op("act", lambda e: e.activation(out=na[:], in_=na[:], func=AF.Exp), reads=[d_p], writes=[d_p])
        kb.op("dve", lambda e: e.memset(state[:], 0.0), writes=[d_state])
        kb.op("dve", lambda e: e.memset(stateb[:], 0.0), writes=[d_stateb])
        pbi = 0
        poi = 0
        for c in range(NBLK):
            b = c % 2
            t0 = c * 128
            kb.dma("sp", xcT[b][:], sx[:, t0:t0 + 128].rearrange("(cb p) t -> p cb t", p=128), reads=[d_sx], writes=[d_xcT[b]])
            kb.dma("sp", zs[b][:], tm[t0:t0 + 128, TM_Z:TM_Z + 1024], writes=[d_zs[b]])
            kb.dma("sp", dtr[b][:], tms[t0:t0 + 128, 0:16], writes=[d_dtr[b]])
            for k in range(8):
                kb.op("pe", lambda e: e.transpose(out=pT[:, k * 128:(k + 1) * 128], in_=xcT[b][:, k, :], identity=C["ident"][:]),
                      reads=[d_xcT[b], C["dep"]], writes=[d_pT], inc=(k == 7))
            kb.op("act", lambda e: e.activation(out=xtm[:], in_=pT[:], func=AF.Copy), reads=[d_pT], writes=[d_xtm])
            for k in range(2):
                kb.op("pe", lambda e: e.transpose(out=pT[:, k * 128:(k + 1) * 128], in_=xcT[b][:, 8 + k, :], identity=C["ident"][:]),
                      reads=[d_xcT[b], C["dep"]], writes=[d_pT], inc=(k == 1))
            kb.op("act", lambda e: e.activation(out=btm[:], in_=pT[:, 0:256], func=AF.Copy), reads=[d_pT], writes=[d_btm])
            kb.op("dve", lambda e: e.tensor_tensor(out=sm[:, 7, :], in0=dtr[b][:], in1=dtb[:], op=ALU.add), reads=[d_dtr[b], d_p], writes=[d_sm])
            kb.op("act", lambda e: e.activation(out=sm[:, 7, :], in_=sm[:, 7, :], func=AF.Exp), reads=[d_sm], writes=[d_sm])
            kb.op("act", lambda e: e.activation(out=sm[:, 0, :], in_=sm[:, 7, :], func=AF.Ln, bias=1.0), reads=[d_sm], writes=[d_sm])
            kb.op("dve", lambda e: e.tensor_tensor(out=sm[:, 1, :], in0=sm[:, 0, :], in1=na[:], op=ALU.mult), reads=[d_sm, d_p], writes=[d_sm])
            kb.op("pe", lambda e: e.matmul(pA[:, 256:272], C["U"][:], sm[:, 1, :], start=True, stop=True), reads=[d_sm, C["dep"]], writes=[d_pA], inc=False)
            kb.op("pe", lambda e: e.matmul(pA[:, 272:288], C["ones"][:], sm[:, 1, :], start=True, stop=True), reads=[d_sm, C["dep"]], writes=[d_pA], inc=False)
            for g in range(2):
                kb.op("pe", lambda e: e.matmul(pA[:, g * 128:(g + 1) * 128], xcT[b][:, 8 + g, :], xcT[b][:, 10 + g, :], start=True, stop=True),
                      reads=[d_xcT[b]], writes=[d_pA], inc=(g == 1))
            kb.op("dve", lambda e: e.tensor_copy(out=sm[:, 2:4, :], in_=pA[:, 256:288].rearrange("p (a h) -> p a h", a=2)), reads=[d_pA], writes=[d_sm])
            kb.op("dve", lambda e: e.tensor_tensor(out=cbm[:], in0=pA[:, 0:256].rearrange("p (g l) -> p g l", g=2),
                                                   in1=C["U"][:].unsqueeze(1).broadcast_to([128, 2, 128]), op=ALU.mult),
                  reads=[d_pA, C["dep"]], writes=[d_cbm])
            kb.op("dve", lambda e: e.tensor_tensor(out=sm[:, 4, :], in0=sm[:, 2, :], in1=sm[:, 3, :], op=ALU.subtract), reads=[d_sm], writes=[d_sm])
            kb.op("act", lambda e: e.activation(out=sm[:, 4, :], in_=sm[:, 4, :], func=AF.Exp), reads=[d_sm], writes=[d_sm])
            kb.op("act", lambda e: e.activation(out=sm[:, 5, :], in_=sm[:, 3, :], func=AF.Exp, scale=-1.0), reads=[d_sm], writes=[d_sm])
            kb.op("act", lambda e: e.activation(out=sm[:, 6, :], in_=sm[:, 2, :], func=AF.Exp, scale=-1.0), reads=[d_sm], writes=[d_sm])
            x3 = xtm[:].rearrange("p (h c) -> p h c", h=16)
            kb.op("dve", lambda e: e.tensor_tensor(out=xdt[:].rearrange("p (h c) -> p h c", h=16), in0=x3,
                                                   in1=sm[:, 0, :].unsqueeze(2).broadcast_to([128, 16, 64]), op=ALU.mult),
                  reads=[d_xtm, d_sm], writes=[d_xdt])
            kb.op("pool", lambda e: e.tensor_tensor(out=xdtd[:].rearrange("p (h c) -> p h c", h=16), in0=xdt[:].rearrange("p (h c) -> p h c", h=16),
                                                    in1=sm[:, 4, :].unsqueeze(2).broadcast_to([128, 16, 64]), op=ALU.mult),
                  reads=[d_xdt, d_sm], writes=[d_xdtd])
            kb.op("dve", lambda e: e.tensor_tensor(out=rhsU[:], in0=sm[:, 1, :].unsqueeze(2).broadcast_to([128, 16, 128]),
                                                   in1=C["U"][:].unsqueeze(1).broadcast_to([128, 16, 128]), op=ALU.mult),
                  reads=[d_sm, C["dep"]], writes=[d_rhsU])
            for q4 in range(4):
                p_ = pbi % 2; pbi += 1
                kb.op("pe", lambda e: e.matmul(pB[p_][:], C["ones"][:], rhsU[:, q4 * 4:(q4 + 1) * 4, :].rearrange("p h l -> p (h l)"),
                                               start=True, stop=True), reads=[d_rhsU, C["dep"]], writes=[d_pB[p_]])
                kb.op("dve", lambda e: e.tensor_tensor(out=v4[p_][:], in0=pB[p_][:].rearrange("p (h l) -> p h l", h=4),
                                                       in1=sm[:, 2, q4 * 4:(q4 + 1) * 4].unsqueeze(2).broadcast_to([128, 4, 128]), op=ALU.subtract),
                      reads=[d_pB[p_], d_sm], writes=[d_v4[p_]])
                kb.op("dve", lambda e: e.tensor_scalar_max(out=v4[p_][:], in0=v4[p_][:], scalar1=0.0), reads=[d_v4[p_]], writes=[d_v4[p_]])
                kb.op("act", lambda e: e.activation(out=d4[p_][:], in_=v4[p_][:], func=AF.Exp, scale=-1.0), reads=[d_v4[p_]], writes=[d_d4[p_]])
                g = q4 // 2
                kb.op("pool", lambda e: e.tensor_tensor(out=mT[:, q4 * 4:(q4 + 1) * 4, :], in0=d4[p_][:],
                                                        in1=cbm[:, g, :].unsqueeze(1).broadcast_to([128, 4, 128]), op=ALU.mult),
                      reads=[d_d4[p_], d_cbm], writes=[d_mT])
            for h in range(16):
                kb.op("pe", lambda e: e.matmul(pY[:, h * 64:(h + 1) * 64], mT[:, h, :], xdt[:, h * 64:(h + 1) * 64], start=True, stop=True),
                      reads=[d_mT, d_xdt], writes=[d_pY], inc=(h == 15))
            for g in range(2):
                o_ = poi % 2; poi += 1
                kb.op("pe", lambda e: e.matmul(pO[o_][:], xcT[b][:, 10 + g, :], stateb[:, g, :], start=True, stop=True),
                      reads=[d_xcT[b], d_stateb], writes=[d_pO[o_]])
                kb.op("dve", lambda e: e.tensor_tensor(out=t2[:, g * 512:(g + 1) * 512].rearrange("p (h c) -> p h c", h=8),
                                                       in0=pO[o_][:].rearrange("p (h c) -> p h c", h=8),
                                                       in1=sm[:, 6, g * 8:(g + 1) * 8].unsqueeze(2).broadcast_to([128, 8, 64]), op=ALU.mult),
                      reads=[d_pO[o_], d_sm], writes=[d_t2])
            kb.op("dve", lambda e: e.tensor_tensor(out=yb[:], in0=pY[:], in1=t2[:], op=ALU.add), reads=[d_pY, d_t2], writes=[d_y])
            for g in range(2):
                o_ = poi % 2; poi += 1
                kb.op("pe", lambda e: e.matmul(pO[o_][:], btm[:, g * 128:(g + 1) * 128], xdtd[:, g * 512:(g + 1) * 512], start=True, stop=True),
                      reads=[d_btm, d_xdtd], writes=[d_pO[o_]])
                kb.op("dve", lambda e: e.tensor_tensor(out=tst[:].rearrange("p (h c) -> p h c", h=8),
                                                       in0=state[:, g, :].rearrange("p (h c) -> p h c", h=8),
                                                       in1=sm[:, 5, g * 8:(g + 1) * 8].unsqueeze(2).broadcast_to([128, 8, 64]), op=ALU.mult),
                      reads=[d_state, d_sm], writes=[d_tst])
                kb.op("dve", lambda e: e.tensor_tensor(out=state[:, g, :], in0=tst[:], in1=pO[o_][:], op=ALU.add),
                      reads=[d_tst, d_pO[o_]], writes=[d_state])
            kb.op("act", lambda e: e.activation(out=stateb[:], in_=state[:], func=AF.Copy), reads=[d_state], writes=[d_stateb])
            kb.op("pool", lambda e: e.tensor_tensor(out=t2[:].rearrange("p (h c) -> p h c", h=16), in0=x3,
                                                    in1=dsk[:].unsqueeze(2).broadcast_to([128, 16, 64]), op=ALU.mult),
                  reads=[d_xtm, d_p], writes=[d_t2])
            kb.op("dve", lambda e: e.tensor_tensor(out=yb[:], in0=yb[:], in1=t2[:], op=ALU.add), reads=[d_y, d_t2], writes=[d_y])
            kb.op("dve", lambda e: e.tensor_tensor(out=yb[:], in0=yb[:], in1=zs[b][:], op=ALU.mult), reads=[d_y, d_zs[b]], writes=[d_y])
            kb.op("act", lambda e: e.activation(out=sq[:], in_=yb[:], func=AF.Square, scale=float(512 ** -0.5)), reads=[d_y], writes=[d_sq])
            kb.op("dve", lambda e: e.reduce_sum(out=st[:, 0:2], in_=sq[:].rearrange("p (g c) -> p g c", g=2), axis=AX.X), reads=[d_sq], writes=[d_st])
            kb.op("act", lambda e: e.activation(out=st[:, 2:4], in_=st[:, 0:2], func=AF.Sqrt, bias=EPS), reads=[d_st], writes=[d_st])
            kb.op("dve", lambda e: e.reciprocal(out=st[:, 4:6], in_=st[:, 2:4]), reads=[d_st], writes=[d_st])
            for g in range(2):
                kb.op("dve", lambda e: e.scalar_tensor_tensor(out=yo[:, g * 512:(g + 1) * 512], in0=yb[:, g * 512:(g + 1) * 512],
                                                              scalar=st[:, 4 + g:5 + g], in1=snw[:, g * 512:(g + 1) * 512], op0=ALU.mult, op1=ALU.mult),
                      reads=[d_y, d_st, d_p], writes=[d_yo])
            for k in range(8):
                kb.op("pe", lambda e: e.transpose(out=pT[:, k * 128:(k + 1) * 128], in_=yo[:, k * 128:(k + 1) * 128], identity=C["ident"][:]),
                      reads=[d_yo, C["dep"]], writes=[d_pT], inc=(k == 7))
            kb.op("act", lambda e: e.activation(out=oT[b][:], in_=pT[:].rearrange("p (k t) -> p k t", k=8), func=AF.Copy), reads=[d_pT], writes=[d_oT[b]])
            kb.dma("sp", obr[0, :, t0:t0 + 128].rearrange("(k p) t -> p k t", p=128), oT[b][:], reads=[d_oT[b]], writes=[d_obr],
                   slot=d_oT[b], merge_w=True)
        kb.barrier()


GLA_SCALE = float(128 ** -0.5)


def phase_C(kb, S, tm, tms, w2_ap, gb_ap, gn_ap, obr, C, stop=None):
    old_embed = kb.embed
    kb.embed = False
    try:
        _phase_C(kb, S, tm, tms, w2_ap, gb_ap, gn_ap, obr, C, stop)
    finally:
        kb.embed = old_embed


def _phase_C(kb, S, tm, tms, w2_ap, gb_ap, gn_ap, obr, C, stop=None):
    nc = kb.nc
    NBLK = S // 128
    with ExitStack() as es:
        def sb(name, shape, dt):
            return es.enter_context(nc.sbuf_tensor(kb.u(name), shape, dt))

        def ps(name, shape, dt):
            return es.enter_context(nc.psum_tensor(kb.u(name), shape, dt))
        w2a = sb("C_w2a", [33, 512], BF16)
        w2f = sb("C_w2f", [33, 512], F32)
        lrb = sb("C_lrb", [128, 16], BF16)
        qT0 = sb("C_qT0", [128, 4, 128], BF16)
        qT1 = sb("C_qT1", [128, 4, 128], BF16)
        gnw = sb("C_gnw", [128, 256], F32)
        lrT = sb("C_lrT", [33, 128], BF16)
        qr_ = [sb("C_q%d" % i, [128, 512], BF16) for i in range(2)]
        kr_ = [sb("C_k%d" % i, [128, 512], BF16) for i in range(2)]
        vr_ = [sb("C_v%d" % i, [128, 1024], BF16) for i in range(2)]
        gg_ = [sb("C_g%d" % i, [128, 1024], BF16) for i in range(2)]
        lr_ = [sb("C_lr%d" % i, [128, 16], F32) for i in range(2)]
        Gm = sb("C_Gm", [128, 512], F32)
        Bc = sb("C_Bc", [128, 512], F32)
        E1 = sb("C_E1", [128, 512], F32)
        E2 = sb("C_E2", [128, 512], F32)
        E3 = sb("C_E3", [128, 512], F32)
        qt = sb("C_qt", [128, 512], BF16)
        kt = sb("C_kt", [128, 512], BF16)
        kh0 = sb("C_kh0", [128, 512], BF16)
        kh1 = sb("C_kh1", [128, 512], BF16)
        qkT = sb("C_qkT", [128, 8, 128], BF16)
        dec = sb("C_dec", [128, 8], F32)
        attn = sb("C_attn", [128, 4, 128], BF16)
        SA = sb("C_SA", [128, 4, 256], F32)
        SB = sb("C_SB", [128, 4, 256], F32)
        S0b = sb("C_S0b", [128, 4, 256], BF16)
        S1b = sb("C_S1b", [128, 4, 256], BF16)
        osb = sb("C_osb", [128, 1024], F32)
        sq = sb("C_sq", [128, 1024], F32)
        st = sb("C_st", [128, 12], F32)
        on = sb("C_on", [128, 1024], F32)
        yo = sb("C_yo", [128, 1024], BF16)
        oT = [sb("C_oT%d" % i, [128, 8, 128], BF16) for i in range(2)]
        pZ = ps("C_pZ", [128, 512], F32)
        pBc = ps("C_pBc", [128, 512], F32)
        pBl = ps("C_pBl", [128, 512], F32)
        pT = ps("C_pT", [128, 1024], BF16)
        pA = ps("C_pA", [128, 512], F32)
        pO = ps("C_pO", [128, 1024], F32)
        pS = ps("C_pS", [128, 2, 256], F32)
        d_w = Dep(); d_lrT = Dep(); d_q = [Dep(), Dep()]; d_k = [Dep(), Dep()]; d_v = [Dep(), Dep()]; d_g = [Dep(), Dep()]
        d_lr = [Dep(), Dep()]; d_Gm = Dep(); d_Bc = Dep(); d_E1 = Dep(); d_E2 = Dep(); d_E3 = Dep(); d_qt = Dep(); d_kt = Dep()
        d_kh = Dep(); d_qkT = Dep(); d_dec = Dep(); d_attn = [Dep() for _ in range(4)]; d_SA = [Dep() for _ in range(4)]
        d_SB = [Dep() for _ in range(4)]; d_S0b = [Dep() for _ in range(4)]; d_S1b = [Dep() for _ in range(4)]
        d_osb = Dep(); d_sq = Dep(); d_st = Dep(); d_on = Dep(); d_yo = Dep(); d_oT = [Dep(), Dep()]
        d_pZ = Dep(); d_pBc = Dep(); d_pBl = Dep(); d_pT = Dep(); d_pA = [Dep() for _ in range(4)]
        d_pO = [Dep() for _ in range(4)]; d_pS = [Dep(), Dep()]; d_obr = Dep()
        kb.op("dve", lambda e: e.memset(w2f[:], 0.0), writes=[d_w])
        kb.dma("sp", w2f[0:16, :], w2_ap, writes=[d_w], slot=d_w)
        kb.dma("sp", w2f[32:33, :], gb_ap, writes=[d_w], slot=d_w, merge_w=True)
        kb.op("dve", lambda e: e.tensor_copy(out=w2a[:], in_=w2f[:]), reads=[d_w], writes=[d_w])
        d_lrb = Dep(); d_qT0 = Dep(); d_qT1 = Dep()
        kb.op("pool", lambda e: e.memset(kh0[:], 0.0), writes=[d_kh])
        kb.op("pool", lambda e: e.memset(kh1[:], 0.0), writes=[d_kh])
        kb.op("pool", lambda e: e.memset(qT0[:], 0.0), writes=[d_qT0])
        kb.op("pool", lambda e: e.memset(qT1[:], 0.0), writes=[d_qT1])
        kb.dma("sp", gnw[:], gn_ap.partition_broadcast(128), writes=[d_w], slot=d_w, merge_w=True)
        kb.op("dve", lambda e: e.memset(lrT[:], 0.0), writes=[d_lrT])
        kb.op("dve", lambda e: e.memset(lrT[32:33, :], 1.0), reads=[d_lrT], writes=[d_lrT])
        kb.op("dve", lambda e: e.memset(SA[:], 0.0), writes=d_SA)
        kb.op("dve", lambda e: e.memset(S0b[:], 0.0), writes=d_S0b)
        psi = 0
        for blk in range(NBLK):
            b = blk % 2
            t0 = blk * 128
            kb.dma("sp", qr_[b][:], tm[t0:t0 + 128, TM_GQ:TM_GQ + 512], writes=[d_q[b]])
            kb.dma("sp", kr_[b][:], tm[t0:t0 + 128, TM_GK:TM_GK + 512], writes=[d_k[b]])
            kb.dma("sp", vr_[b][:], tm[t0:t0 + 128, TM_GV:TM_GV + 1024], writes=[d_v[b]])
            kb.dma("sp", gg_[b][:], tm[t0:t0 + 128, TM_GG:TM_GG + 1024], writes=[d_g[b]])
            kb.dma("sp", lr_[b][:], tms[t0:t0 + 128, 32:48], writes=[d_lr[b]])
            kb.op("dve", lambda e: e.tensor_copy(out=lrb[:], in_=lr_[b][:]), reads=[d_lr[b]], writes=[d_lrb])
            kb.op("pe", lambda e: e.transpose(out=pT[0:16, 0:128], in_=lrb[:], identity=C["ident"][:]),
                  reads=[d_lrb, C["dep"]], writes=[d_pT])
            kb.op("dve", lambda e: e.tensor_copy(out=lrT[0:16, :], in_=pT[0:16, 0:128]), reads=[d_pT, d_lrT], writes=[d_lrT])
            kb.op("pe", lambda e: e.matmul(pZ[:], lrT[:], w2a[:], start=True, stop=True), reads=[d_lrT, d_w], writes=[d_pZ])
            kb.op("act", lambda e: e.activation(out=Gm[:], in_=pZ[:], func=AF.Exp, scale=-1.0), reads=[d_pZ], writes=[d_Gm])
            kb.op("act", lambda e: e.activation(out=Gm[:], in_=Gm[:], func=AF.Ln, bias=1.0), reads=[d_Gm], writes=[d_Gm])
            if stop == 1:
                kb.barrier()
                return
            kb.op("pe", lambda e: e.matmul(pBc[:], C["U64s"][:], Gm[:], start=True, stop=True), reads=[d_Gm, C["dep"]], writes=[d_pBc])
            kb.op("pe", lambda e: e.matmul(pBl[:], C["BO64s"][:], Gm[:], start=True, stop=True), reads=[d_Gm, C["dep"]], writes=[d_pBl])
            for h in range(4):
                kb.op("pe", lambda e: e.matmul(pZ[:, h * 128:(h + 1) * 128], Gm[:, h * 128:(h + 1) * 128], C["BO64s"][:],
                                               start=True, stop=True), reads=[d_Gm, C["dep"]], writes=[d_pZ], inc=(h == 3))
            kb.op("act", lambda e: e.activation(out=dec[:], in_=pZ[:, 0:512:64], func=AF.Exp, scale=-1.0), reads=[d_pZ], writes=[d_dec])
            if stop == 2:
                kb.barrier()
                return
            kb.op("dve", lambda e: e.tensor_copy(out=Bc[:], in_=pBc[:]), reads=[d_pBc], writes=[d_Bc])
            kb.op("act", lambda e: e.activation(out=E1[:], in_=Bc[:], func=AF.Exp, scale=-1.0), reads=[d_Bc], writes=[d_E1])
            kb.op("act", lambda e: e.activation(out=E2[:], in_=Bc[:], func=AF.Exp), reads=[d_Bc], writes=[d_E2])
            kb.op("dve", lambda e: e.tensor_tensor(out=E3[:], in0=Bc[:], in1=pBl[:], op=ALU.subtract), reads=[d_Bc, d_pBl], writes=[d_E3])
            kb.op("act", lambda e: e.activation(out=E3[:], in_=E3[:], func=AF.Exp), reads=[d_E3], writes=[d_E3])
            kb.op("dve", lambda e: e.scalar_tensor_tensor(out=qt[:], in0=qr_[b][:], scalar=GLA_SCALE, in1=E1[:], op0=ALU.mult, op1=ALU.mult),
                  reads=[d_q[b], d_E1], writes=[d_qt])
            kb.op("pool", lambda e: e.tensor_tensor(out=kt[:], in0=kr_[b][:], in1=E2[:], op=ALU.mult), reads=[d_k[b], d_E2], writes=[d_kt])
            kb.op("dve", lambda e: e.tensor_tensor(out=kh0[0:64, :], in0=kr_[b][0:64, :], in1=E3[0:64, :], op=ALU.mult), reads=[d_k[b], d_E3], writes=[d_kh])
            kb.op("dve", lambda e: e.tensor_tensor(out=kh1[64:128, :], in0=kr_[b][64:128, :], in1=E3[64:128, :], op=ALU.mult), reads=[d_k[b], d_E3, d_kh], writes=[d_kh])
            if stop == 3:
                kb.barrier()
                return
            for h in range(4):
                kb.op("pe", lambda e: e.transpose(out=pT[:, h * 128:(h + 1) * 128], in_=qt[:, h * 128:(h + 1) * 128], identity=C["ident"][:]),
                      reads=[d_qt, C["dep"]], writes=[d_pT], inc=False)
            for h in range(4):
                kb.op("pe", lambda e: e.transpose(out=pT[:, (4 + h) * 128:(5 + h) * 128], in_=kt[:, h * 128:(h + 1) * 128], identity=C["ident"][:]),
                      reads=[d_kt, C["dep"]], writes=[d_pT], inc=(h == 3))
            if stop == 31:
                kb.barrier()
                return
            kb.op("act", lambda e: e.activation(out=qkT[:], in_=pT[:].rearrange("p (k t) -> p k t", k=8), func=AF.Copy), reads=[d_pT], writes=[d_qkT])
            if stop == 32:
                kb.barrier()
                return
            kb.op("act", lambda e: e.activation(out=qT0[:, :, 0:64], in_=pT[:, 0:512].rearrange("p (k t) -> p k t", k=4)[:, :, 0:64], func=AF.Copy),
                  reads=[d_pT], writes=[d_qT0])
            kb.op("act", lambda e: e.activation(out=qT1[:, :, 64:128], in_=pT[:, 0:512].rearrange("p (k t) -> p k t", k=4)[:, :, 64:128], func=AF.Copy),
                  reads=[d_pT], writes=[d_qT1])
            if stop == 4:
                kb.barrier()
                return
            for h in range(4):
                kb.op("pe", lambda e: e.matmul(pA[:, h * 128:(h + 1) * 128], qkT[:, 4 + h, :], qkT[:, h, :], start=True, stop=True),
                      reads=[d_qkT], writes=[d_pA[h]])
                kb.op("dve", lambda e: e.tensor_tensor(out=attn[:, h, :], in0=pA[:, h * 128:(h + 1) * 128], in1=C["U64"][:], op=ALU.mult),
                      reads=[d_pA[h], C["dep"]], writes=[d_attn[h]])
                if stop == 41:
                    kb.barrier()
                    return
                s0 = psi % 2; psi += 1
                kb.op("pe", lambda e: e.matmul(pS[:, s0, :], kh0[:, h * 128:(h + 1) * 128], vr_[b][:, h * 256:(h + 1) * 256], start=True, stop=True),
                      reads=[d_kh, d_v[b]], writes=[d_pS[s0]])
                kb.op("dve", lambda e: e.scalar_tensor_tensor(out=SB[:, h, :], in0=SA[:, h, :], scalar=dec[:, 2 * h:2 * h + 1], in1=pS[:, s0, :],
                                                              op0=ALU.mult, op1=ALU.add),
                      reads=[d_SA[h], d_dec, d_pS[s0]], writes=[d_SB[h]])
                kb.op("act", lambda e: e.activation(out=S1b[:, h, :], in_=SB[:, h, :], func=AF.Copy), reads=[d_SB[h]], writes=[d_S1b[h]])
                if stop == 42:
                    kb.barrier()
                    return
                s1 = psi % 2; psi += 1
                kb.op("pe", lambda e: e.matmul(pS[:, s1, :], kh1[:, h * 128:(h + 1) * 128], vr_[b][:, h * 256:(h + 1) * 256], start=True, stop=True),
                      reads=[d_kh, d_v[b]], writes=[d_pS[s1]])
                if stop == 43:
                    kb.barrier()
                    return
                kb.op("pe", lambda e: e.matmul(pO[:, h * 256:(h + 1) * 256], attn[:, h, :], vr_[b][:, h * 256:(h + 1) * 256], start=True, stop=False),
                      reads=[d_attn[h], d_v[b]], writes=[d_pO[h]], inc=False)
                kb.op("pe", lambda e: e.matmul(pO[:, h * 256:(h + 1) * 256], qT0[:, h, :], S0b[:, h, :], start=False, stop=False),
                      reads=[d_qT0, d_S0b[h]], writes=[d_pO[h]], inc=False)
                kb.op("pe", lambda e: e.matmul(pO[:, h * 256:(h + 1) * 256], qT1[:, h, :], S1b[:, h, :], start=False, stop=True),
                      reads=[d_qT1, d_S1b[h]], writes=[d_pO[h]])
                if stop == 44:
                    kb.barrier()
                    return
                kb.op("dve", lambda e: e.scalar_tensor_tensor(out=SA[:, h, :], in0=SB[:, h, :], scalar=dec[:, 2 * h + 1:2 * h + 2], in1=pS[:, s1, :],
                                                              op0=ALU.mult, op1=ALU.add),
                      reads=[d_SB[h], d_dec, d_pS[s1]], writes=[d_SA[h]])
                kb.op("act", lambda e: e.activation(out=S0b[:, h, :], in_=SA[:, h, :], func=AF.Copy), reads=[d_SA[h]], writes=[d_S0b[h]])
            if stop == 45:
                kb.barrier()
                return
            kb.op("act", lambda e: e.activation(out=osb[:], in_=pO[:], func=AF.Copy), reads=d_pO, writes=[d_osb])
            if stop == 5:
                kb.barrier()
                return
            kb.op("act", lambda e: e.activation(out=sq[:], in_=osb[:], func=AF.Square, scale=float(256 ** -0.5)), reads=[d_osb], writes=[d_sq])
            kb.op("dve", lambda e: e.reduce_sum(out=st[:, 0:4], in_=sq[:].rearrange("p (h c) -> p h c", h=4), axis=AX.X), reads=[d_sq], writes=[d_st])
            kb.op("act", lambda e: e.activation(out=st[:, 4:8], in_=st[:, 0:4], func=AF.Sqrt, bias=EPS), reads=[d_st], writes=[d_st])
            kb.op("dve", lambda e: e.reciprocal(out=st[:, 8:12], in_=st[:, 4:8]), reads=[d_st], writes=[d_st])
            for h in range(4):
                kb.op("dve", lambda e: e.scalar_tensor_tensor(out=on[:, h * 256:(h + 1) * 256], in0=osb[:, h * 256:(h + 1) * 256],
                                                              scalar=st[:, 8 + h:9 + h], in1=gnw[:], op0=ALU.mult, op1=ALU.mult),
                      reads=[d_osb, d_st, d_w], writes=[d_on])
            kb.op("pool", lambda e: e.tensor_tensor(out=yo[:], in0=on[:], in1=gg_[b][:], op=ALU.mult), reads=[d_on, d_g[b]], writes=[d_yo])
            for k in range(8):
                kb.op("pe", lambda e: e.transpose(out=pT[:, k * 128:(k + 1) * 128], in_=yo[:, k * 128:(k + 1) * 128], identity=C["ident"][:]),
                      reads=[d_yo, C["dep"]], writes=[d_pT], inc=(k == 7))
            kb.op("act", lambda e: e.activation(out=oT[b][:], in_=pT[:].rearrange("p (k t) -> p k t", k=8), func=AF.Copy), reads=[d_pT], writes=[d_oT[b]])
            kb.dma("sp", obr[2, :, t0:t0 + 128].rearrange("(k p) t -> p k t", p=128), oT[b][:], reads=[d_oT[b]], writes=[d_obr],
                   slot=d_oT[b], merge_w=True)
        kb.barrier()


PER_LAYER = [
    ("pre", [1, D]), ("post", [1, D]), ("wtm", [D, NTM]), ("wsm", [D, 128]), ("wfm", [D, NFM]),
    ("cw", [128, 12, 4]), ("cb", [128, 12]), ("dtb", [1, 16]), ("alog", [1, 16]), ("dsk", [1, 16]), ("sn", [1, 1024]),
    ("fgb", [1, 16]), ("w2", [16, 512]), ("gb", [1, 512]), ("gn", [1, 256]), ("qn", [1, 512]), ("kvn", [1, 512]),
    ("wuq", [512, 1536]), ("wukv", [512, 2048]), ("wbr", [4, 1024, D]), ("wout", [D, D]),
]


def build_program(S, L, debug=False):
    nc = bass.Bass("TRN2", target_bir_lowering=False)
    I32 = mybir.dt.int32
    x = nc.dram_tensor("x", [S, D], F32, kind="ExternalInput").ap()
    pos = nc.dram_tensor("pos", [128, S // 128], I32, kind="ExternalInput").ap()
    cst = nc.dram_tensor("cst", [6, 128, 128], F32, kind="ExternalInput").ap()
    invf = nc.dram_tensor("invf", [1, 32], F32, kind="ExternalInput").ap()
    W = {}
    for name, shape in PER_LAYER:
        W[name] = nc.dram_tensor(name, [L] + shape, F32, kind="ExternalInput").ap()
    out = nc.dram_tensor("out", [S, D], F32, kind="ExternalOutput").ap()
    knd = "ExternalOutput" if debug else "Internal"
    tm = nc.dram_tensor("s_tm", [S, NTM], BF16, kind=knd).ap()
    tms = nc.dram_tensor("s_tms", [S, 128], F32, kind=knd).ap()
    fm = nc.dram_tensor("s_fm", [NFM, S], BF16, kind=knd).ap()
    obr = nc.dram_tensor("s_obr", [4, 1024, S], BF16, kind=knd).ap()
    sx = nc.dram_tensor("s_sx", [1536, S], BF16, kind="Internal").ap()
    mq = nc.dram_tensor("s_mq", [1536, S], BF16, kind="Internal").ap()
    mk = nc.dram_tensor("s_mk", [1088, S], BF16, kind="Internal").ap()
    mv = nc.dram_tensor("s_mv", [S, 1024], BF16, kind="Internal").ap()
    xmid = [nc.dram_tensor("s_xmid%d" % i, [S, D], F32, kind="Internal").ap() for i in range(max(L - 1, 1))]
    wbr16 = nc.dram_tensor("s_wbr16", [4, 1024, D], BF16, kind="Internal").ap()
    wout16 = nc.dram_tensor("s_wout16", [D, D], BF16, kind="Internal").ap()
    kb = KB(nc)
    with ExitStack() as es:
        C = load_consts(kb, es, cst)
        rope = setup_rope(kb, es, S, pos, invf)
        cur = x
        d_w16 = Dep()
        for l in range(L):
            dst = out if l == L - 1 else xmid[l]
            d_cast = Dep()
            for br in range(4):
                kb.dma("pool", wbr16[br], W["wbr"][l][br], writes=[d_w16], slot=d_cast, merge_w=(br > 0))
            kb.dma("pool", wout16, W["wout"][l], writes=[d_w16], slot=d_cast, merge_w=True)
            phase_P(kb, S, cur, W["pre"][l], W["wtm"][l], W["wsm"][l], W["wfm"][l], tm, tms, fm, (C["ident"], C["dep"]))
            phase_A(kb, S, fm, tm, tms, W["cw"][l], W["cb"][l], W["dtb"][l], W["alog"][l], W["dsk"][l], W["sn"][l], sx, obr, C)
            phase_B(kb, S, fm, tm, tms, W["fgb"][l], obr, C)
            phase_C(kb, S, tm, tms, W["w2"][l], W["gb"][l], W["gn"][l], obr, C)
            phase_D(kb, S, tm, tms, W["qn"][l], W["kvn"][l], W["wuq"][l], W["wukv"][l], mq, mk, mv, obr, C, rope)
            phase_T(kb, S, obr, fm, cur, wbr16, wout16, W["post"][l], dst, C, w16dep=d_w16)
            cur = dst
        kb.barrier()
    return nc, kb


def host_prepare(inputs, L):
    g = lambda k: np.asarray(inputs[k], dtype=np.float32)
    Wd = {n: [] for n, _ in PER_LAYER}
    for l in range(L):
        wtm, wsm, wfm = host_layout_win(g("w_in")[l])
        cw, cb = host_layout_ssd(g("conv_w")[l], g("conv_b")[l])
        wuq, wukv = host_layout_mla(g("w_uq")[l], g("w_ukv")[l])
        vals = dict(pre=g("pre_norm")[l].reshape(1, D), post=g("post_norm")[l].reshape(1, D), wtm=wtm, wsm=wsm, wfm=wfm, cw=cw, cb=cb,
                    dtb=g("dt_bias")[l].reshape(1, 16), alog=g("a_log")[l].reshape(1, 16), dsk=g("d_skip")[l].reshape(1, 16),
                    sn=g("ssm_norm")[l].reshape(1, 1024), fgb=g("fgate_b")[l].reshape(1, 16), w2=g("gla_w2")[l],
                    gb=g("gla_b")[l].reshape(1, 512), gn=g("gla_norm")[l].reshape(1, 256), qn=g("q_norm")[l].reshape(1, 512),
                    kvn=g("kv_norm")[l].reshape(1, 512), wuq=wuq, wukv=wukv, wbr=g("w_branch")[l], wout=g("w_out")[l])
        for n, _ in PER_LAYER:
            Wd[n].append(np.ascontiguousarray(vals[n], dtype=np.float32))
    return {n: np.ascontiguousarray(np.stack(v)) for n, v in Wd.items()}


_CACHE = {}


def kernel(**inputs):
    x = np.asarray(inputs["x"], dtype=np.float32)
    positions = np.asarray(inputs["positions"], dtype=np.int32)
    B, S, _ = x.shape
    L = np.asarray(inputs["w_in"]).shape[0]
    key = (S, L)
    if key not in _CACHE:
        _CACHE[key] = build_program(S, L)[0]
    nc = _CACHE[key]
    Wd = host_prepare(inputs, L)
    cst = host_consts()
    invf = host_invf()
    n_cores = B
    in_maps = []
    for c in range(n_cores):
        b = c % B
        m = {"x": np.ascontiguousarray(x[b]), "pos": np.ascontiguousarray(positions[b].reshape(S // 128, 128).T),
             "cst": cst, "invf": invf}
        m.update(Wd)
        in_maps.append(m)
    res = run_bass_kernel_spmd(nc, in_maps, core_ids=list(range(n_cores)))
    outs = [np.asarray(res.results[b]["out"], dtype=np.float32) for b in range(B)]
    return np.stack(outs, axis=0)
```
